# Optimizing a Trainium2 kernel written in Bass

```python
import jax, jax.numpy as jnp
from jax import lax
import numpy as np

D_MODEL = 2048
BATCH = 2
SEQ = 4096
DEPTH = 2
DEC_BATCH = 32
DEC_SEQ = 32
PAST_LEN = 2048

CHUNK = 64
N_MIXERS = 2
N_A_LAYERS = (DEPTH + 1) // 2
N_B_LAYERS = DEPTH // 2
HGRN_HEADS = 16
HGRN_DK = D_MODEL // HGRN_HEADS
HGRN_DV = D_MODEL // HGRN_HEADS
HGRN_QW = HGRN_HEADS * HGRN_DK
HGRN_VW = HGRN_HEADS * HGRN_DV
ATT_HEADS = 16
ATT_HEAD_DIM = D_MODEL // ATT_HEADS
N_PREV_CHUNKS = 8
BAND = N_PREV_CHUNKS * CHUNK
MAX_BACK = 2 * CHUNK
REL_ROWS = MAX_BACK + CHUNK
D_FF = 5632
CONV_W = 3
EPS = 1e-6

kernel_name = "hybrid_stream_hgrn2_bandattn_convffn_step"


def rms_norm(x, g):
    xf = x.astype(jnp.float32)
    y = xf * lax.rsqrt(jnp.mean(xf * xf, axis=-1, keepdims=True) + EPS)
    return (y * g.astype(jnp.float32)).astype(x.dtype)


def gla_scan(q, k, v, log_f, S0):
    B, L, H, _ = q.shape
    blk = min(CHUNK, L)
    nb = L // blk

    def to_blocks(t):
        return t.astype(jnp.float32).reshape(B, nb, blk, H, t.shape[-1]).transpose(1, 0, 3, 2, 4)

    causal = jnp.tril(jnp.ones((blk, blk), dtype=bool))

    def step(S, inp):
        qc, kc, vc, gc = inp
        b = jnp.cumsum(gc, axis=2)
        diff = b[:, :, :, None, :] - b[:, :, None, :, :]
        decay = jnp.exp(jnp.where(causal[:, :, None], diff, -jnp.inf))
        scores = jnp.einsum('bhtsk,bhsk->bhts', qc[:, :, :, None, :] * decay, kc)
        o = (jnp.einsum('bhts,bhsv->bhtv', scores, vc)
             + jnp.einsum('bhtk,bhkv->bhtv', qc * jnp.exp(b), S))
        b_end = b[:, :, -1:, :]
        S = (jnp.exp(b_end[:, :, 0, :])[..., None] * S
             + jnp.einsum('bhsk,bhsv->bhkv', kc * jnp.exp(b_end - b), vc))
        return S, o

    S, o = lax.scan(step, S0.astype(jnp.float32), (to_blocks(q), to_blocks(k), to_blocks(v), to_blocks(log_f)))
    o = o.transpose(1, 0, 3, 2, 4).reshape(B, L, H, v.shape[-1])
    return o.astype(q.dtype), S.astype(S0.dtype)


def hgrn2_mixer(h, S0, w_in, lb, norm_o, w_o):
    B, L, _ = h.shape
    proj = h @ w_in
    q = proj[..., :HGRN_QW].reshape(B, L, HGRN_HEADS, HGRN_DK)
    fz = proj[..., HGRN_QW:2 * HGRN_QW].reshape(B, L, HGRN_HEADS, HGRN_DK).astype(jnp.float32)
    iv = proj[..., 2 * HGRN_QW:2 * HGRN_QW + HGRN_VW].reshape(B, L, HGRN_HEADS, HGRN_DV)
    g = proj[..., 2 * HGRN_QW + HGRN_VW:].reshape(B, L, HGRN_HEADS, HGRN_DV)
    lbh = lb.reshape(HGRN_HEADS, HGRN_DK)
    log_f = jnp.log(lbh + (1.0 - lbh) * jax.nn.sigmoid(fz))
    k = (1.0 - lbh) * jax.nn.sigmoid(-fz)
    o, S = gla_scan(q, k, iv, log_f, S0)
    o = rms_norm(o, norm_o) * jax.nn.silu(g)
    return o.reshape(B, L, HGRN_VW) @ w_o, S


def attn_project(h, w_qkv, q_norm, k_norm):
    B, L, _ = h.shape
    proj = h @ w_qkv
    q = proj[..., :D_MODEL].reshape(B, L, ATT_HEADS, ATT_HEAD_DIM)
    k = proj[..., D_MODEL:2 * D_MODEL].reshape(B, L, ATT_HEADS, ATT_HEAD_DIM)
    v = proj[..., 2 * D_MODEL:].reshape(B, L, ATT_HEADS, ATT_HEAD_DIM)
    return rms_norm(q, q_norm), rms_norm(k, k_norm), v


def band_attend(q, k, v, q_pos, k_pos, rel_bias):
    s = jnp.einsum('bnqhd,bnkhd->bnhqk', q, k).astype(jnp.float32) * (ATT_HEAD_DIM ** -0.5)
    rel = k_pos[:, None, :] - q_pos[:, :, None]
    idx = jnp.clip(rel, -MAX_BACK, CHUNK - 1) + MAX_BACK
    bias = jnp.transpose(rel_bias.astype(jnp.float32)[idx], (0, 3, 1, 2))
    q_chunk = (q_pos // CHUNK)[:, :, None]
    k_chunk = (k_pos // CHUNK)[:, None, :]
    valid = (k_pos[:, None, :] >= 0) & (k_chunk <= q_chunk) & (k_chunk >= q_chunk - N_PREV_CHUNKS)
    s = jnp.where(valid[None, :, None], s + bias[None], -jnp.inf)
    p = jax.nn.softmax(s, axis=-1).astype(v.dtype)
    return jnp.einsum('bnhqk,bnkhd->bnqhd', p, v)


def band_attn_prompt(h, w_qkv, q_norm, k_norm, rel_bias, w_o):
    B, L, _ = h.shape
    q, k, v = attn_project(h, w_qkv, q_norm, k_norm)
    nc = L // CHUNK
    pad = ((0, 0), (BAND, 0), (0, 0), (0, 0))
    idx = (jnp.arange(nc) * CHUNK)[:, None] + jnp.arange(BAND + CHUNK)[None, :]
    kb = jnp.pad(k, pad)[:, idx]
    vb = jnp.pad(v, pad)[:, idx]
    q_pos = jnp.arange(L).reshape(nc, CHUNK)
    o = band_attend(q.reshape(B, nc, CHUNK, ATT_HEADS, ATT_HEAD_DIM), kb, vb, q_pos, idx - BAND, rel_bias)
    keep = min(BAND, L)
    return o.reshape(B, L, D_MODEL) @ w_o, k[:, L - keep:], v[:, L - keep:]


def band_attn_sample(h, cache_k, cache_v, w_qkv, q_norm, k_norm, rel_bias, w_o):
    B, L, _ = h.shape
    q, k, v = attn_project(h, w_qkv, q_norm, k_norm)
    r = cache_k.shape[1]
    k_all = jnp.concatenate([cache_k.astype(k.dtype), k], axis=1)[:, None]
    v_all = jnp.concatenate([cache_v.astype(v.dtype), v], axis=1)[:, None]
    k_pos = jnp.arange(PAST_LEN - r, PAST_LEN + L)[None]
    q_pos = (PAST_LEN + jnp.arange(L))[None]
    o = band_attend(q[:, None], k_all, v_all, q_pos, k_pos, rel_bias)
    return o.reshape(B, L, D_MODEL) @ w_o, k, v


def conv_ffn(h, conv_state, w_in, conv_w, conv_b, w_down):
    L = h.shape[1]
    up = h @ w_in
    u = up[..., :D_FF]
    gate = up[..., D_FF:]
    u_ext = jnp.concatenate([conv_state.astype(u.dtype), u], axis=1)
    c = conv_b
    for j in range(CONV_W):
        c = c + u_ext[:, j:j + L] * conv_w[j]
    return (jax.nn.silu(c) * gate) @ w_down, u_ext[:, -(CONV_W - 1):]


def setup_inputs(seed: int = 0) -> dict:
    key = jax.random.key(seed)
    ks = jax.random.split(key, 24)

    def nrm(k, shape, scale):
        return jax.random.normal(k, shape, jnp.float32) * scale

    band_rows = min(BAND, PAST_LEN)
    return {
        "x_prompt": nrm(ks[0], (BATCH, SEQ, D_MODEL), 1.0),
        "x_sample": nrm(ks[1], (DEC_BATCH, DEC_SEQ, D_MODEL), 1.0),
        "state_a_S": nrm(ks[2], (N_A_LAYERS, DEC_BATCH, HGRN_HEADS, HGRN_DK, HGRN_DV), 0.5),
        "cache_b_k": nrm(ks[3], (N_B_LAYERS, DEC_BATCH, band_rows, ATT_HEADS, ATT_HEAD_DIM), 1.0),
        "cache_b_v": nrm(ks[4], (N_B_LAYERS, DEC_BATCH, band_rows, ATT_HEADS, ATT_HEAD_DIM), 1.0),
        "state_ffn_conv": nrm(ks[5], (DEPTH, DEC_BATCH, CONV_W - 1, D_FF), 1.0),
        "norm_mix": 1.0 + nrm(ks[6], (DEPTH, D_MODEL), 0.02),
        "norm_ffn": 1.0 + nrm(ks[7], (DEPTH, D_MODEL), 0.02),
        "a_w_in": nrm(ks[8], (N_A_LAYERS, D_MODEL, 2 * HGRN_QW + 2 * HGRN_VW), D_MODEL ** -0.5),
        "a_gamma_lb": nrm(ks[9], (DEPTH + 1, HGRN_QW), 1.0),
        "a_norm_o": 1.0 + nrm(ks[10], (N_A_LAYERS, HGRN_DV), 0.02),
        "a_w_o": nrm(ks[11], (N_A_LAYERS, HGRN_VW, D_MODEL), HGRN_VW ** -0.5),
        "b_w_qkv": nrm(ks[12], (N_B_LAYERS, D_MODEL, 3 * D_MODEL), D_MODEL ** -0.5),
        "b_q_norm": 1.0 + nrm(ks[13], (N_B_LAYERS, ATT_HEAD_DIM), 0.02),
        "b_k_norm": 1.0 + nrm(ks[14], (N_B_LAYERS, ATT_HEAD_DIM), 0.02),
        "b_rel_bias": nrm(ks[15], (N_B_LAYERS, REL_ROWS, ATT_HEADS), 0.1),
        "b_w_o": nrm(ks[16], (N_B_LAYERS, D_MODEL, D_MODEL), D_MODEL ** -0.5),
        "f_w_in": nrm(ks[17], (DEPTH, D_MODEL, 2 * D_FF), D_MODEL ** -0.5),
        "f_conv_w": nrm(ks[18], (DEPTH, CONV_W, D_FF), CONV_W ** -0.5),
        "f_conv_b": nrm(ks[19], (DEPTH, D_FF), 0.02),
        "f_w_down": nrm(ks[20], (DEPTH, D_FF, D_MODEL), D_FF ** -0.5),
    }


def reference(x_prompt, x_sample, state_a_S, cache_b_k, cache_b_v, state_ffn_conv,
              norm_mix, norm_ffn, a_w_in, a_gamma_lb, a_norm_o, a_w_o,
              b_w_qkv, b_q_norm, b_k_norm, b_rel_bias, b_w_o,
              f_w_in, f_conv_w, f_conv_b, f_w_down):
    lb_all = jnp.cumsum(jax.nn.softmax(a_gamma_lb.astype(jnp.float32), axis=0), axis=0)
    xp, xs = x_prompt, x_sample
    B = x_prompt.shape[0]
    a_Sp, a_Ss, b_kp, b_vp, b_ks, b_vs, f_cp, f_cs = [], [], [], [], [], [], [], []
    for i in range(DEPTH):
        j = i // N_MIXERS
        hp = rms_norm(xp, norm_mix[i])
        hs = rms_norm(xs, norm_mix[i])
        if i % N_MIXERS == 0:
            S_zero = jnp.zeros((B, HGRN_HEADS, HGRN_DK, HGRN_DV), jnp.float32)
            yp, Sp = hgrn2_mixer(hp, S_zero, a_w_in[j], lb_all[i], a_norm_o[j], a_w_o[j])
            ys, Ss = hgrn2_mixer(hs, state_a_S[j], a_w_in[j], lb_all[i], a_norm_o[j], a_w_o[j])
            a_Sp.append(Sp)
            a_Ss.append(Ss)
        else:
            yp, kp, vp = band_attn_prompt(hp, b_w_qkv[j], b_q_norm[j], b_k_norm[j], b_rel_bias[j], b_w_o[j])
            ys, ks_, vs_ = band_attn_sample(hs, cache_b_k[j], cache_b_v[j], b_w_qkv[j], b_q_norm[j],
                                            b_k_norm[j], b_rel_bias[j], b_w_o[j])
            b_kp.append(kp)
            b_vp.append(vp)
            b_ks.append(ks_)
            b_vs.append(vs_)
        xp = xp + yp
        xs = xs + ys
        conv_zero = jnp.zeros((B, CONV_W - 1, D_FF), xp.dtype)
        fp, cp = conv_ffn(rms_norm(xp, norm_ffn[i]), conv_zero, f_w_in[i], f_conv_w[i], f_conv_b[i], f_w_down[i])
        fs, cs = conv_ffn(rms_norm(xs, norm_ffn[i]), state_ffn_conv[i], f_w_in[i], f_conv_w[i], f_conv_b[i], f_w_down[i])
        xp = xp + fp
        xs = xs + fs
        f_cp.append(cp)
        f_cs.append(cs)
    return (xp, xs, jnp.stack(a_Sp), jnp.stack(a_Ss), jnp.stack(b_kp), jnp.stack(b_vp),
            jnp.stack(b_ks), jnp.stack(b_vs), jnp.stack(f_cp), jnp.stack(f_cs))
```

```python
import numpy as np
from contextlib import ExitStack
import concourse.bass as bass
import concourse.mybir as mybir
from concourse.bass_utils import run_bass_kernel_spmd

F32 = mybir.dt.float32
BF16 = mybir.dt.bfloat16
AF = mybir.ActivationFunctionType
ALU = mybir.AluOpType

D = 2048
DFF = 5632
NFU = 44
EPS = 1e-6
NEG = -30000.0
SC = 128.0 ** -0.5


class Buf:
    __slots__ = ("name", "w", "r")

    def __init__(self, name="b"):
        self.name = name
        self.w = None
        self.r = {}


class Prog:
    ENGS = ("pe", "act", "dve", "pool", "sp")

    def __init__(self, nc, n_dma_sp=8, n_dma_pool=4):
        self.nc = nc
        self.ops = {e: [] for e in self.ENGS}
        self.known = {e: {} for e in self.ENGS}
        self.cnt = {e: 0 for e in self.ENGS}
        self.sems = {}
        self.dma_pool = {"sp": [f"dsp{i}" for i in range(n_dma_sp)],
                         "pool": [f"dpl{i}" for i in range(n_dma_pool)]}
        self.dma_tgt = {}
        self.dma_idx = {"sp": 0, "pool": 0}

    def semkeys(self):
        ks = [f"c_{e}" for e in ("pe", "act", "dve", "pool")]
        for q in self.dma_pool.values():
            ks += q
        return ks

    def _collect(self, eng, reads, writes):
        waits = {}

        def add(ev):
            if ev is None:
                return
            k, v = ev
            if waits.get(k, 0) < v:
                waits[k] = v
        for b in reads:
            add(b.w)
        for b in writes:
            add(b.w)
            for k, v in b.r.items():
                add((k, v))
        out = []
        kn = self.known[eng]
        for k, v in waits.items():
            if eng == "pe" and k == "c_pe":
                continue
            if kn.get(k, 0) >= v:
                continue
            kn[k] = v
            out.append((k, v))
        return out

    def _commit(self, ev, reads, writes):
        k, v = ev
        for b in reads:
            if b.r.get(k, 0) < v:
                b.r[k] = v
        for b in writes:
            b.w = ev
            b.r = {}

    def op(self, eng, fn, reads=(), writes=()):
        waits = self._collect(eng, reads, writes)
        self.cnt[eng] += 1
        ev = (f"c_{eng}", self.cnt[eng])
        self.ops[eng].append((waits, fn, (ev[0], 1)))
        self._commit(ev, reads, writes)
        return ev

    def dma(self, q, out, in_, reads=(), writes=(), **kw):
        pool = self.dma_pool[q]
        i = self.dma_idx[q]
        self.dma_idx[q] = i + 1
        sk = pool[i % len(pool)]
        prev = self.dma_tgt.get(sk, 0)
        waits = self._collect(q, reads, writes)
        if prev > 0 and self.known[q].get(sk, 0) < prev:
            self.known[q][sk] = prev
            waits.append((sk, prev))
        tgt = prev + 16
        self.dma_tgt[sk] = tgt
        ev = (sk, tgt)
        self.ops[q].append((waits, (lambda e, o=out, i_=in_, k=kw: e.dma_start(out=o, in_=i_, **k)), (sk, 16)))
        self._commit(ev, reads, writes)
        return ev

    def _all_events(self):
        waits = [(sk, t) for sk, t in self.dma_tgt.items() if t > 0]
        for e in ("pe", "act", "dve", "pool"):
            if self.cnt[e] > 0:
                waits.append((f"c_{e}", self.cnt[e]))
        return waits

    def barrier(self):
        allw = self._all_events()
        for e in self.ENGS:
            w = []
            for k, v in allw:
                if k == f"c_{e}" and e == "pe":
                    continue
                if self.known[e].get(k, 0) < v:
                    self.known[e][k] = v
                    w.append((k, v))
            if w:
                self.ops[e].append((w, None, None))

    def finish(self):
        self.ops["sp"].append((self._all_events(), None, None))

    def replay(self, block):
        sems = self.sems

        def run(engname):
            def f(e):
                for waits, fn, inc in self.ops[engname]:
                    for k, v in waits:
                        e.wait_ge(sems[k], v)
                    if fn is not None:
                        ins = fn(e)
                        if inc is not None:
                            ins.then_inc(sems[inc[0]], inc[1])
            return f
        block.tensor(run("pe"))
        block.scalar(run("act"))
        block.vector(run("dve"))
        block.gpsimd(run("pool"))
        block.sync(run("sp"))


class T:
    def __init__(self, t, name):
        self.t = t
        self.b = Buf(name)

    def __getitem__(self, idx):
        return self.t[idx]


class Ring:
    def __init__(self, items):
        self.items = items
        self.i = 0

    def nxt(self):
        x = self.items[self.i % len(self.items)]
        self.i += 1
        return x


def chunks(a, b, scol=None):
    out = []
    end = b if scol is None else min(b, scol)
    c = a
    while c < end:
        n = min(512, end - c)
        out.append((c, n, False))
        c += n
    if scol is not None and b > scol:
        out.append((scol, 128, True))
    return out


def build_program():
    nc = bass.Bass("TRN2", target_bir_lowering=False)

    def din(name, shape):
        return nc.dram_tensor(name, shape, F32, kind="ExternalInput").ap()

    def dout(name, shape):
        return nc.dram_tensor(name, shape, F32, kind="ExternalOutput").ap()

    xs = din("xs", [33 * 128, D])
    s0 = din("s0", [4, 16, 128, 128])
    ck = din("ck", [4, 512, D])
    cv = din("cv", [4, 512, D])
    cst = din("cst", [16, DFF])
    w_in = din("w_in", [D, 8192])
    w_oa = din("w_oa", [D, D])
    w_qkv = din("w_qkv", [D, 6144])
    w_ob = din("w_ob", [D, D])
    f_in = din("f_in", [2, D, 2 * DFF])
    f_dn = din("f_dn", [2, DFF, D])
    d_gains = din("gains", [128, 4, 16])
    d_gam = din("gam", [128, 3, 16])
    d_pvec = din("pvec", [128, 3])
    d_c0 = din("c0t", [128, 16])
    d_T3 = din("T3", [128, 16, 128])
    d_T4 = din("T4", [128, 16, 128])
    d_Tn = din("Tn", [128, 16, 32])
    d_cw = din("cw", [128, 2, 3, NFU])
    d_cb = din("cb", [128, 2, NFU])
    d_ident = din("ident", [128, 128])
    d_cmask = din("cmask", [128, 128])
    d_smask = din("smask", [128, 128])
    d_scanp = din("scanp", [128, 512])
    d_scans = din("scans", [128, 128])
    d_kneg = din("kneg", [128, 13])
    d_seqm = din("seqm", [128, 4])

    o_yp = dout("yp", [1024, D])
    o_ys = dout("ys", [128, D])
    o_Sp = dout("Sp", [16, 128, 128])
    o_Ss = dout("Ss", [4, 16, 128, 128])
    o_kp = dout("kp", [512, D])
    o_vp = dout("vp", [512, D])
    o_ks = dout("ks", [128, D])
    o_vs = dout("vs", [128, D])
    o_co = dout("co", [2, 10, DFF])

    kscr = nc.dram_tensor("kscr", [16, 128, 512], BF16, kind="Internal").ap()
    vscr = nc.dram_tensor("vscr", [512, D], BF16, kind="Internal").ap()
    Bkscr = Buf("kscr")
    Bvscr = Buf("vscr")

    P = Prog(nc)
    es = ExitStack()
    with es:
        for k in P.semkeys():
            P.sems[k] = es.enter_context(nc.semaphore(k))

        def sb(name, shape, dt):
            return T(es.enter_context(nc.sbuf_tensor("s_" + name, shape, dt)), name)

        def ps(name, shape, dt):
            return T(es.enter_context(nc.psum_tensor("p_" + name, shape, dt)), name)

        def bufs(lst):
            return [x.b if hasattr(x, 'b') else x for x in lst]

        def ACT(out, in_, func, R, W, **kw):
            P.op("act", lambda e: e.activation(out=out, in_=in_, func=func, **kw), bufs(R), bufs(W))

        def MM(out, lhsT, rhs, start, stop, R, W):
            P.op("pe", lambda e: e.matmul(out, lhsT, rhs, start=start, stop=stop), bufs(R), bufs(W))

        def TR(out, in_, ident, R, W):
            P.op("pe", lambda e: e.transpose(out, in_, ident), bufs(R), bufs(W))

        def TT(out, in0, in1, op, R, W):
            P.op("dve", lambda e: e.tensor_tensor(out=out, in0=in0, in1=in1, op=op), bufs(R), bufs(W))

        def TS(out, in0, s1, op0, R, W, s2=None, op1=None):
            if op1 is None:
                P.op("dve", lambda e: e.tensor_scalar(out=out, in0=in0, scalar1=s1, scalar2=None, op0=op0), bufs(R), bufs(W))
            else:
                P.op("dve", lambda e: e.tensor_scalar(out=out, in0=in0, scalar1=s1, scalar2=s2, op0=op0, op1=op1), bufs(R), bufs(W))

        def STT(out, in0, scalar, in1, op0, op1, R, W):
            P.op("dve", lambda e: e.scalar_tensor_tensor(out=out, in0=in0, scalar=scalar, in1=in1, op0=op0, op1=op1), bufs(R), bufs(W))

        def CPV(out, in_, R, W):
            P.op("dve", lambda e: e.tensor_copy(out=out, in_=in_), bufs(R), bufs(W))

        def RECIP(out, in_, R, W):
            P.op("dve", lambda e: e.reciprocal(out=out, in_=in_), bufs(R), bufs(W))

        def SCAN(out, d0, d1, R, W):
            P.op("dve", lambda e: e.tensor_tensor_scan(out=out, data0=d0, data1=d1, initial=0.0, op0=ALU.mult, op1=ALU.add), bufs(R), bufs(W))

        def MEMSET(ap, val, W):
            P.op("dve", lambda e: e.memset(ap, val), [], bufs(W))

        def CPA(out, in_, R, W):
            ACT(out, in_, AF.Copy, R, W)

        def DMA(out, in_, R, W, q="sp"):
            P.dma(q, out, in_, bufs(R), bufs(W))

        xT = sb("xT", [128, 16, 896], F32)
        hT = sb("hT", [128, 16, 896], BF16)
        wring = Ring([sb(f"wr{i}", [128, 4096], BF16) for i in range(3)])
        Vg = sb("Vg", [128, 12, 256], BF16)
        g4 = sb("g4", [128, 2, 896], BF16)
        oacc = sb("oacc", [128, 896], F32)
        tf = Ring([sb(f"tf{i}", [128, 512], F32) for i in range(6)])
        tb = Ring([sb(f"tb{i}", [128, 512], BF16) for i in range(4)])
        us = Ring([sb(f"us{i}", [128, 516], F32) for i in range(2)])
        S = sb("S", [128, 16, 128], F32)
        Sb = sb("Sb", [128, 16, 128], BF16)
        S0f = sb("S0f", [128, 4, 128], F32)
        arena = sb("arena", [128, 9600], BF16)
        identf = sb("identf", [128, 128], F32)
        identb = sb("identb", [128, 128], BF16)
        onesb = sb("onesb", [128, 128], BF16)
        cmask = sb("cmask", [128, 128], F32)
        smask = sb("smask", [128, 128], F32)
        scanp = sb("scanp", [128, 512], F32)
        scans = sb("scans", [128, 128], F32)
        gains = sb("gains", [128, 4, 16], F32)
        gam = sb("gam", [128, 3, 16], F32)
        lbt = sb("lbt", [128, 8, 16], F32)
        oml = sb("oml", [128, 16], F32)
        pvec = sb("pvec", [128, 3], F32)
        c0t = sb("c0t", [128, 16], F32)
        kneg = sb("kneg", [128, 13], F32)
        cbias = sb("cbias", [128, 13, 16], F32)
        seqm = sb("seqm", [128, 4], F32)
        cw = sb("cw", [128, 2, 3, NFU], F32)
        cb = sb("cb", [128, 2, NFU], F32)
        cstT = sb("cstT", [128, NFU, 16], F32)
        ucol = sb("ucol", [128, NFU, 10], F32)
        uH = sb("uH", [128, 2, NFU, 2], F32)
        onec = sb("onec", [128, 1], F32)
        epsc = sb("epsc", [128, 1], F32)
        Et = sb("Et", [128, 16], F32)
        E12t = sb("E12t", [128, 8], F32)
        Est = sb("Est", [128, 4], F32)
        T3g = sb("T3g", [128, 2, 128], F32)
        T4g = sb("T4g", [128, 2, 128], F32)
        Tng = sb("Tng", [128, 2, 32], F32)
        eT3g = sb("eT3g", [128, 2, 128], F32)
        eT4g = sb("eT4g", [128, 2, 128], F32)
        eTng = sb("eTng", [128, 2, 32], F32)

        pm = Ring([ps(f"pm{i}", [128, 512], F32) for i in range(4)])
        pss_ = Ring([ps(f"ps{i}", [128, 512], F32) for i in range(2)])
        ptr = Ring([ps(f"pt{i}", [128, 1024], BF16) for i in range(2)])

        class AV:
            def __init__(self, off, n, name):
                self.ap = arena.t[:, off:off + n]
                self.b = Buf(name)
                self.off = off

            def __getitem__(self, idx):
                return self.ap[idx]

        def carve(spec):
            off = 0
            d = {}
            for name, n in spec:
                d[name] = AV(off, n, name)
                off += n
            assert off <= 9600, off
            return d

        A_ = carve([("kT", 896), ("KpT", 896), ("qT", 896), ("Q2T", 896), ("gsT", 896), ("S0b", 512),
                    ("A0", 128), ("A1", 128), ("A2", 128), ("Kt0", 128), ("Kt1", 128), ("Km0", 128), ("Km1", 128)])
        B_ = carve([("KTg", 3072), ("QTg", 1792), ("kc", 1024), ("vc", 1024), ("KcT", 1024),
                    ("E0a", 128), ("E0b", 128), ("E4a", 128), ("E4b", 128), ("E3a", 128), ("E3b", 128),
                    ("E1a", 128), ("E1b", 128), ("E2a", 128), ("E2b", 128), ("Eca", 160), ("Ecb", 160)])

        for t_, d_ in ((identf, d_ident), (cmask, d_cmask), (smask, d_smask), (scanp, d_scanp), (scans, d_scans),
                       (gains, d_gains), (gam, d_gam), (pvec, d_pvec), (c0t, d_c0), (kneg, d_kneg), (seqm, d_seqm),
                       (cw, d_cw), (cb, d_cb)):
            DMA(t_.t[:], d_, [], [t_])
        DMA(identb.t[:], d_ident, [], [identb], q="pool")
        MEMSET(onesb.t[:], 1.0, [onesb])
        MEMSET(onec.t[:], 1.0, [onec])
        MEMSET(epsc.t[:], EPS, [epsc])
        MEMSET(S.t[:], 0.0, [S])
        MEMSET(Sb.t[:], 0.0, [Sb])
        MEMSET(uH.t[:], 0.0, [uH])
        g0, g1, g2 = gam.t[:, 0, :], gam.t[:, 1, :], gam.t[:, 2, :]
        L = lambda i: lbt.t[:, i, :]
        TT(L(0), g0, g1, ALU.max, [gam], [lbt])
        TT(L(0), L(0), g2, ALU.max, [gam, lbt], [lbt])
        for i, gi in enumerate((g0, g1, g2)):
            TT(L(1 + i), gi, L(0), ALU.subtract, [gam, lbt], [lbt])
            ACT(L(1 + i), L(1 + i), AF.Exp, [lbt], [lbt])
        TT(L(4), L(1), L(2), ALU.add, [lbt], [lbt])
        TT(L(4), L(4), L(3), ALU.add, [lbt], [lbt])
        RECIP(L(5), L(4), [lbt], [lbt])
        TT(L(6), L(1), L(5), ALU.mult, [lbt], [lbt])
        TS(oml.t[:], L(6), -1.0, ALU.mult, [lbt], [oml], s2=1.0, op1=ALU.add)
        for t in range(13):
            TS(cbias.t[:, t, :], c0t.t[:], kneg.t[:, t:t + 1], ALU.add, [c0t, kneg], [cbias])
        for pc in range(11):
            tt_ = tf.nxt()
            DMA(tt_.t[0:16, :], cst[:, pc * 512:(pc + 1) * 512], [], [tt_])
            pp = pss_.nxt()
            for i in range(4):
                TR(pp.t[:, i * 16:(i + 1) * 16], tt_.t[0:16, i * 128:(i + 1) * 128], identf.t[0:16, 0:16], [tt_, identf], [pp])
            CPV(cstT.t[:, pc * 4:(pc + 1) * 4, :], pp.t[:, 0:64].rearrange("p (a b) -> p a b", b=16), [pp], [cstT])

        def load_unit(src):
            w = wring.nxt()
            v = w.t[:, 0:4096].rearrange("p (k n) -> p k n", n=256)
            DMA(v, src.rearrange("(kc p) n -> p kc n", p=128), [], [w], q="pool")
            return w, v

        def load_rows(src):
            w = wring.nxt()
            v = w.t[:, 0:4096].rearrange("p (k n) -> p k n", n=2048)
            for kc_ in range(2):
                P.dma("pool", v[:, kc_, :], src[kc_ * 128:(kc_ + 1) * 128, :], [], [w.b], max_dma_last_dim=4096)
            return w, v

        def load_x(blk):
            tiles = blk["tiles"] + ([32] if blk["sample"] else [])
            for j, gt in enumerate(tiles):
                for q4 in range(4):
                    xt_ = tf.nxt()
                    DMA(xt_.t[:], xs[gt * 128:(gt + 1) * 128, q4 * 512:(q4 + 1) * 512], [], [xt_])
                    pp = pm.nxt()
                    for i in range(4):
                        TR(pp.t[:, i * 128:(i + 1) * 128], xt_.t[:, i * 128:(i + 1) * 128], identf.t[:], [xt_, identf], [pp])
                    CPA(xT.t[:, q4 * 4:(q4 + 1) * 4, j * 128:(j + 1) * 128], pp.t[:].rearrange("p (a b) -> p a b", b=128), [pp], [xT])

        def store_x(blk):
            tiles = blk["tiles"] + ([32] if blk["sample"] else [])
            for j, gt in enumerate(tiles):
                if gt < 24:
                    continue
                dst = o_ys if gt == 32 else o_yp[(gt - 24) * 128:(gt - 23) * 128, :]
                for q4 in range(4):
                    pp = pm.nxt()
                    for i in range(4):
                        TR(pp.t[:, i * 128:(i + 1) * 128], xT.t[:, q4 * 4 + i, j * 128:(j + 1) * 128], identf.t[:], [xT, identf], [pp])
                    st = tf.nxt()
                    CPA(st.t[:], pp.t[:], [pp], [st])
                    DMA(dst[:, q4 * 512:(q4 + 1) * 512], st.t[:], [st], [])

        def rstd_from(p, n, scale):
            rs = tf.nxt()
            ACT(rs.t[:, :n], p.t[:, :n], AF.Sqrt, [p, epsc], [rs], scale=scale, bias=epsc.t[:])
            rr = tf.nxt()
            RECIP(rr.t[:, :n], rs.t[:, :n], [rs], [rr])
            return rr

        def norm(gi, a, b):
            for (c0, n, _) in chunks(a, b):
                pssq = pm.nxt()
                for kc in range(16):
                    sq = tb.nxt()
                    ACT(sq.t[:, :n], xT.t[:, kc, c0:c0 + n], AF.Square, [xT], [sq])
                    MM(pssq.t[:, :n], onesb.t[:], sq.t[:, :n], kc == 0, kc == 15, [onesb, sq], [pssq])
                rr = rstd_from(pssq, n, 1.0 / D)
                for kc in range(16):
                    STT(hT.t[:, kc, c0:c0 + n], xT.t[:, kc, c0:c0 + n], gains.t[:, gi, kc:kc + 1], rr.t[:, :n],
                        ALU.mult, ALU.mult, [xT, gains, rr], [hT])

        def proj_T(Wt, Wv, hh, c0, n):
            p = pm.nxt()
            for kc in range(16):
                MM(p.t[:, :n], Wv[:, kc, hh * 128:(hh + 1) * 128], hT.t[:, kc, c0:c0 + n], kc == 0, kc == 15, [Wt, hT], [p])
            return p

        def wo_partial(Wt, Wv, a, b):
            for cu in range(16):
                for (c0, n, _) in chunks(a, b):
                    py = pm.nxt()
                    for hh in range(2):
                        MM(py.t[:, :n], Wv[:, hh, cu * 128:(cu + 1) * 128], g4.t[:, hh, c0:c0 + n], hh == 0, hh == 1, [Wt, g4], [py])
                    TT(xT.t[:, cu, c0:c0 + n], xT.t[:, cu, c0:c0 + n], py.t[:, :n], ALU.add, [xT, py], [xT])

        def stage_hgrn(blk):
            nt = len(blk["tiles"])
            samp = blk["sample"]
            full = blk["full"]
            scol = nt * 128 if samp else None
            ntt = nt + (1 if samp else 0)
            Tn_ = ntt * 128
            kT, KpT, qT, Q2T, gsT, S0b = A_["kT"], A_["KpT"], A_["qT"], A_["Q2T"], A_["gsT"], A_["S0b"]
            Aring = Ring([A_["A0"], A_["A1"]])
            Ktr = Ring([A_["Kt0"], A_["Kt1"]])
            Kmr = Ring([A_["Km0"], A_["Km1"]])
            for a_ in (A_["A0"], A_["A1"]):
                MEMSET(a_.ap[:, :], 0.0, [a_])
            for g in range(8):
                Wi, Wiv = load_unit(w_in[:, 4096 + 256 * g:4096 + 256 * (g + 1)])
                Wf, Wfv = load_unit(w_in[:, 2048 + 256 * g:2048 + 256 * (g + 1)])
                for j in range(ntt):
                    pv = pm.nxt()
                    for kc in range(16):
                        MM(pv.t[:, :256], hT.t[:, kc, j * 128:(j + 1) * 128], Wiv[:, kc, :], kc == 0, kc == 15, [hT, Wi], [pv])
                    CPA(Vg.t[:, j, :], pv.t[:, :256], [pv], [Vg])
                if full:
                    Wq, Wqv = load_unit(w_in[:, 256 * g:256 * (g + 1)])
                    WgT, Wgv_ = load_unit(w_in[:, 6144 + 256 * g:6144 + 256 * (g + 1)])
                for hh in range(2):
                    h = 2 * g + hh
                    for (c0, n, is_s) in chunks(0, Tn_, scol):
                        pf = proj_T(Wf, Wfv, hh, c0, n)
                        sn = tf.nxt()
                        ACT(sn.t[:, :n], pf.t[:, :n], AF.Sigmoid, [pf], [sn], scale=-1.0)
                        kk = tf.nxt()
                        TS(kk.t[:, :n], sn.t[:, :n], oml.t[:, h:h + 1], ALU.mult, [sn, oml], [kk])
                        lf = tf.nxt()
                        ACT(lf.t[:, :n], kk.t[:, :n], AF.Ln, [kk, onec], [lf], scale=-1.0, bias=onec.t[:])
                        bb = tf.nxt()
                        mk = scans.t[:, 0:n] if is_s else scanp.t[:, 0:n]
                        SCAN(bb.t[:, :n], mk, lf.t[:, :n], [lf, scans, scanp], [bb])
                        enb = tf.nxt()
                        ACT(enb.t[:, :n], bb.t[:, :n], AF.Exp, [bb], [enb], scale=-1.0)
                        TT(kT.ap[:, c0:c0 + n], kk.t[:, :n], enb.t[:, :n], ALU.mult, [kk, enb], [kT])
                        if is_s:
                            ACT(Est.t[:, 0:4], bb.t[:, 0:128].rearrange("p (a b) -> p a b", b=32)[:, :, 31], AF.Exp, [bb], [Est])
                        else:
                            ACT(Et.t[:, c0 // 64:(c0 + n) // 64], bb.t[:, 0:n].rearrange("p (a b) -> p a b", b=64)[:, :, 63],
                                AF.Exp, [bb], [Et])
                        if full:
                            pq = proj_T(Wq, Wqv, hh, c0, n)
                            eb = sn
                            ACT(eb.t[:, :n], bb.t[:, :n], AF.Exp, [bb], [eb])
                            TT(qT.ap[:, c0:c0 + n], pq.t[:, :n], eb.t[:, :n], ALU.mult, [pq, eb], [qT])
                    if nt > 0:
                        v3 = lambda av, lo, hi: av.ap[:, 0:nt * 128].rearrange("p (a b) -> p a b", b=128)[:, :, lo:hi]
                        CPA(v3(KpT, 64, 128), v3(kT, 64, 128), [kT], [KpT])
                        if full:
                            CPA(v3(Q2T, 0, 64), v3(qT, 0, 64), [qT], [Q2T])
                        for j in range(nt):
                            e1 = Et.t[:, 2 * j:2 * j + 1]
                            TS(KpT.ap[:, j * 128:j * 128 + 64], kT.ap[:, j * 128:j * 128 + 64], e1, ALU.mult, [kT, Et], [KpT])
                            if full:
                                TS(Q2T.ap[:, j * 128 + 64:(j + 1) * 128], qT.ap[:, j * 128 + 64:(j + 1) * 128], e1, ALU.mult, [qT, Et], [Q2T])
                        ev = Et.t[:, 0:2 * nt].rearrange("p (a b) -> p a b", b=2)
                        TT(E12t.t[:, 0:nt], ev[:, :, 0], ev[:, :, 1], ALU.mult, [Et], [E12t])
                    for j in range(nt):
                        cs = slice(j * 128, (j + 1) * 128)
                        vj = Vg.t[:, j, hh * 128:(hh + 1) * 128]
                        if full:
                            psc = pss_.nxt()
                            MM(psc.t[0:64, 0:64], kT.ap[:, j * 128:j * 128 + 64], qT.ap[:, j * 128:j * 128 + 64], True, True, [kT, qT], [psc])
                            MM(psc.t[:, 64:128], KpT.ap[:, cs], qT.ap[:, j * 128 + 64:(j + 1) * 128], True, True, [KpT, qT], [psc])
                            A = Aring.nxt()
                            TT(A.ap[0:64, 0:64], psc.t[0:64, 0:64], cmask.t[0:64, 0:64], ALU.mult, [psc, cmask], [A])
                            TT(A.ap[:, 64:128], psc.t[:, 64:128], cmask.t[:, 64:128], ALU.mult, [psc, cmask], [A])
                            po = pss_.nxt()
                            MM(po.t[:, 0:128], vj, A.ap[:, :], True, False, [Vg, A], [po])
                            MM(po.t[:, 0:128], Sb.t[:, h, :], Q2T.ap[:, cs], False, True, [Sb, Q2T], [po])
                            CPA(oacc.t[:, cs], po.t[:, 0:128], [po], [oacc])
                        pt = ptr.nxt()
                        TR(pt.t[:, 0:128], KpT.ap[:, cs], identb.t[:], [KpT, identb], [pt])
                        Kt = Ktr.nxt()
                        CPA(Kt.ap[:, :], pt.t[:, 0:128], [pt], [Kt])
                        px = pss_.nxt()
                        MM(px.t[:, 0:128], Kt.ap[:, :], vj, True, True, [Kt, Vg], [px])
                        tmp = tf.nxt()
                        TS(tmp.t[:, :128], px.t[:, 0:128], Et.t[:, 2 * j + 1:2 * j + 2], ALU.mult, [px, Et], [tmp])
                        STT(S.t[:, h, :], S.t[:, h, :], E12t.t[:, j:j + 1], tmp.t[:, :128], ALU.mult, ALU.add, [S, E12t, tmp], [S])
                        CPA(Sb.t[:, h, :], S.t[:, h, :], [S], [Sb])
                    if samp:
                        cs = slice(scol, scol + 128)
                        vj = Vg.t[:, nt, hh * 128:(hh + 1) * 128]
                        DMA(S0f.t[:], s0[:, h].rearrange("s k v -> k s v"), [], [S0f])
                        CPA(S0b.ap[:, :], S0f.t[:].rearrange("p a b -> p (a b)"), [S0f], [S0b])
                        psc = pss_.nxt()
                        MM(psc.t[:, 0:128], kT.ap[:, cs], qT.ap[:, cs], True, True, [kT, qT], [psc])
                        A2 = A_["A2"]
                        TT(A2.ap[:, :], psc.t[:, 0:128], smask.t[:], ALU.mult, [psc, smask], [A2])
                        po = pss_.nxt()
                        for s in range(4):
                            MM(po.t[:, 32 * s:32 * s + 32], vj, A2.ap[:, 32 * s:32 * s + 32], True, False, [Vg, A2], [po])
                            MM(po.t[:, 32 * s:32 * s + 32], S0b.ap[:, 128 * s:128 * s + 128], qT.ap[:, scol + 32 * s:scol + 32 * s + 32],
                               False, True, [S0b, qT], [po])
                        CPA(oacc.t[:, cs], po.t[:, 0:128], [po], [oacc])
                        pt = ptr.nxt()
                        TR(pt.t[:, 0:128], kT.ap[:, cs], identb.t[:], [kT, identb], [pt])
                        for s in range(4):
                            Km = Kmr.nxt()
                            TS(Km.ap[:, :], pt.t[:, 0:128], seqm.t[:, s:s + 1], ALU.mult, [pt, seqm], [Km])
                            px = pss_.nxt()
                            MM(px.t[:, 0:128], Km.ap[:, :], vj, True, True, [Km, Vg], [px])
                            t1 = tf.nxt()
                            TS(t1.t[:, :128], S0f.t[:, s, :], Est.t[:, s:s + 1], ALU.mult, [S0f, Est], [t1])
                            so = tf.nxt()
                            STT(so.t[:, :128], px.t[:, 0:128], Est.t[:, s:s + 1], t1.t[:, :128], ALU.mult, ALU.add, [px, Est, t1], [so])
                            DMA(o_Ss[s, h], so.t[:, :128], [so], [])
                    if full:
                        for (c0, n, _) in chunks(0, Tn_):
                            pg = proj_T(WgT, Wgv_, hh, c0, n)
                            ACT(gsT.ap[:, c0:c0 + n], pg.t[:, :n], AF.Silu, [pg], [gsT])
                            osq = tb.nxt()
                            ACT(osq.t[:, :n], oacc.t[:, c0:c0 + n], AF.Square, [oacc], [osq])
                            pq_ = pm.nxt()
                            MM(pq_.t[:, :n], onesb.t[:], osq.t[:, :n], True, True, [onesb, osq], [pq_])
                            rr = rstd_from(pq_, n, 1.0 / 128)
                            on = tf.nxt()
                            STT(on.t[:, :n], oacc.t[:, c0:c0 + n], pvec.t[:, 0:1], rr.t[:, :n], ALU.mult, ALU.mult, [oacc, pvec, rr], [on])
                            TT(g4.t[:, hh, c0:c0 + n], on.t[:, :n], gsT.ap[:, c0:c0 + n], ALU.mult, [on, gsT], [g4])
                if full:
                    Wo, Wov = load_rows(w_oa[256 * g:256 * (g + 1), :])
                    wo_partial(Wo, Wov, 0, Tn_)
            if blk["last"]:
                DMA(o_Sp.rearrange("h k v -> k h v"), S.t[:], [S], [])

        def stage_ffn(l, blk, a):
            nt = len(blk["tiles"])
            samp = blk["sample"]
            scol = nt * 128 if samp else None
            Tn_ = (nt + (1 if samp else 0)) * 128
            norm(1 + 2 * l, a, Tn_)
            chs = chunks(a, Tn_, scol)
            first = blk["first"]
            for sl in range(22):
                Wu, Wuv = load_unit(f_in[l][:, 256 * sl:256 * (sl + 1)])
                Wg, Wgv = load_unit(f_in[l][:, DFF + 256 * sl:DFF + 256 * (sl + 1)])
                for fu in range(2):
                    U = 2 * sl + fu
                    prev = None
                    for (c0, n, is_s) in chs:
                        pu = proj_T(Wu, Wuv, fu, c0, n)
                        pg = proj_T(Wg, Wgv, fu, c0, n)
                        u_ = us.nxt()
                        w0 = cw.t[:, l, 0, U:U + 1]
                        w1 = cw.t[:, l, 1, U:U + 1]
                        w2 = cw.t[:, l, 2, U:U + 1]
                        cc = tf.nxt()
                        ACT(cc.t[:, :n], pu.t[:, :n], AF.Identity, [pu, cw, cb], [cc], scale=w2, bias=cb.t[:, l, U:U + 1])
                        c1 = tf.nxt()
                        c2 = tf.nxt()
                        if is_s:
                            u4 = u_.t[:, 0:136].rearrange("p (s c) -> p s c", c=34)
                            CPV(u4[:, :, 0:2], cstT.t[:, U, 8 * l:8 * l + 8].rearrange("p (s c) -> p s c", c=2), [cstT], [u_])
                            CPA(u4[:, :, 2:34], pu.t[:, 0:128].rearrange("p (s c) -> p s c", c=32), [pu], [u_])
                            v3 = lambda t_: t_.t[:, 0:128].rearrange("p (s c) -> p s c", c=32)
                            STT(v3(c1), u4[:, :, 1:33], w1, v3(cc), ALU.mult, ALU.add, [u_, cw, cc], [c1])
                            STT(v3(c2), u4[:, :, 0:32], w0, v3(c1), ALU.mult, ALU.add, [u_, cw, c1], [c2])
                            CPV(ucol.t[:, U, 2:10].rearrange("p (s c) -> p s c", c=2), u4[:, :, 32:34], [u_], [ucol])
                        else:
                            if prev is not None:
                                CPV(u_.t[:, 0:2], prev[0].t[:, prev[1]:prev[1] + 2], [prev[0]], [u_])
                            elif first:
                                MEMSET(u_.t[:, 0:2], 0.0, [u_])
                            else:
                                CPV(u_.t[:, 0:2], uH.t[:, l, U, :], [uH], [u_])
                            CPA(u_.t[:, 2:2 + n], pu.t[:, :n], [pu], [u_])
                            STT(c1.t[:, :n], u_.t[:, 1:1 + n], w1, cc.t[:, :n], ALU.mult, ALU.add, [u_, cw, cc], [c1])
                            STT(c2.t[:, :n], u_.t[:, 0:n], w0, c1.t[:, :n], ALU.mult, ALU.add, [u_, cw, c1], [c2])
                            prev = (u_, n)
                            last_prompt = (c0 + n == nt * 128)
                            if last_prompt:
                                if blk["last"]:
                                    CPV(ucol.t[:, U, 0:2], u_.t[:, n:n + 2], [u_], [ucol])
                                else:
                                    CPV(uH.t[:, l, U, :], u_.t[:, n:n + 2], [u_], [uH])
                        sl_ = cc
                        ACT(sl_.t[:, :n], c2.t[:, :n], AF.Silu, [c2], [sl_])
                        TT(g4.t[:, fu, c0:c0 + n], sl_.t[:, :n], pg.t[:, :n], ALU.mult, [sl_, pg], [g4])
                Wd, Wdv = load_rows(f_dn[l][256 * sl:256 * (sl + 1), :])
                wo_partial(Wd, Wdv, a, Tn_)
            if blk["last"]:
                for q4 in range(11):
                    pp = pm.nxt()
                    for i in range(4):
                        TR(pp.t[0:10, i * 128:(i + 1) * 128], ucol.t[:, q4 * 4 + i, :], identf.t[:], [ucol, identf], [pp])
                    st = tf.nxt()
                    CPA(st.t[0:10, :], pp.t[0:10, :], [pp], [st])
                    DMA(o_co[l][:, q4 * 512:(q4 + 1) * 512], st.t[0:10, :], [st], [])

        import os
        KSUB = int(os.environ.get("KSUB", "99"))
        KSUB2 = int(os.environ.get("KSUB2", "99"))

        def stage_attn(blk):
            nt = len(blk["tiles"])
            samp = blk["sample"]
            scol = nt * 128 if samp else None
            ntt = nt + (1 if samp else 0)
            Tn_ = ntt * 128
            nh = blk["nh"]
            qj0 = blk["qj0"]
            qa = qj0 * 128
            nkt = nh + nt
            gt0 = blk["tiles"][0]
            KTg, QTg, kcb, vcb, KcT = B_["KTg"], B_["QTg"], B_["kc"], B_["vc"], B_["KcT"]
            KT3 = KTg.ap[:, :].rearrange("p (h c) -> p h c", h=2)
            QT3 = QTg.ap[:, :].rearrange("p (h c) -> p h c", h=2)
            E0r = Ring([B_["E0a"], B_["E0b"]])
            E4r = Ring([B_["E4a"], B_["E4b"]])
            E3r = Ring([B_["E3a"], B_["E3b"]])
            E1r = Ring([B_["E1a"], B_["E1b"]])
            E2r = Ring([B_["E2a"], B_["E2b"]])
            Ecr = Ring([B_["Eca"], B_["Ecb"]])
            for e_ in ("E0a", "E0b", "E4a", "E4b"):
                MEMSET(B_[e_].ap[:, :], 0.0, [B_[e_]])

            def normalize(p, n, col):
                sq = tb.nxt()
                ACT(sq.t[:, :n], p.t[:, :n], AF.Square, [p], [sq])
                p2 = pm.nxt()
                MM(p2.t[:, :n], onesb.t[:], sq.t[:, :n], True, True, [onesb, sq], [p2])
                rr = rstd_from(p2, n, 1.0 / 128)
                kn = tf.nxt()
                STT(kn.t[:, :n], p.t[:, :n], pvec.t[:, col:col + 1], rr.t[:, :n], ALU.mult, ALU.mult, [p, pvec, rr], [kn])
                return kn

            def out_tile_row(gt):
                if gt == 32:
                    return ("s", 0)
                if gt >= 28:
                    return ("p", (gt - 28) * 128)
                return None

            tiles = blk["tiles"] + ([32] if samp else [])
            def attn_inner(g):
                for hh in range(2):
                    h = 2 * g + hh
                    hs = slice(hh * 128, (hh + 1) * 128)
                    for j in range(qj0, nt):
                        qc = QT3[:, hh, j * 128:(j + 1) * 128]
                        ks0 = nh + j - 4
                        psA = pm.nxt()
                        psB = pss_.nxt()
                        for kt in range(5):
                            dst = psA.t[:, kt * 128:(kt + 1) * 128] if kt < 4 else psB.t[:, 0:128]
                            MM(dst, KT3[:, hh, (ks0 + kt) * 128:(ks0 + kt + 1) * 128], qc, True, True, [KTg, QTg], [psA if kt < 4 else psB])
                        if KSUB2 <= 1:
                            continue
                        kti = [gt0 - nh + ks0 + kt - 19 for kt in range(5)]
                        E0 = E0r.nxt()
                        ACT(E0.ap[:, 0:64], psA.t[:, 0:64], AF.Exp, [psA, cbias], [E0], scale=SC, bias=cbias.t[:, kti[0], h:h + 1])
                        ACT(E0.ap[64:128, 64:128], psA.t[64:128, 64:128], AF.Exp, [psA, cbias], [E0], scale=SC, bias=cbias.t[64:128, kti[0], h:h + 1])
                        E1 = E1r.nxt()
                        ACT(E1.ap[:, :], psA.t[:, 128:256], AF.Exp, [psA, cbias], [E1], scale=SC, bias=cbias.t[:, kti[1], h:h + 1])
                        E2 = E2r.nxt()
                        ACT(E2.ap[:, :], psA.t[:, 256:384], AF.Exp, [psA, cbias], [E2], scale=SC, bias=cbias.t[:, kti[2], h:h + 1])
                        t3 = tf.nxt()
                        ACT(t3.t[:, :128], psA.t[:, 384:512], AF.Exp, [psA, kneg], [t3], scale=SC, bias=kneg.t[:, kti[3]:kti[3] + 1])
                        E3 = E3r.nxt()
                        TT(E3.ap[:, :], t3.t[:, :128], eT3g.t[:, hh, :], ALU.mult, [t3, eT3g], [E3])
                        t4 = tf.nxt()
                        ACT(t4.t[0:64, :128], psB.t[0:64, 0:128], AF.Exp, [psB, kneg], [t4], scale=SC, bias=kneg.t[0:64, kti[4]:kti[4] + 1])
                        ACT(t4.t[64:128, 64:128], psB.t[64:128, 64:128], AF.Exp, [psB, kneg], [t4], scale=SC, bias=kneg.t[64:128, kti[4]:kti[4] + 1])
                        E4 = E4r.nxt()
                        TT(E4.ap[0:64, :], t4.t[0:64, :128], eT4g.t[0:64, hh, :], ALU.mult, [t4, eT4g], [E4])
                        TT(E4.ap[64:128, 64:128], t4.t[64:128, 64:128], eT4g.t[64:128, hh, 64:128], ALU.mult, [t4, eT4g], [E4])
                        Es = [E0, E1, E2, E3, E4]
                        if KSUB2 <= 2:
                            continue
                        pden = pss_.nxt()
                        KSUB3 = int(os.environ.get("KSUB3", "3"))
                        for kt in range(5):
                            if KSUB3 & 1:
                                MM(pden.t[:, 0:128], onesb.t[:], Es[kt].ap[:, :], kt == 0, kt == 4, [onesb, Es[kt]], [pden])
                            if KSUB3 & 16:
                                MM(pden.t[:, 0:128], onesb.t[:], g4.t[:, 0, kt * 128:(kt + 1) * 128], kt == 0, kt == 4, [onesb, g4], [pden])
                            if KSUB3 & 32:
                                MM(pden.t[:, 0:128], onesb.t[:], Es[kt].ap[:, :], kt == 0, kt == 4, [onesb], [pden])
                            if KSUB3 & 4:
                                MM(pden.t[:, 0:128], onesb.t[:], Es[1].ap[:, :], kt == 0, kt == 4, [onesb, Es[1]], [pden])
                            if KSUB3 & 8:
                                MM(pden.t[:, 0:128], onesb.t[:], Es[kt].ap[:, :], True, True, [onesb, Es[kt]], [pden])
                        po = pss_.nxt()
                        for kt in range(5):
                            if KSUB3 & 2:
                                MM(po.t[:, 0:128], Vg.t[:, ks0 + kt, hs], Es[kt].ap[:, :], kt == 0, kt == 4, [Vg, Es[kt]], [po])
                        if KSUB2 <= 3:
                            continue
                        rd = tf.nxt()
                        TS(rd.t[:, :128], pden.t[:, 0:128], 1e-30, ALU.add, [pden], [rd])
                        rr = tf.nxt()
                        RECIP(rr.t[:, :128], rd.t[:, :128], [rd], [rr])
                        TT(g4.t[:, hh, j * 128:(j + 1) * 128], po.t[:, 0:128], rr.t[:, :128], ALU.mult, [po, rr], [g4])
                if samp:
                    kc3 = kcb.ap[:, :].rearrange("p (t c) -> p t c", c=256)
                    vc3 = vcb.ap[:, :].rearrange("p (t c) -> p t c", c=256)
                    for s in range(4):
                        DMA(kc3, ck[s][:, 256 * g:256 * (g + 1)].rearrange("(t p) c -> p t c", p=128), [], [kcb], q="pool")
                        DMA(vc3, cv[s][:, 256 * g:256 * (g + 1)].rearrange("(t p) c -> p t c", p=128), [], [vcb], q="pool")
                        pt = ptr.nxt()
                        for hh in range(2):
                            for kt in range(4):
                                i = hh * 4 + kt
                                TR(pt.t[:, i * 128:(i + 1) * 128], kc3[:, kt, hh * 128:(hh + 1) * 128], identb.t[:], [kcb, identb], [pt])
                        CPA(KcT.ap[:, :], pt.t[:, :], [pt], [KcT])
                        for hh in range(2):
                            h = 2 * g + hh
                            hs = slice(hh * 128, (hh + 1) * 128)
                            qc = QT3[:, hh, scol + 32 * s:scol + 32 * s + 32]
                            psc = pss_.nxt()
                            for kt in range(4):
                                i = hh * 4 + kt
                                MM(psc.t[:, kt * 32:(kt + 1) * 32], KcT.ap[:, i * 128:(i + 1) * 128], qc, True, True, [KcT, QTg], [psc])
                            MM(psc.t[:, 128:160], KT3[:, hh, nkt * 128:(nkt + 1) * 128], qc, True, True, [KTg, QTg], [psc])
                            Ec = Ecr.nxt()
                            ACT(Ec.ap[:, 0:96], psc.t[:, 0:96], AF.Exp, [psc, c0t], [Ec], scale=SC, bias=c0t.t[:, h:h + 1])
                            t3 = tf.nxt()
                            ACT(t3.t[:, :32], psc.t[:, 96:128], AF.Exp, [psc], [t3], scale=SC)
                            TT(Ec.ap[:, 96:128], t3.t[:, :32], eT3g.t[:, hh, 0:32], ALU.mult, [t3, eT3g], [Ec])
                            tn = tf.nxt()
                            ACT(tn.t[:, :32], psc.t[:, 128:160], AF.Exp, [psc], [tn], scale=SC)
                            STT(Ec.ap[:, 128:160], tn.t[:, :32], seqm.t[:, s:s + 1], eTng.t[:, hh, :], ALU.mult, ALU.mult, [tn, seqm, eTng], [Ec])
                            pden = pss_.nxt()
                            po = pss_.nxt()
                            for kt in range(5):
                                ec = Ec.ap[:, 128:160] if kt == 4 else Ec.ap[:, kt * 32:(kt + 1) * 32]
                                MM(pden.t[:, 0:32], onesb.t[:], ec, kt == 0, kt == 4, [onesb, Ec], [pden])
                            for kt in range(5):
                                ec = Ec.ap[:, 128:160] if kt == 4 else Ec.ap[:, kt * 32:(kt + 1) * 32]
                                lv = Vg.t[:, nkt, hs] if kt == 4 else vc3[:, kt, hs]
                                MM(po.t[:, 0:32], lv, ec, kt == 0, kt == 4, [Vg, vcb, Ec], [po])
                            rd = tf.nxt()
                            TS(rd.t[:, :32], pden.t[:, 0:32], 1e-30, ALU.add, [pden], [rd])
                            rr = tf.nxt()
                            RECIP(rr.t[:, :32], rd.t[:, :32], [rd], [rr])
                            TT(g4.t[:, hh, scol + 32 * s:scol + 32 * s + 32], po.t[:, 0:32], rr.t[:, :32], ALU.mult, [po, rr], [g4])
                if KSUB <= 3:
                    return
                Wo, Wov = load_rows(w_ob[256 * g:256 * (g + 1), :])
                wo_partial(Wo, Wov, qa, Tn_)


            KONLY = int(os.environ.get("KONLY", "0"))
            for g in range(8):
                if KONLY:
                    break
                Wv, Wvv = load_unit(w_qkv[:, 4096 + 256 * g:4096 + 256 * (g + 1)])
                Wk, Wkv = load_unit(w_qkv[:, 2048 + 256 * g:2048 + 256 * (g + 1)])
                DMA(T3g.t[:], d_T3[:, 2 * g:2 * g + 2, :], [], [T3g])
                DMA(T4g.t[:], d_T4[:, 2 * g:2 * g + 2, :], [], [T4g])
                DMA(Tng.t[:], d_Tn[:, 2 * g:2 * g + 2, :], [], [Tng])
                ACT(eT3g.t[:], T3g.t[:], AF.Exp, [T3g], [eT3g])
                ACT(eT4g.t[:], T4g.t[:], AF.Exp, [T4g], [eT4g])
                ACT(eTng.t[:], Tng.t[:], AF.Exp, [Tng], [eTng])
                if nh:
                    DMA(Vg.t[:, 0:4, :], vscr[:, 256 * g:256 * (g + 1)].rearrange("(t p) c -> p t c", p=128), [Bvscr], [Vg])
                    for hh in range(2):
                        DMA(KT3[:, hh, 0:512], kscr[2 * g + hh], [Bkscr], [KTg])
                for j, gt in enumerate(tiles):
                    pv = pm.nxt()
                    for kc in range(16):
                        MM(pv.t[:, :256], hT.t[:, kc, j * 128:(j + 1) * 128], Wvv[:, kc, :], kc == 0, kc == 15, [hT, Wv], [pv])
                    CPA(Vg.t[:, nh + j, :], pv.t[:, :256], [pv], [Vg])
                    orow = out_tile_row(gt)
                    if orow is not None:
                        st = tf.nxt()
                        CPA(st.t[:, :256], pv.t[:, :256], [pv], [st])
                        dst = o_vs if orow[0] == "s" else o_vp[orow[1]:orow[1] + 128, :]
                        DMA(dst[:, 256 * g:256 * (g + 1)], st.t[:, :256], [st], [])
                Wq, Wqv = load_unit(w_qkv[:, 256 * g:256 * (g + 1)])
                for hh in range(2):
                    h = 2 * g + hh
                    for (c0, n, _) in chunks(0, Tn_):
                        pk = proj_T(Wk, Wkv, hh, c0, n)
                        kn = normalize(pk, n, 2)
                        CPA(KT3[:, hh, nh * 128 + c0:nh * 128 + c0 + n], kn.t[:, :n], [kn], [KTg])
                        for jj in range(n // 128):
                            gt = tiles[c0 // 128 + jj]
                            orow = out_tile_row(gt)
                            if orow is None:
                                continue
                            pso = pss_.nxt()
                            TR(pso.t[:, 0:128], kn.t[:, jj * 128:(jj + 1) * 128], identf.t[:], [kn, identf], [pso])
                            st = tf.nxt()
                            CPV(st.t[:, :128], pso.t[:, 0:128], [pso], [st])
                            dst = o_ks if orow[0] == "s" else o_kp[orow[1]:orow[1] + 128, :]
                            DMA(dst[:, h * 128:(h + 1) * 128], st.t[:, :128], [st], [])
                    for (c0, n, _) in chunks(qa, Tn_):
                        pq = proj_T(Wq, Wqv, hh, c0, n)
                        qn = normalize(pq, n, 1)
                        CPA(QT3[:, hh, c0:c0 + n], qn.t[:, :n], [qn], [QTg])
                if KSUB <= 1:
                    continue
                if nh == 0:
                    DMA(vscr[:, 256 * g:256 * (g + 1)].rearrange("(t p) c -> p t c", p=128), Vg.t[:, nt - 4:nt, :], [Vg], [Bvscr])
                    for hh in range(2):
                        DMA(kscr[2 * g + hh], KT3[:, hh, (nt - 4) * 128:nt * 128], [KTg], [Bkscr])
                if KSUB <= 2:
                    continue
                attn_inner(g)
            if KONLY:
                for g in range(KONLY):
                    attn_inner(g)

        blocks = [
            dict(tiles=list(range(0, 7)), full=False, sample=False, last=False, first=True),
            dict(tiles=list(range(7, 14)), full=False, sample=False, last=False, first=False),
            dict(tiles=list(range(14, 19)), full=False, sample=False, last=False, first=False),
            dict(tiles=list(range(19, 26)), full=True, sample=False, last=False, first=True, nh=0, qj0=4),
            dict(tiles=list(range(26, 32)), full=True, sample=True, last=True, first=False, nh=4, qj0=0),
        ]
        import os
        STOP = int(os.environ.get("KSTOP", "99"))
        for bi, blk in enumerate(blocks):
            nt = len(blk["tiles"])
            Tn_ = (nt + (1 if blk["sample"] else 0)) * 128
            if STOP <= 1:
                break
            KONLY = int(os.environ.get("KONLY", "0"))
            if KONLY:
                if bi != 3:
                    continue
                stage_attn(blk)
                break
            P.barrier()
            load_x(blk)
            norm(0, 0, Tn_)
            if STOP <= 2:
                break
            stage_hgrn(blk)
            if STOP <= 3:
                break
            if STOP <= 4 and bi == 2:
                break
            if blk["full"]:
                qa = blk["qj0"] * 128
                if STOP <= 5:
                    break
                stage_ffn(0, blk, 0)
                if STOP <= 6:
                    break
                norm(2, 0, Tn_)
                P.barrier()
                stage_attn(blk)
                if STOP <= 7:
                    break
                stage_ffn(1, blk, qa)
                store_x(blk)
                if STOP <= 8:
                    break
        P.finish()
        with nc.Block() as block:
            P.replay(block)
    return nc


_CACHE = {}


def _host_consts():
    ident = np.eye(128, dtype=np.float32)
    s = np.arange(128)[:, None]
    t = np.arange(128)[None, :]
    cmask = (s <= t).astype(np.float32)
    smask = ((s <= t) & (s // 32 == t // 32)).astype(np.float32)
    scanp = np.ones((128, 512), np.float32)
    scanp[:, ::64] = 0.0
    scans = np.ones((128, 128), np.float32)
    scans[:, ::32] = 0.0
    seqm = (np.arange(128)[:, None] // 32 == np.arange(4)[None, :]).astype(np.float32)
    return dict(ident=ident, cmask=cmask, smask=smask, scanp=scanp, scans=scans, seqm=seqm)


def kernel(x_prompt, x_sample, state_a_S, cache_b_k, cache_b_v, state_ffn_conv,
           norm_mix, norm_ffn, a_w_in, a_gamma_lb, a_norm_o, a_w_o,
           b_w_qkv, b_q_norm, b_k_norm, b_rel_bias, b_w_o,
           f_w_in, f_conv_w, f_conv_b, f_w_down):
    f32 = lambda a: np.ascontiguousarray(np.asarray(a, dtype=np.float32))
    x_prompt, x_sample = f32(x_prompt), f32(x_sample)
    if "nc" not in _CACHE:
        _CACHE["nc"] = build_program()
    nc = _CACHE["nc"]
    cst_ = _host_consts()
    gains = np.stack([f32(norm_mix)[0], f32(norm_ffn)[0], f32(norm_mix)[1], f32(norm_ffn)[1]], 0)
    gains = np.ascontiguousarray(gains.reshape(4, 16, 128).transpose(2, 0, 1))
    gam = np.ascontiguousarray(f32(a_gamma_lb).reshape(3, 16, 128).transpose(2, 0, 1))
    pvec = np.ascontiguousarray(np.stack([f32(a_norm_o)[0], f32(b_q_norm)[0], f32(b_k_norm)[0]], 1))
    rb = f32(b_rel_bias)[0]
    c0t = np.ascontiguousarray(np.broadcast_to(rb[0][None, :], (128, 16)))
    kk = np.arange(128)[:, None]
    qq = np.arange(128)[None, :]
    T3 = np.ascontiguousarray(rb[np.maximum(kk - qq, 0)].transpose(0, 2, 1))
    T4 = np.ascontiguousarray(rb[np.minimum(kk - qq, 63) + 128].transpose(0, 2, 1))
    ii = np.arange(32)[None, :]
    Tn = np.ascontiguousarray(rb[(kk % 32) - ii + 128].transpose(0, 2, 1))
    cw = np.ascontiguousarray(f32(f_conv_w).reshape(2, 3, NFU, 128).transpose(3, 0, 1, 2))
    cb = np.ascontiguousarray(f32(f_conv_b).reshape(2, NFU, 128).transpose(2, 0, 1))
    shared = dict(w_in=f32(a_w_in)[0], w_oa=f32(a_w_o)[0], w_qkv=f32(b_w_qkv)[0], w_ob=f32(b_w_o)[0],
                  f_in=f32(f_w_in), f_dn=f32(f_w_down), gains=gains, gam=gam, pvec=pvec, c0t=c0t,
                  T3=T3, T4=T4, Tn=Tn, cw=cw, cb=cb, **cst_)
    sS = f32(state_a_S)[0]
    cK = f32(cache_b_k)[0].reshape(32, 512, D)
    cV = f32(cache_b_v)[0].reshape(32, 512, D)
    cst = f32(state_ffn_conv)
    in_maps = []
    for c in range(8):
        b, q = c // 4, c % 4
        base = 1024 * (q - 3)
        xs = np.zeros((33 * 128, D), np.float32)
        lo = max(base, 0)
        xs[lo - base:4096] = x_prompt[b, lo:base + 4096]
        xs[4096:] = x_sample[4 * c:4 * c + 4].reshape(128, D)
        kneg = np.zeros((128, 13), np.float32)
        for t in range(13):
            if base + (19 + t) * 128 < 0:
                kneg[:, t] = NEG
        m = dict(shared)
        m.update(xs=xs, s0=np.ascontiguousarray(sS[4 * c:4 * c + 4]), ck=np.ascontiguousarray(cK[4 * c:4 * c + 4]),
                 cv=np.ascontiguousarray(cV[4 * c:4 * c + 4]),
                 cst=np.ascontiguousarray(cst[:, 4 * c:4 * c + 4].reshape(16, DFF)), kneg=kneg)
        in_maps.append(m)
    import os
    ncores = int(os.environ.get("KCORES", "8"))
    res = run_bass_kernel_spmd(nc, in_maps[:ncores], core_ids=list(range(ncores)))
    R = list(res.results) + [res.results[0]] * (8 - ncores)
    y_p = np.zeros((2, 4096, D), np.float32)
    y_s = np.zeros((32, 32, D), np.float32)
    Sp = np.zeros((1, 2, 16, 128, 128), np.float32)
    Ss = np.zeros((1, 32, 16, 128, 128), np.float32)
    kp = np.zeros((1, 2, 512, 16, 128), np.float32)
    vp = np.zeros((1, 2, 512, 16, 128), np.float32)
    ks = np.zeros((1, 32, 32, 16, 128), np.float32)
    vs = np.zeros((1, 32, 32, 16, 128), np.float32)
    cp = np.zeros((2, 2, 2, DFF), np.float32)
    cs = np.zeros((2, 32, 2, DFF), np.float32)
    for c in range(8):
        b, q = c // 4, c % 4
        r = R[c]
        y_p[b, 1024 * q:1024 * (q + 1)] = r["yp"]
        y_s[4 * c:4 * c + 4] = r["ys"].reshape(4, 32, D)
        Ss[0, 4 * c:4 * c + 4] = r["Ss"]
        ks[0, 4 * c:4 * c + 4] = r["ks"].reshape(4, 32, 16, 128)
        vs[0, 4 * c:4 * c + 4] = r["vs"].reshape(4, 32, 16, 128)
        cs[:, 4 * c:4 * c + 4] = r["co"][:, 2:10].reshape(2, 4, 2, DFF)
        if q == 3:
            Sp[0, b] = r["Sp"]
            kp[0, b] = r["kp"].reshape(512, 16, 128)
            vp[0, b] = r["vp"].reshape(512, 16, 128)
            cp[:, b] = r["co"][:, 0:2]
    return (y_p, y_s, Sp, Ss, kp, vp, ks, vs, cp, cs)
```

```python
import numpy as np
from contextlib import ExitStack
import concourse.bass as bass
import concourse.mybir as mybir
from concourse.bass_utils import run_bass_kernel_spmd

F32 = mybir.dt.float32
BF16 = mybir.dt.bfloat16
AF = mybir.ActivationFunctionType
ALU = mybir.AluOpType

D = 2048
DFF = 5632
NFU = 44
EPS = 1e-6
NEG = -30000.0
SC = 128.0 ** -0.5


class Buf:
    __slots__ = ("name", "w", "r")

    def __init__(self, name="b"):
        self.name = name
        self.w = None
        self.r = {}


class Prog:
    ENGS = ("pe", "act", "dve", "pool", "sp")

    def __init__(self, nc, n_dma_sp=8, n_dma_pool=6):
        self.nc = nc
        self.ops = {e: [] for e in self.ENGS}
        self.known = {e: {} for e in self.ENGS}
        self.cnt = {e: 0 for e in self.ENGS}
        self.sems = {}
        self.dma_pool = {"sp": [f"dsp{i}" for i in range(n_dma_sp)],
                         "pool": [f"dpl{i}" for i in range(n_dma_pool)]}
        self.dma_tgt = {}
        self.dma_idx = {"sp": 0, "pool": 0}

    def semkeys(self):
        ks = [f"c_{e}" for e in ("pe", "act", "dve", "pool")]
        for q in self.dma_pool.values():
            ks += q
        return ks

    def _collect(self, eng, reads, writes):
        waits = {}

        def add(ev):
            if ev is None:
                return
            k, v = ev
            if waits.get(k, 0) < v:
                waits[k] = v
        for b in reads:
            add(b.w)
        for b in writes:
            add(b.w)
            for k, v in b.r.items():
                add((k, v))
        out = []
        kn = self.known[eng]
        for k, v in waits.items():
            if eng == "pe" and k == "c_pe":
                continue
            if kn.get(k, 0) >= v:
                continue
            kn[k] = v
            out.append((k, v))
        return out

    def _commit(self, ev, reads, writes):
        k, v = ev
        for b in reads:
            if b.r.get(k, 0) < v:
                b.r[k] = v
        for b in writes:
            b.w = ev
            b.r = {}

    def op(self, eng, fn, reads=(), writes=()):
        waits = self._collect(eng, reads, writes)
        self.cnt[eng] += 1
        ev = (f"c_{eng}", self.cnt[eng])
        self.ops[eng].append((waits, fn, (ev[0], 1)))
        self._commit(ev, reads, writes)
        return ev

    def dma(self, q, out, in_, reads=(), writes=(), **kw):
        pool = self.dma_pool[q]
        i = self.dma_idx[q]
        self.dma_idx[q] = i + 1
        sk = pool[i % len(pool)]
        prev = self.dma_tgt.get(sk, 0)
        waits = self._collect(q, reads, writes)
        if prev > 0 and self.known[q].get(sk, 0) < prev:
            self.known[q][sk] = prev
            waits.append((sk, prev))
        tgt = prev + 16
        self.dma_tgt[sk] = tgt
        ev = (sk, tgt)
        self.ops[q].append((waits, (lambda e, o=out, i_=in_, k=kw: e.dma_start(out=o, in_=i_, **k)), (sk, 16)))
        self._commit(ev, reads, writes)
        return ev

    def _all_events(self):
        waits = [(sk, t) for sk, t in self.dma_tgt.items() if t > 0]
        for e in ("pe", "act", "dve", "pool"):
            if self.cnt[e] > 0:
                waits.append((f"c_{e}", self.cnt[e]))
        return waits

    def barrier(self):
        allw = self._all_events()
        for e in self.ENGS:
            w = []
            for k, v in allw:
                if k == f"c_{e}" and e == "pe":
                    continue
                if self.known[e].get(k, 0) < v:
                    self.known[e][k] = v
                    w.append((k, v))
            if w:
                self.ops[e].append((w, None, None))

    def finish(self):
        self.ops["sp"].append((self._all_events(), None, None))

    def replay(self, block):
        sems = self.sems

        def run(engname):
            def f(e):
                for waits, fn, inc in self.ops[engname]:
                    for k, v in waits:
                        e.wait_ge(sems[k], v)
                    if fn is not None:
                        ins = fn(e)
                        if inc is not None:
                            ins.then_inc(sems[inc[0]], inc[1])
            return f
        block.tensor(run("pe"))
        block.scalar(run("act"))
        block.vector(run("dve"))
        block.gpsimd(run("pool"))
        block.sync(run("sp"))


class T:
    def __init__(self, t, name):
        self.t = t
        self.b = Buf(name)

    def __getitem__(self, idx):
        return self.t[idx]


class Ring:
    def __init__(self, items):
        self.items = items
        self.i = 0

    def nxt(self):
        x = self.items[self.i % len(self.items)]
        self.i += 1
        return x


def chunks(a, b, scol=None):
    out = []
    end = b if scol is None else min(b, scol)
    c = a
    while c < end:
        n = min(512, end - c)
        out.append((c, n, False))
        c += n
    if scol is not None and b > scol:
        out.append((scol, 128, True))
    return out


def build_program():
    nc = bass.Bass("TRN2", target_bir_lowering=False)

    def din(name, shape):
        return nc.dram_tensor(name, shape, F32, kind="ExternalInput").ap()

    def dout(name, shape):
        return nc.dram_tensor(name, shape, F32, kind="ExternalOutput").ap()

    xs = din("xs", [33 * 128, D])
    s0 = din("s0", [4, 16, 128, 128])
    ck = din("ck", [4, 512, D])
    cv = din("cv", [4, 512, D])
    cst = din("cst", [16, DFF])
    w_in = din("w_in", [D, 8192])
    w_oa = din("w_oa", [D, D])
    w_qkv = din("w_qkv", [D, 6144])
    w_ob = din("w_ob", [D, D])
    f_in = din("f_in", [2, D, 2 * DFF])
    f_dn = din("f_dn", [2, DFF, D])
    d_gains = din("gains", [128, 4, 16])
    d_gam = din("gam", [128, 3, 16])
    d_pvec = din("pvec", [128, 3])
    d_c0 = din("c0t", [128, 16])
    d_T3 = din("T3", [128, 16, 128])
    d_T4 = din("T4", [128, 16, 128])
    d_Tn = din("Tn", [128, 16, 32])
    d_cw = din("cw", [128, 2, 3, NFU])
    d_cb = din("cb", [128, 2, NFU])
    d_ident = din("ident", [128, 128])
    d_cmask = din("cmask", [128, 128])
    d_smask = din("smask", [128, 128])
    d_scanp = din("scanp", [128, 512])
    d_scans = din("scans", [128, 128])
    d_kneg = din("kneg", [128, 13])
    d_seqm = din("seqm", [128, 4])

    o_yp = dout("yp", [1024, D])
    o_ys = dout("ys", [128, D])
    o_Sp = dout("Sp", [16, 128, 128])
    o_Ss = dout("Ss", [4, 16, 128, 128])
    o_kp = dout("kp", [512, D])
    o_vp = dout("vp", [512, D])
    o_ks = dout("ks", [128, D])
    o_vs = dout("vs", [128, D])
    o_co = dout("co", [2, 10, DFF])

    kscr = nc.dram_tensor("kscr", [16, 128, 512], BF16, kind="Internal").ap()
    vscr = nc.dram_tensor("vscr", [512, D], BF16, kind="Internal").ap()
    Bkscr = Buf("kscr")
    Bvscr = Buf("vscr")

    P = Prog(nc)
    es = ExitStack()
    with es:
        for k in P.semkeys():
            P.sems[k] = es.enter_context(nc.semaphore(k))

        def sb(name, shape, dt):
            return T(es.enter_context(nc.sbuf_tensor("s_" + name, shape, dt)), name)

        def ps(name, shape, dt):
            return T(es.enter_context(nc.psum_tensor("p_" + name, shape, dt)), name)

        def bufs(lst):
            return [x.b if hasattr(x, 'b') else x for x in lst]

        def ACT(out, in_, func, R, W, **kw):
            P.op("act", lambda e: e.activation(out=out, in_=in_, func=func, **kw), bufs(R), bufs(W))

        def MM(out, lhsT, rhs, start, stop, R, W):
            P.op("pe", lambda e: e.matmul(out, lhsT, rhs, start=start, stop=stop), bufs(R), bufs(W))

        def TR(out, in_, ident, R, W):
            P.op("pe", lambda e: e.transpose(out, in_, ident), bufs(R), bufs(W))

        def TT(out, in0, in1, op, R, W):
            P.op("dve", lambda e: e.tensor_tensor(out=out, in0=in0, in1=in1, op=op), bufs(R), bufs(W))

        def TS(out, in0, s1, op0, R, W, s2=None, op1=None):
            if op1 is None:
                P.op("dve", lambda e: e.tensor_scalar(out=out, in0=in0, scalar1=s1, scalar2=None, op0=op0), bufs(R), bufs(W))
            else:
                P.op("dve", lambda e: e.tensor_scalar(out=out, in0=in0, scalar1=s1, scalar2=s2, op0=op0, op1=op1), bufs(R), bufs(W))

        def STT(out, in0, scalar, in1, op0, op1, R, W):
            P.op("dve", lambda e: e.scalar_tensor_tensor(out=out, in0=in0, scalar=scalar, in1=in1, op0=op0, op1=op1), bufs(R), bufs(W))

        def CPV(out, in_, R, W):
            P.op("dve", lambda e: e.tensor_copy(out=out, in_=in_), bufs(R), bufs(W))

        def RECIP(out, in_, R, W):
            P.op("dve", lambda e: e.reciprocal(out=out, in_=in_), bufs(R), bufs(W))

        def SCAN(out, d0, d1, R, W):
            P.op("dve", lambda e: e.tensor_tensor_scan(out=out, data0=d0, data1=d1, initial=0.0, op0=ALU.mult, op1=ALU.add), bufs(R), bufs(W))

        def MEMSET(ap, val, W):
            P.op("dve", lambda e: e.memset(ap, val), [], bufs(W))

        def CPA(out, in_, R, W):
            ACT(out, in_, AF.Copy, R, W)

        def DMA(out, in_, R, W, q="sp"):
            P.dma(q, out, in_, bufs(R), bufs(W))

        xT = sb("xT", [128, 16, 896], F32)
        hT = sb("hT", [128, 16, 896], BF16)
        wring = Ring([sb(f"wr{i}", [128, 4096], BF16) for i in range(3)])
        Vg = sb("Vg", [128, 12, 256], BF16)
        g4 = sb("g4", [128, 2, 896], BF16)
        oacc = sb("oacc", [128, 896], F32)
        tf = Ring([sb(f"tf{i}", [128, 512], F32) for i in range(6)])
        tb = Ring([sb(f"tb{i}", [128, 512], BF16) for i in range(4)])
        us = Ring([sb(f"us{i}", [128, 516], F32) for i in range(2)])
        S = sb("S", [128, 16, 128], F32)
        Sall = sb("Sall", [128, 8, 128], F32)
        tmpall = sb("tmpall", [128, 7, 128], F32)
        cmask4 = sb("cmask4", [128, 4, 128], F32)
        S0f = sb("S0f", [128, 4, 128], F32)
        arena = sb("arena", [128, 9600], BF16)
        identf = sb("identf", [128, 128], F32)
        identb = sb("identb", [128, 128], BF16)
        onesb = sb("onesb", [128, 128], BF16)
        cmask = sb("cmask", [128, 128], F32)
        smask = sb("smask", [128, 128], F32)
        scanp = sb("scanp", [128, 512], F32)
        scans = sb("scans", [128, 128], F32)
        gains = sb("gains", [128, 4, 16], F32)
        gam = sb("gam", [128, 3, 16], F32)
        lbt = sb("lbt", [128, 8, 16], F32)
        oml = sb("oml", [128, 16], F32)
        pvec = sb("pvec", [128, 3], F32)
        c0t = sb("c0t", [128, 16], F32)
        kneg = sb("kneg", [128, 13], F32)
        cbias = sb("cbias", [128, 13, 16], F32)
        seqm = sb("seqm", [128, 4], F32)
        cw = sb("cw", [128, 2, 3, NFU], F32)
        cb = sb("cb", [128, 2, NFU], F32)
        cstT = sb("cstT", [128, NFU, 16], F32)
        ucol = sb("ucol", [128, NFU, 10], F32)
        uH = sb("uH", [128, 2, NFU, 2], F32)
        onec = sb("onec", [128, 1], F32)
        epsc = sb("epsc", [128, 1], F32)
        Et = sb("Et", [128, 16], F32)
        E12t = sb("E12t", [128, 8], F32)
        Est = sb("Est", [128, 4], F32)
        T3g = sb("T3g", [128, 2, 128], F32)
        T4g = sb("T4g", [128, 2, 128], F32)
        Tng = sb("Tng", [128, 2, 32], F32)
        eT3g = sb("eT3g", [128, 2, 128], F32)
        eT4g = sb("eT4g", [128, 2, 128], F32)
        eTng = sb("eTng", [128, 2, 32], F32)

        pm = Ring([ps(f"pm{i}", [128, 512], F32) for i in range(4)])
        pss_ = Ring([ps(f"ps{i}", [128, 512], F32) for i in range(2)])
        ptr = Ring([ps(f"pt{i}", [128, 1024], BF16) for i in range(2)])

        class AV:
            def __init__(self, off, n, name):
                self.ap = arena.t[:, off:off + n]
                self.b = Buf(name)
                self.off = off

            def __getitem__(self, idx):
                return self.ap[idx]

        def carve(spec):
            off = 0
            d = {}
            for name, n in spec:
                d[name] = AV(off, n, name)
                off += n
            assert off <= 9600, off
            return d

        A_ = carve([("kT", 896), ("KpT", 896), ("qT", 896), ("Q2T", 896), ("gsT", 896), ("S0b", 512),
                    ("A0", 512), ("A1", 512), ("A2", 128), ("Km0", 128), ("Km1", 128), ("Ktall", 896), ("Sball", 896)])
        B_ = carve([("KTg", 3072), ("QTg", 1792), ("kc", 1024), ("vc", 1024), ("KcT", 1024),
                    ("E0a", 128), ("E0b", 128), ("E4a", 128), ("E4b", 128), ("E3a", 128), ("E3b", 128),
                    ("E1a", 128), ("E1b", 128), ("E2a", 128), ("E2b", 128), ("Eca", 160), ("Ecb", 160)])

        for t_, d_ in ((identf, d_ident), (cmask, d_cmask), (smask, d_smask), (scanp, d_scanp), (scans, d_scans),
                       (gains, d_gains), (gam, d_gam), (pvec, d_pvec), (c0t, d_c0), (kneg, d_kneg), (seqm, d_seqm),
                       (cw, d_cw), (cb, d_cb)):
            DMA(t_.t[:], d_, [], [t_])
        DMA(identb.t[:], d_ident, [], [identb], q="pool")
        MEMSET(onesb.t[:], 1.0, [onesb])
        MEMSET(onec.t[:], 1.0, [onec])
        MEMSET(epsc.t[:], EPS, [epsc])
        MEMSET(S.t[:], 0.0, [S])
        for i4 in range(4):
            DMA(cmask4.t[:, i4, :], d_cmask, [], [cmask4])
        MEMSET(uH.t[:], 0.0, [uH])
        g0, g1, g2 = gam.t[:, 0, :], gam.t[:, 1, :], gam.t[:, 2, :]
        L = lambda i: lbt.t[:, i, :]
        TT(L(0), g0, g1, ALU.max, [gam], [lbt])
        TT(L(0), L(0), g2, ALU.max, [gam, lbt], [lbt])
        for i, gi in enumerate((g0, g1, g2)):
            TT(L(1 + i), gi, L(0), ALU.subtract, [gam, lbt], [lbt])
            ACT(L(1 + i), L(1 + i), AF.Exp, [lbt], [lbt])
        TT(L(4), L(1), L(2), ALU.add, [lbt], [lbt])
        TT(L(4), L(4), L(3), ALU.add, [lbt], [lbt])
        RECIP(L(5), L(4), [lbt], [lbt])
        TT(L(6), L(1), L(5), ALU.mult, [lbt], [lbt])
        TS(oml.t[:], L(6), -1.0, ALU.mult, [lbt], [oml], s2=1.0, op1=ALU.add)
        for t in range(13):
            TS(cbias.t[:, t, :], c0t.t[:], kneg.t[:, t:t + 1], ALU.add, [c0t, kneg], [cbias])
        for pc in range(11):
            tt_ = tf.nxt()
            DMA(tt_.t[0:16, :], cst[:, pc * 512:(pc + 1) * 512], [], [tt_])
            pp = pss_.nxt()
            for i in range(4):
                TR(pp.t[:, i * 16:(i + 1) * 16], tt_.t[0:16, i * 128:(i + 1) * 128], identf.t[0:16, 0:16], [tt_, identf], [pp])
            CPV(cstT.t[:, pc * 4:(pc + 1) * 4, :], pp.t[:, 0:64].rearrange("p (a b) -> p a b", b=16), [pp], [cstT])

        def load_unit(src):
            w = wring.nxt()
            v = w.t[:, 0:4096].rearrange("p (k n) -> p k n", n=256)
            DMA(v, src.rearrange("(kc p) n -> p kc n", p=128), [], [w], q="pool")
            return w, v

        def load_rows(src):
            w = wring.nxt()
            v = w.t[:, 0:4096].rearrange("p (k n) -> p k n", n=2048)
            for kc_ in range(2):
                P.dma("pool", v[:, kc_, :], src[kc_ * 128:(kc_ + 1) * 128, :], [], [w.b], max_dma_last_dim=4096)
            return w, v

        def load_x(blk):
            tiles = blk["tiles"] + ([32] if blk["sample"] else [])
            for j, gt in enumerate(tiles):
                for q4 in range(4):
                    xt_ = tf.nxt()
                    DMA(xt_.t[:], xs[gt * 128:(gt + 1) * 128, q4 * 512:(q4 + 1) * 512], [], [xt_])
                    pp = pm.nxt()
                    for i in range(4):
                        TR(pp.t[:, i * 128:(i + 1) * 128], xt_.t[:, i * 128:(i + 1) * 128], identf.t[:], [xt_, identf], [pp])
                    CPA(xT.t[:, q4 * 4:(q4 + 1) * 4, j * 128:(j + 1) * 128], pp.t[:].rearrange("p (a b) -> p a b", b=128), [pp], [xT])

        def store_x(blk):
            tiles = blk["tiles"] + ([32] if blk["sample"] else [])
            for j, gt in enumerate(tiles):
                if gt < 24:
                    continue
                dst = o_ys if gt == 32 else o_yp[(gt - 24) * 128:(gt - 23) * 128, :]
                for q4 in range(4):
                    pp = pm.nxt()
                    for i in range(4):
                        TR(pp.t[:, i * 128:(i + 1) * 128], xT.t[:, q4 * 4 + i, j * 128:(j + 1) * 128], identf.t[:], [xT, identf], [pp])
                    st = tf.nxt()
                    CPA(st.t[:], pp.t[:], [pp], [st])
                    DMA(dst[:, q4 * 512:(q4 + 1) * 512], st.t[:], [st], [])

        def rstd_from(p, n, scale):
            rs = tf.nxt()
            ACT(rs.t[:, :n], p.t[:, :n], AF.Sqrt, [p, epsc], [rs], scale=scale, bias=epsc.t[:])
            rr = tf.nxt()
            RECIP(rr.t[:, :n], rs.t[:, :n], [rs], [rr])
            return rr

        def norm(gi, a, b):
            for (c0, n, _) in chunks(a, b):
                pssq = pm.nxt()
                for kc in range(16):
                    sq = tb.nxt()
                    ACT(sq.t[:, :n], xT.t[:, kc, c0:c0 + n], AF.Square, [xT], [sq])
                    MM(pssq.t[:, :n], onesb.t[:], sq.t[:, :n], kc == 0, kc == 15, [onesb, sq], [pssq])
                rr = rstd_from(pssq, n, 1.0 / D)
                for kc in range(16):
                    STT(hT.t[:, kc, c0:c0 + n], xT.t[:, kc, c0:c0 + n], gains.t[:, gi, kc:kc + 1], rr.t[:, :n],
                        ALU.mult, ALU.mult, [xT, gains, rr], [hT])

        def proj_T(Wt, Wv, hh, c0, n):
            p = pm.nxt()
            for kc in range(16):
                MM(p.t[:, :n], Wv[:, kc, hh * 128:(hh + 1) * 128], hT.t[:, kc, c0:c0 + n], kc == 0, kc == 15, [Wt, hT], [p])
            return p

        def wo_partial(Wt, Wv, a, b):
            for cu in range(16):
                for (c0, n, _) in chunks(a, b):
                    py = pm.nxt()
                    for hh in range(2):
                        MM(py.t[:, :n], Wv[:, hh, cu * 128:(cu + 1) * 128], g4.t[:, hh, c0:c0 + n], hh == 0, hh == 1, [Wt, g4], [py])
                    TT(xT.t[:, cu, c0:c0 + n], xT.t[:, cu, c0:c0 + n], py.t[:, :n], ALU.add, [xT, py], [xT])

        def stage_hgrn(blk):
            nt = len(blk["tiles"])
            samp = blk["sample"]
            full = blk["full"]
            scol = nt * 128 if samp else None
            ntt = nt + (1 if samp else 0)
            Tn_ = ntt * 128
            kT, KpT, qT, Q2T, gsT, S0b = A_["kT"], A_["KpT"], A_["qT"], A_["Q2T"], A_["gsT"], A_["S0b"]
            Aring = Ring([A_["A0"], A_["A1"]])
            Ktall, Sball = A_["Ktall"], A_["Sball"]
            Kmr = Ring([A_["Km0"], A_["Km1"]])
            for a_ in (A_["A0"], A_["A1"]):
                MEMSET(a_.ap[:, :], 0.0, [a_])
            for g in range(8):
                Wi, Wiv = load_unit(w_in[:, 4096 + 256 * g:4096 + 256 * (g + 1)])
                Wf, Wfv = load_unit(w_in[:, 2048 + 256 * g:2048 + 256 * (g + 1)])
                for j in range(ntt):
                    pv = pm.nxt()
                    for kc in range(16):
                        MM(pv.t[:, :256], hT.t[:, kc, j * 128:(j + 1) * 128], Wiv[:, kc, :], kc == 0, kc == 15, [hT, Wi], [pv])
                    CPA(Vg.t[:, j, :], pv.t[:, :256], [pv], [Vg])
                if full:
                    Wq, Wqv = load_unit(w_in[:, 256 * g:256 * (g + 1)])
                    WgT, Wgv_ = load_unit(w_in[:, 6144 + 256 * g:6144 + 256 * (g + 1)])
                for hh in range(2):
                    h = 2 * g + hh
                    for (c0, n, is_s) in chunks(0, Tn_, scol):
                        pf = proj_T(Wf, Wfv, hh, c0, n)
                        sn = tf.nxt()
                        ACT(sn.t[:, :n], pf.t[:, :n], AF.Sigmoid, [pf], [sn], scale=-1.0)
                        kk = tf.nxt()
                        TS(kk.t[:, :n], sn.t[:, :n], oml.t[:, h:h + 1], ALU.mult, [sn, oml], [kk])
                        lf = tf.nxt()
                        ACT(lf.t[:, :n], kk.t[:, :n], AF.Ln, [kk, onec], [lf], scale=-1.0, bias=onec.t[:])
                        bb = tf.nxt()
                        mk = scans.t[:, 0:n] if is_s else scanp.t[:, 0:n]
                        SCAN(bb.t[:, :n], mk, lf.t[:, :n], [lf, scans, scanp], [bb])
                        enb = tf.nxt()
                        ACT(enb.t[:, :n], bb.t[:, :n], AF.Exp, [bb], [enb], scale=-1.0)
                        TT(kT.ap[:, c0:c0 + n], kk.t[:, :n], enb.t[:, :n], ALU.mult, [kk, enb], [kT])
                        if is_s:
                            ACT(Est.t[:, 0:4], bb.t[:, 0:128].rearrange("p (a b) -> p a b", b=32)[:, :, 31], AF.Exp, [bb], [Est])
                        else:
                            ACT(Et.t[:, c0 // 64:(c0 + n) // 64], bb.t[:, 0:n].rearrange("p (a b) -> p a b", b=64)[:, :, 63],
                                AF.Exp, [bb], [Et])
                        if full:
                            pq = proj_T(Wq, Wqv, hh, c0, n)
                            eb = sn
                            ACT(eb.t[:, :n], bb.t[:, :n], AF.Exp, [bb], [eb])
                            TT(qT.ap[:, c0:c0 + n], pq.t[:, :n], eb.t[:, :n], ALU.mult, [pq, eb], [qT])
                    if nt > 0:
                        v3 = lambda av, lo, hi: av.ap[:, 0:nt * 128].rearrange("p (a b) -> p a b", b=128)[:, :, lo:hi]
                        CPA(v3(KpT, 64, 128), v3(kT, 64, 128), [kT], [KpT])
                        if full:
                            CPA(v3(Q2T, 0, 64), v3(qT, 0, 64), [qT], [Q2T])
                        for j in range(nt):
                            e1 = Et.t[:, 2 * j:2 * j + 1]
                            TS(KpT.ap[:, j * 128:j * 128 + 64], kT.ap[:, j * 128:j * 128 + 64], e1, ALU.mult, [kT, Et], [KpT])
                            if full:
                                TS(Q2T.ap[:, j * 128 + 64:(j + 1) * 128], qT.ap[:, j * 128 + 64:(j + 1) * 128], e1, ALU.mult, [qT, Et], [Q2T])
                        ev = Et.t[:, 0:2 * nt].rearrange("p (a b) -> p a b", b=2)
                        TT(E12t.t[:, 0:nt], ev[:, :, 0], ev[:, :, 1], ALU.mult, [Et], [E12t])
                    if nt > 0:
                        hs_ = slice(hh * 128, (hh + 1) * 128)
                        pt = ptr.nxt()
                        for j in range(nt):
                            TR(pt.t[:, j * 128:(j + 1) * 128], KpT.ap[:, j * 128:(j + 1) * 128], identb.t[:], [KpT, identb], [pt])
                        CPA(Ktall.ap[:, 0:nt * 128], pt.t[:, 0:nt * 128], [pt], [Ktall])
                        for j0 in range(0, nt, 4):
                            jn = min(4, nt - j0)
                            px = pm.nxt()
                            for jj in range(jn):
                                j = j0 + jj
                                MM(px.t[:, jj * 128:(jj + 1) * 128], Ktall.ap[:, j * 128:(j + 1) * 128], Vg.t[:, j, hs_], True, True, [Ktall, Vg], [px])
                            for jj in range(jn):
                                j = j0 + jj
                                TS(tmpall.t[:, j, :], px.t[:, jj * 128:(jj + 1) * 128], Et.t[:, 2 * j + 1:2 * j + 2], ALU.mult, [px, Et], [tmpall])
                        CPV(Sall.t[:, 0, :], S.t[:, h, :], [S], [Sall])
                        for j in range(nt):
                            STT(Sall.t[:, j + 1, :], Sall.t[:, j, :], E12t.t[:, j:j + 1], tmpall.t[:, j, :], ALU.mult, ALU.add, [Sall, E12t, tmpall], [Sall])
                        CPV(S.t[:, h, :], Sall.t[:, nt, :], [Sall], [S])
                        if full:
                            CPA(Sball.ap[:, 0:nt * 128], Sall.t[:, 0:nt, :].rearrange("p a b -> p (a b)"), [Sall], [Sball])
                            for j0 in range(0, nt, 4):
                                jn = min(4, nt - j0)
                                psc = pm.nxt()
                                for jj in range(jn):
                                    j = j0 + jj
                                    MM(psc.t[0:64, jj * 128:jj * 128 + 64], kT.ap[:, j * 128:j * 128 + 64], qT.ap[:, j * 128:j * 128 + 64], True, True, [kT, qT], [psc])
                                    MM(psc.t[:, jj * 128 + 64:(jj + 1) * 128], KpT.ap[:, j * 128:(j + 1) * 128], qT.ap[:, j * 128 + 64:(j + 1) * 128], True, True, [KpT, qT], [psc])
                                A = Aring.nxt()
                                a3 = A.ap[:, 0:jn * 128].rearrange("p (a b) -> p a b", b=128)
                                p3 = psc.t[:, 0:jn * 128].rearrange("p (a b) -> p a b", b=128)
                                TT(a3[0:64, :, 0:64], p3[0:64, :, 0:64], cmask4.t[0:64, 0:jn, 0:64], ALU.mult, [psc, cmask4], [A])
                                TT(a3[:, :, 64:128], p3[:, :, 64:128], cmask4.t[:, 0:jn, 64:128], ALU.mult, [psc, cmask4], [A])
                                po = pm.nxt()
                                for jj in range(jn):
                                    j = j0 + jj
                                    MM(po.t[:, jj * 128:(jj + 1) * 128], Vg.t[:, j, hs_], A.ap[:, jj * 128:(jj + 1) * 128], True, False, [Vg, A], [po])
                                    MM(po.t[:, jj * 128:(jj + 1) * 128], Sball.ap[:, j * 128:(j + 1) * 128], Q2T.ap[:, j * 128:(j + 1) * 128], False, True, [Sball, Q2T], [po])
                                CPA(oacc.t[:, j0 * 128:(j0 + jn) * 128], po.t[:, 0:jn * 128], [po], [oacc])
                    if samp:
                        cs = slice(scol, scol + 128)
                        vj = Vg.t[:, nt, hh * 128:(hh + 1) * 128]
                        DMA(S0f.t[:], s0[:, h].rearrange("s k v -> k s v"), [], [S0f])
                        CPA(S0b.ap[:, :], S0f.t[:].rearrange("p a b -> p (a b)"), [S0f], [S0b])
                        psc = pss_.nxt()
                        MM(psc.t[:, 0:128], kT.ap[:, cs], qT.ap[:, cs], True, True, [kT, qT], [psc])
                        A2 = A_["A2"]
                        TT(A2.ap[:, :], psc.t[:, 0:128], smask.t[:], ALU.mult, [psc, smask], [A2])
                        po = pss_.nxt()
                        for s in range(4):
                            MM(po.t[:, 32 * s:32 * s + 32], vj, A2.ap[:, 32 * s:32 * s + 32], True, False, [Vg, A2], [po])
                            MM(po.t[:, 32 * s:32 * s + 32], S0b.ap[:, 128 * s:128 * s + 128], qT.ap[:, scol + 32 * s:scol + 32 * s + 32],
                               False, True, [S0b, qT], [po])
                        CPA(oacc.t[:, cs], po.t[:, 0:128], [po], [oacc])
                        pt = ptr.nxt()
                        TR(pt.t[:, 0:128], kT.ap[:, cs], identb.t[:], [kT, identb], [pt])
                        for s in range(4):
                            Km = Kmr.nxt()
                            TS(Km.ap[:, :], pt.t[:, 0:128], seqm.t[:, s:s + 1], ALU.mult, [pt, seqm], [Km])
                            px = pss_.nxt()
                            MM(px.t[:, 0:128], Km.ap[:, :], vj, True, True, [Km, Vg], [px])
                            t1 = tf.nxt()
                            TS(t1.t[:, :128], S0f.t[:, s, :], Est.t[:, s:s + 1], ALU.mult, [S0f, Est], [t1])
                            so = tf.nxt()
                            STT(so.t[:, :128], px.t[:, 0:128], Est.t[:, s:s + 1], t1.t[:, :128], ALU.mult, ALU.add, [px, Est, t1], [so])
                            DMA(o_Ss[s, h], so.t[:, :128], [so], [])
                    if full:
                        for (c0, n, _) in chunks(0, Tn_):
                            pg = proj_T(WgT, Wgv_, hh, c0, n)
                            ACT(gsT.ap[:, c0:c0 + n], pg.t[:, :n], AF.Silu, [pg], [gsT])
                            osq = tb.nxt()
                            ACT(osq.t[:, :n], oacc.t[:, c0:c0 + n], AF.Square, [oacc], [osq])
                            pq_ = pm.nxt()
                            MM(pq_.t[:, :n], onesb.t[:], osq.t[:, :n], True, True, [onesb, osq], [pq_])
                            rr = rstd_from(pq_, n, 1.0 / 128)
                            on = tf.nxt()
                            STT(on.t[:, :n], oacc.t[:, c0:c0 + n], pvec.t[:, 0:1], rr.t[:, :n], ALU.mult, ALU.mult, [oacc, pvec, rr], [on])
                            TT(g4.t[:, hh, c0:c0 + n], on.t[:, :n], gsT.ap[:, c0:c0 + n], ALU.mult, [on, gsT], [g4])
                if full:
                    Wo, Wov = load_rows(w_oa[256 * g:256 * (g + 1), :])
                    wo_partial(Wo, Wov, 0, Tn_)
            if blk["last"]:
                DMA(o_Sp.rearrange("h k v -> k h v"), S.t[:], [S], [])

        def stage_ffn(l, blk, a):
            nt = len(blk["tiles"])
            samp = blk["sample"]
            scol = nt * 128 if samp else None
            Tn_ = (nt + (1 if samp else 0)) * 128
            norm(1 + 2 * l, a, Tn_)
            chs = chunks(a, Tn_, scol)
            first = blk["first"]
            for sl in range(22):
                Wu, Wuv = load_unit(f_in[l][:, 256 * sl:256 * (sl + 1)])
                Wg, Wgv = load_unit(f_in[l][:, DFF + 256 * sl:DFF + 256 * (sl + 1)])
                for fu in range(2):
                    U = 2 * sl + fu
                    prev = None
                    for (c0, n, is_s) in chs:
                        pu = proj_T(Wu, Wuv, fu, c0, n)
                        pg = proj_T(Wg, Wgv, fu, c0, n)
                        u_ = us.nxt()
                        w0 = cw.t[:, l, 0, U:U + 1]
                        w1 = cw.t[:, l, 1, U:U + 1]
                        w2 = cw.t[:, l, 2, U:U + 1]
                        cc = tf.nxt()
                        ACT(cc.t[:, :n], pu.t[:, :n], AF.Identity, [pu, cw, cb], [cc], scale=w2, bias=cb.t[:, l, U:U + 1])
                        c1 = tf.nxt()
                        c2 = tf.nxt()
                        if is_s:
                            u4 = u_.t[:, 0:136].rearrange("p (s c) -> p s c", c=34)
                            CPV(u4[:, :, 0:2], cstT.t[:, U, 8 * l:8 * l + 8].rearrange("p (s c) -> p s c", c=2), [cstT], [u_])
                            CPA(u4[:, :, 2:34], pu.t[:, 0:128].rearrange("p (s c) -> p s c", c=32), [pu], [u_])
                            v3 = lambda t_: t_.t[:, 0:128].rearrange("p (s c) -> p s c", c=32)
                            STT(v3(c1), u4[:, :, 1:33], w1, v3(cc), ALU.mult, ALU.add, [u_, cw, cc], [c1])
                            STT(v3(c2), u4[:, :, 0:32], w0, v3(c1), ALU.mult, ALU.add, [u_, cw, c1], [c2])
                            CPV(ucol.t[:, U, 2:10].rearrange("p (s c) -> p s c", c=2), u4[:, :, 32:34], [u_], [ucol])
                        else:
                            if prev is not None:
                                CPV(u_.t[:, 0:2], prev[0].t[:, prev[1]:prev[1] + 2], [prev[0]], [u_])
                            elif first:
                                MEMSET(u_.t[:, 0:2], 0.0, [u_])
                            else:
                                CPV(u_.t[:, 0:2], uH.t[:, l, U, :], [uH], [u_])
                            CPA(u_.t[:, 2:2 + n], pu.t[:, :n], [pu], [u_])
                            STT(c1.t[:, :n], u_.t[:, 1:1 + n], w1, cc.t[:, :n], ALU.mult, ALU.add, [u_, cw, cc], [c1])
                            STT(c2.t[:, :n], u_.t[:, 0:n], w0, c1.t[:, :n], ALU.mult, ALU.add, [u_, cw, c1], [c2])
                            prev = (u_, n)
                            last_prompt = (c0 + n == nt * 128)
                            if last_prompt:
                                if blk["last"]:
                                    CPV(ucol.t[:, U, 0:2], u_.t[:, n:n + 2], [u_], [ucol])
                                else:
                                    CPV(uH.t[:, l, U, :], u_.t[:, n:n + 2], [u_], [uH])
                        sl_ = cc
                        ACT(sl_.t[:, :n], c2.t[:, :n], AF.Silu, [c2], [sl_])
                        TT(g4.t[:, fu, c0:c0 + n], sl_.t[:, :n], pg.t[:, :n], ALU.mult, [sl_, pg], [g4])
                Wd, Wdv = load_rows(f_dn[l][256 * sl:256 * (sl + 1), :])
                wo_partial(Wd, Wdv, a, Tn_)
            if blk["last"]:
                for q4 in range(11):
                    pp = pm.nxt()
                    for i in range(4):
                        TR(pp.t[0:10, i * 128:(i + 1) * 128], ucol.t[:, q4 * 4 + i, :], identf.t[:], [ucol, identf], [pp])
                    st = tf.nxt()
                    CPA(st.t[0:10, :], pp.t[0:10, :], [pp], [st])
                    DMA(o_co[l][:, q4 * 512:(q4 + 1) * 512], st.t[0:10, :], [st], [])

        import os
        KSUB = int(os.environ.get("KSUB", "99"))
        KSUB2 = int(os.environ.get("KSUB2", "99"))

        def stage_attn(blk):
            nt = len(blk["tiles"])
            samp = blk["sample"]
            scol = nt * 128 if samp else None
            ntt = nt + (1 if samp else 0)
            Tn_ = ntt * 128
            nh = blk["nh"]
            qj0 = blk["qj0"]
            qa = qj0 * 128
            nkt = nh + nt
            gt0 = blk["tiles"][0]
            KTg, QTg, kcb, vcb, KcT = B_["KTg"], B_["QTg"], B_["kc"], B_["vc"], B_["KcT"]
            KT3 = KTg.ap[:, :].rearrange("p (h c) -> p h c", h=2)
            QT3 = QTg.ap[:, :].rearrange("p (h c) -> p h c", h=2)
            E0r = Ring([B_["E0a"], B_["E0b"]])
            E4r = Ring([B_["E4a"], B_["E4b"]])
            E3r = Ring([B_["E3a"], B_["E3b"]])
            E1r = Ring([B_["E1a"], B_["E1b"]])
            E2r = Ring([B_["E2a"], B_["E2b"]])
            Ecr = Ring([B_["Eca"], B_["Ecb"]])
            for e_ in ("E0a", "E0b", "E4a", "E4b"):
                MEMSET(B_[e_].ap[:, :], 0.0, [B_[e_]])

            def normalize(p, n, col):
                sq = tb.nxt()
                ACT(sq.t[:, :n], p.t[:, :n], AF.Square, [p], [sq])
                p2 = pm.nxt()
                MM(p2.t[:, :n], onesb.t[:], sq.t[:, :n], True, True, [onesb, sq], [p2])
                rr = rstd_from(p2, n, 1.0 / 128)
                kn = tf.nxt()
                STT(kn.t[:, :n], p.t[:, :n], pvec.t[:, col:col + 1], rr.t[:, :n], ALU.mult, ALU.mult, [p, pvec, rr], [kn])
                return kn

            def out_tile_row(gt):
                if gt == 32:
                    return ("s", 0)
                if gt >= 28:
                    return ("p", (gt - 28) * 128)
                return None

            tiles = blk["tiles"] + ([32] if samp else [])
            def attn_inner(g):
                for hh in range(2):
                    h = 2 * g + hh
                    hs = slice(hh * 128, (hh + 1) * 128)
                    for j in range(qj0, nt):
                        qc = QT3[:, hh, j * 128:(j + 1) * 128]
                        ks0 = nh + j - 4
                        psA = pm.nxt()
                        psB = pss_.nxt()
                        for kt in range(5):
                            dst = psA.t[:, kt * 128:(kt + 1) * 128] if kt < 4 else psB.t[:, 0:128]
                            MM(dst, KT3[:, hh, (ks0 + kt) * 128:(ks0 + kt + 1) * 128], qc, True, True, [KTg, QTg], [psA if kt < 4 else psB])
                        if KSUB2 <= 1:
                            continue
                        kti = [gt0 - nh + ks0 + kt - 19 for kt in range(5)]
                        E0 = E0r.nxt()
                        ACT(E0.ap[:, 0:64], psA.t[:, 0:64], AF.Exp, [psA, cbias], [E0], scale=SC, bias=cbias.t[:, kti[0], h:h + 1])
                        ACT(E0.ap[64:128, 64:128], psA.t[64:128, 64:128], AF.Exp, [psA, cbias], [E0], scale=SC, bias=cbias.t[64:128, kti[0], h:h + 1])
                        E1 = E1r.nxt()
                        ACT(E1.ap[:, :], psA.t[:, 128:256], AF.Exp, [psA, cbias], [E1], scale=SC, bias=cbias.t[:, kti[1], h:h + 1])
                        E2 = E2r.nxt()
                        ACT(E2.ap[:, :], psA.t[:, 256:384], AF.Exp, [psA, cbias], [E2], scale=SC, bias=cbias.t[:, kti[2], h:h + 1])
                        t3 = tf.nxt()
                        ACT(t3.t[:, :128], psA.t[:, 384:512], AF.Exp, [psA, kneg], [t3], scale=SC, bias=kneg.t[:, kti[3]:kti[3] + 1])
                        E3 = E3r.nxt()
                        TT(E3.ap[:, :], t3.t[:, :128], eT3g.t[:, hh, :], ALU.mult, [t3, eT3g], [E3])
                        t4 = tf.nxt()
                        ACT(t4.t[0:64, :128], psB.t[0:64, 0:128], AF.Exp, [psB, kneg], [t4], scale=SC, bias=kneg.t[0:64, kti[4]:kti[4] + 1])
                        ACT(t4.t[64:128, 64:128], psB.t[64:128, 64:128], AF.Exp, [psB, kneg], [t4], scale=SC, bias=kneg.t[64:128, kti[4]:kti[4] + 1])
                        E4 = E4r.nxt()
                        TT(E4.ap[0:64, :], t4.t[0:64, :128], eT4g.t[0:64, hh, :], ALU.mult, [t4, eT4g], [E4])
                        TT(E4.ap[64:128, 64:128], t4.t[64:128, 64:128], eT4g.t[64:128, hh, 64:128], ALU.mult, [t4, eT4g], [E4])
                        Es = [E0, E1, E2, E3, E4]
                        if KSUB2 <= 2:
                            continue
                        pden = pss_.nxt()
                        KSUB3 = int(os.environ.get("KSUB3", "3"))
                        for kt in range(5):
                            if KSUB3 & 1:
                                MM(pden.t[:, 0:128], onesb.t[:], Es[kt].ap[:, :], kt == 0, kt == 4, [onesb, Es[kt]], [pden])
                            if KSUB3 & 16:
                                MM(pden.t[:, 0:128], onesb.t[:], g4.t[:, 0, kt * 128:(kt + 1) * 128], kt == 0, kt == 4, [onesb, g4], [pden])
                            if KSUB3 & 32:
                                MM(pden.t[:, 0:128], onesb.t[:], Es[kt].ap[:, :], kt == 0, kt == 4, [onesb], [pden])
                            if KSUB3 & 4:
                                MM(pden.t[:, 0:128], onesb.t[:], Es[1].ap[:, :], kt == 0, kt == 4, [onesb, Es[1]], [pden])
                            if KSUB3 & 8:
                                MM(pden.t[:, 0:128], onesb.t[:], Es[kt].ap[:, :], True, True, [onesb, Es[kt]], [pden])
                        po = pss_.nxt()
                        for kt in range(5):
                            if KSUB3 & 2:
                                MM(po.t[:, 0:128], Vg.t[:, ks0 + kt, hs], Es[kt].ap[:, :], kt == 0, kt == 4, [Vg, Es[kt]], [po])
                        if KSUB2 <= 3:
                            continue
                        rd = tf.nxt()
                        TS(rd.t[:, :128], pden.t[:, 0:128], 1e-30, ALU.add, [pden], [rd])
                        rr = tf.nxt()
                        RECIP(rr.t[:, :128], rd.t[:, :128], [rd], [rr])
                        TT(g4.t[:, hh, j * 128:(j + 1) * 128], po.t[:, 0:128], rr.t[:, :128], ALU.mult, [po, rr], [g4])
                if samp:
                    kc3 = kcb.ap[:, :].rearrange("p (t c) -> p t c", c=256)
                    vc3 = vcb.ap[:, :].rearrange("p (t c) -> p t c", c=256)
                    for s in range(4):
                        DMA(kc3, ck[s][:, 256 * g:256 * (g + 1)].rearrange("(t p) c -> p t c", p=128), [], [kcb], q="pool")
                        DMA(vc3, cv[s][:, 256 * g:256 * (g + 1)].rearrange("(t p) c -> p t c", p=128), [], [vcb], q="pool")
                        pt = ptr.nxt()
                        for hh in range(2):
                            for kt in range(4):
                                i = hh * 4 + kt
                                TR(pt.t[:, i * 128:(i + 1) * 128], kc3[:, kt, hh * 128:(hh + 1) * 128], identb.t[:], [kcb, identb], [pt])
                        CPA(KcT.ap[:, :], pt.t[:, :], [pt], [KcT])
                        for hh in range(2):
                            h = 2 * g + hh
                            hs = slice(hh * 128, (hh + 1) * 128)
                            qc = QT3[:, hh, scol + 32 * s:scol + 32 * s + 32]
                            psc = pss_.nxt()
                            for kt in range(4):
                                i = hh * 4 + kt
                                MM(psc.t[:, kt * 32:(kt + 1) * 32], KcT.ap[:, i * 128:(i + 1) * 128], qc, True, True, [KcT, QTg], [psc])
                            MM(psc.t[:, 128:160], KT3[:, hh, nkt * 128:(nkt + 1) * 128], qc, True, True, [KTg, QTg], [psc])
                            Ec = Ecr.nxt()
                            ACT(Ec.ap[:, 0:96], psc.t[:, 0:96], AF.Exp, [psc, c0t], [Ec], scale=SC, bias=c0t.t[:, h:h + 1])
                            t3 = tf.nxt()
                            ACT(t3.t[:, :32], psc.t[:, 96:128], AF.Exp, [psc], [t3], scale=SC)
                            TT(Ec.ap[:, 96:128], t3.t[:, :32], eT3g.t[:, hh, 0:32], ALU.mult, [t3, eT3g], [Ec])
                            tn = tf.nxt()
                            ACT(tn.t[:, :32], psc.t[:, 128:160], AF.Exp, [psc], [tn], scale=SC)
                            STT(Ec.ap[:, 128:160], tn.t[:, :32], seqm.t[:, s:s + 1], eTng.t[:, hh, :], ALU.mult, ALU.mult, [tn, seqm, eTng], [Ec])
                            pden = pss_.nxt()
                            po = pss_.nxt()
                            for kt in range(5):
                                ec = Ec.ap[:, 128:160] if kt == 4 else Ec.ap[:, kt * 32:(kt + 1) * 32]
                                MM(pden.t[:, 0:32], onesb.t[:], ec, kt == 0, kt == 4, [onesb, Ec], [pden])
                            for kt in range(5):
                                ec = Ec.ap[:, 128:160] if kt == 4 else Ec.ap[:, kt * 32:(kt + 1) * 32]
                                lv = Vg.t[:, nkt, hs] if kt == 4 else vc3[:, kt, hs]
                                MM(po.t[:, 0:32], lv, ec, kt == 0, kt == 4, [Vg, vcb, Ec], [po])
                            rd = tf.nxt()
                            TS(rd.t[:, :32], pden.t[:, 0:32], 1e-30, ALU.add, [pden], [rd])
                            rr = tf.nxt()
                            RECIP(rr.t[:, :32], rd.t[:, :32], [rd], [rr])
                            TT(g4.t[:, hh, scol + 32 * s:scol + 32 * s + 32], po.t[:, 0:32], rr.t[:, :32], ALU.mult, [po, rr], [g4])
                if KSUB <= 3:
                    return
                Wo, Wov = load_rows(w_ob[256 * g:256 * (g + 1), :])
                wo_partial(Wo, Wov, qa, Tn_)


            KONLY = int(os.environ.get("KONLY", "0"))
            for g in range(8):
                if KONLY:
                    break
                Wv, Wvv = load_unit(w_qkv[:, 4096 + 256 * g:4096 + 256 * (g + 1)])
                Wk, Wkv = load_unit(w_qkv[:, 2048 + 256 * g:2048 + 256 * (g + 1)])
                DMA(T3g.t[:], d_T3[:, 2 * g:2 * g + 2, :], [], [T3g])
                DMA(T4g.t[:], d_T4[:, 2 * g:2 * g + 2, :], [], [T4g])
                DMA(Tng.t[:], d_Tn[:, 2 * g:2 * g + 2, :], [], [Tng])
                ACT(eT3g.t[:], T3g.t[:], AF.Exp, [T3g], [eT3g])
                ACT(eT4g.t[:], T4g.t[:], AF.Exp, [T4g], [eT4g])
                ACT(eTng.t[:], Tng.t[:], AF.Exp, [Tng], [eTng])
                if nh:
                    DMA(Vg.t[:, 0:4, :], vscr[:, 256 * g:256 * (g + 1)].rearrange("(t p) c -> p t c", p=128), [Bvscr], [Vg])
                    for hh in range(2):
                        DMA(KT3[:, hh, 0:512], kscr[2 * g + hh], [Bkscr], [KTg])
                for j, gt in enumerate(tiles):
                    pv = pm.nxt()
                    for kc in range(16):
                        MM(pv.t[:, :256], hT.t[:, kc, j * 128:(j + 1) * 128], Wvv[:, kc, :], kc == 0, kc == 15, [hT, Wv], [pv])
                    CPA(Vg.t[:, nh + j, :], pv.t[:, :256], [pv], [Vg])
                    orow = out_tile_row(gt)
                    if orow is not None:
                        st = tf.nxt()
                        CPA(st.t[:, :256], pv.t[:, :256], [pv], [st])
                        dst = o_vs if orow[0] == "s" else o_vp[orow[1]:orow[1] + 128, :]
                        DMA(dst[:, 256 * g:256 * (g + 1)], st.t[:, :256], [st], [])
                Wq, Wqv = load_unit(w_qkv[:, 256 * g:256 * (g + 1)])
                for hh in range(2):
                    h = 2 * g + hh
                    for (c0, n, _) in chunks(0, Tn_):
                        pk = proj_T(Wk, Wkv, hh, c0, n)
                        kn = normalize(pk, n, 2)
                        CPA(KT3[:, hh, nh * 128 + c0:nh * 128 + c0 + n], kn.t[:, :n], [kn], [KTg])
                        for jj in range(n // 128):
                            gt = tiles[c0 // 128 + jj]
                            orow = out_tile_row(gt)
                            if orow is None:
                                continue
                            pso = pss_.nxt()
                            TR(pso.t[:, 0:128], kn.t[:, jj * 128:(jj + 1) * 128], identf.t[:], [kn, identf], [pso])
                            st = tf.nxt()
                            CPV(st.t[:, :128], pso.t[:, 0:128], [pso], [st])
                            dst = o_ks if orow[0] == "s" else o_kp[orow[1]:orow[1] + 128, :]
                            DMA(dst[:, h * 128:(h + 1) * 128], st.t[:, :128], [st], [])
                    for (c0, n, _) in chunks(qa, Tn_):
                        pq = proj_T(Wq, Wqv, hh, c0, n)
                        qn = normalize(pq, n, 1)
                        CPA(QT3[:, hh, c0:c0 + n], qn.t[:, :n], [qn], [QTg])
                if KSUB <= 1:
                    continue
                if nh == 0:
                    DMA(vscr[:, 256 * g:256 * (g + 1)].rearrange("(t p) c -> p t c", p=128), Vg.t[:, nt - 4:nt, :], [Vg], [Bvscr])
                    for hh in range(2):
                        DMA(kscr[2 * g + hh], KT3[:, hh, (nt - 4) * 128:nt * 128], [KTg], [Bkscr])
                if KSUB <= 2:
                    continue
                attn_inner(g)
            if KONLY:
                for g in range(KONLY):
                    attn_inner(g)

        blocks = [
            dict(tiles=list(range(0, 7)), full=False, sample=False, last=False, first=True),
            dict(tiles=list(range(7, 14)), full=False, sample=False, last=False, first=False),
            dict(tiles=list(range(14, 19)), full=False, sample=False, last=False, first=False),
            dict(tiles=list(range(19, 26)), full=True, sample=False, last=False, first=True, nh=0, qj0=4),
            dict(tiles=list(range(26, 32)), full=True, sample=True, last=True, first=False, nh=4, qj0=0),
        ]
        import os
        STOP = int(os.environ.get("KSTOP", "99"))
        for bi, blk in enumerate(blocks):
            nt = len(blk["tiles"])
            Tn_ = (nt + (1 if blk["sample"] else 0)) * 128
            if STOP <= 1:
                break
            KONLY = int(os.environ.get("KONLY", "0"))
            if KONLY:
                if bi != 3:
                    continue
                stage_attn(blk)
                break
            P.barrier()
            load_x(blk)
            norm(0, 0, Tn_)
            if STOP <= 2:
                break
            stage_hgrn(blk)
            if STOP <= 3:
                break
            if STOP <= 4 and bi == 2:
                break
            if blk["full"]:
                qa = blk["qj0"] * 128
                if STOP <= 5:
                    break
                stage_ffn(0, blk, 0)
                if STOP <= 6:
                    break
                norm(2, 0, Tn_)
                P.barrier()
                stage_attn(blk)
                if STOP <= 7:
                    break
                stage_ffn(1, blk, qa)
                store_x(blk)
                if STOP <= 8:
                    break
        P.finish()
        with nc.Block() as block:
            P.replay(block)
    return nc


_CACHE = {}


def _host_consts():
    ident = np.eye(128, dtype=np.float32)
    s = np.arange(128)[:, None]
    t = np.arange(128)[None, :]
    cmask = (s <= t).astype(np.float32)
    smask = ((s <= t) & (s // 32 == t // 32)).astype(np.float32)
    scanp = np.ones((128, 512), np.float32)
    scanp[:, ::64] = 0.0
    scans = np.ones((128, 128), np.float32)
    scans[:, ::32] = 0.0
    seqm = (np.arange(128)[:, None] // 32 == np.arange(4)[None, :]).astype(np.float32)
    return dict(ident=ident, cmask=cmask, smask=smask, scanp=scanp, scans=scans, seqm=seqm)


def kernel(x_prompt, x_sample, state_a_S, cache_b_k, cache_b_v, state_ffn_conv,
           norm_mix, norm_ffn, a_w_in, a_gamma_lb, a_norm_o, a_w_o,
           b_w_qkv, b_q_norm, b_k_norm, b_rel_bias, b_w_o,
           f_w_in, f_conv_w, f_conv_b, f_w_down):
    f32 = lambda a: np.ascontiguousarray(np.asarray(a, dtype=np.float32))
    x_prompt, x_sample = f32(x_prompt), f32(x_sample)
    if "nc" not in _CACHE:
        _CACHE["nc"] = build_program()
    nc = _CACHE["nc"]
    cst_ = _host_consts()
    gains = np.stack([f32(norm_mix)[0], f32(norm_ffn)[0], f32(norm_mix)[1], f32(norm_ffn)[1]], 0)
    gains = np.ascontiguousarray(gains.reshape(4, 16, 128).transpose(2, 0, 1))
    gam = np.ascontiguousarray(f32(a_gamma_lb).reshape(3, 16, 128).transpose(2, 0, 1))
    pvec = np.ascontiguousarray(np.stack([f32(a_norm_o)[0], f32(b_q_norm)[0], f32(b_k_norm)[0]], 1))
    rb = f32(b_rel_bias)[0]
    c0t = np.ascontiguousarray(np.broadcast_to(rb[0][None, :], (128, 16)))
    kk = np.arange(128)[:, None]
    qq = np.arange(128)[None, :]
    T3 = np.ascontiguousarray(rb[np.maximum(kk - qq, 0)].transpose(0, 2, 1))
    T4 = np.ascontiguousarray(rb[np.minimum(kk - qq, 63) + 128].transpose(0, 2, 1))
    ii = np.arange(32)[None, :]
    Tn = np.ascontiguousarray(rb[(kk % 32) - ii + 128].transpose(0, 2, 1))
    cw = np.ascontiguousarray(f32(f_conv_w).reshape(2, 3, NFU, 128).transpose(3, 0, 1, 2))
    cb = np.ascontiguousarray(f32(f_conv_b).reshape(2, NFU, 128).transpose(2, 0, 1))
    shared = dict(w_in=f32(a_w_in)[0], w_oa=f32(a_w_o)[0], w_qkv=f32(b_w_qkv)[0], w_ob=f32(b_w_o)[0],
                  f_in=f32(f_w_in), f_dn=f32(f_w_down), gains=gains, gam=gam, pvec=pvec, c0t=c0t,
                  T3=T3, T4=T4, Tn=Tn, cw=cw, cb=cb, **cst_)
    sS = f32(state_a_S)[0]
    cK = f32(cache_b_k)[0].reshape(32, 512, D)
    cV = f32(cache_b_v)[0].reshape(32, 512, D)
    cst = f32(state_ffn_conv)
    in_maps = []
    for c in range(8):
        b, q = c // 4, c % 4
        base = 1024 * (q - 3)
        xs = np.zeros((33 * 128, D), np.float32)
        lo = max(base, 0)
        xs[lo - base:4096] = x_prompt[b, lo:base + 4096]
        xs[4096:] = x_sample[4 * c:4 * c + 4].reshape(128, D)
        kneg = np.zeros((128, 13), np.float32)
        for t in range(13):
            if base + (19 + t) * 128 < 0:
                kneg[:, t] = NEG
        m = dict(shared)
        m.update(xs=xs, s0=np.ascontiguousarray(sS[4 * c:4 * c + 4]), ck=np.ascontiguousarray(cK[4 * c:4 * c + 4]),
                 cv=np.ascontiguousarray(cV[4 * c:4 * c + 4]),
                 cst=np.ascontiguousarray(cst[:, 4 * c:4 * c + 4].reshape(16, DFF)), kneg=kneg)
        in_maps.append(m)
    import os
    ncores = int(os.environ.get("KCORES", "8"))
    res = run_bass_kernel_spmd(nc, in_maps[:ncores], core_ids=list(range(ncores)))
    R = list(res.results) + [res.results[0]] * (8 - ncores)
    y_p = np.zeros((2, 4096, D), np.float32)
    y_s = np.zeros((32, 32, D), np.float32)
    Sp = np.zeros((1, 2, 16, 128, 128), np.float32)
    Ss = np.zeros((1, 32, 16, 128, 128), np.float32)
    kp = np.zeros((1, 2, 512, 16, 128), np.float32)
    vp = np.zeros((1, 2, 512, 16, 128), np.float32)
    ks = np.zeros((1, 32, 32, 16, 128), np.float32)
    vs = np.zeros((1, 32, 32, 16, 128), np.float32)
    cp = np.zeros((2, 2, 2, DFF), np.float32)
    cs = np.zeros((2, 32, 2, DFF), np.float32)
    for c in range(8):
        b, q = c // 4, c % 4
        r = R[c]
        y_p[b, 1024 * q:1024 * (q + 1)] = r["yp"]
        y_s[4 * c:4 * c + 4] = r["ys"].reshape(4, 32, D)
        Ss[0, 4 * c:4 * c + 4] = r["Ss"]
        ks[0, 4 * c:4 * c + 4] = r["ks"].reshape(4, 32, 16, 128)
        vs[0, 4 * c:4 * c + 4] = r["vs"].reshape(4, 32, 16, 128)
        cs[:, 4 * c:4 * c + 4] = r["co"][:, 2:10].reshape(2, 4, 2, DFF)
        if q == 3:
            Sp[0, b] = r["Sp"]
            kp[0, b] = r["kp"].reshape(512, 16, 128)
            vp[0, b] = r["vp"].reshape(512, 16, 128)
            cp[:, b] = r["co"][:, 0:2]
    return (y_p, y_s, Sp, Ss, kp, vp, ks, vs, cp, cs)
```

```python
import numpy as np
from contextlib import ExitStack
import concourse.bass as bass
import concourse.mybir as mybir
from concourse.bass_utils import run_bass_kernel_spmd

F32 = mybir.dt.float32
BF16 = mybir.dt.bfloat16
AF = mybir.ActivationFunctionType
ALU = mybir.AluOpType

D = 2048
DFF = 5632
NFU = 44
EPS = 1e-6
NEG = -30000.0
SC = 128.0 ** -0.5


class Buf:
    __slots__ = ("name", "w", "r")

    def __init__(self, name="b"):
        self.name = name
        self.w = None
        self.r = {}


class Prog:
    ENGS = ("pe", "act", "dve", "pool", "sp")

    def __init__(self, nc, n_dma_sp=8, n_dma_pool=6):
        self.nc = nc
        self.ops = {e: [] for e in self.ENGS}
        self.known = {e: {} for e in self.ENGS}
        self.cnt = {e: 0 for e in self.ENGS}
        self.sems = {}
        self.dma_pool = {"sp": [f"dsp{i}" for i in range(n_dma_sp)],
                         "pool": [f"dpl{i}" for i in range(n_dma_pool)]}
        self.dma_tgt = {}
        self.dma_idx = {"sp": 0, "pool": 0}

    def semkeys(self):
        ks = [f"c_{e}" for e in ("pe", "act", "dve", "pool")]
        for q in self.dma_pool.values():
            ks += q
        return ks

    def _collect(self, eng, reads, writes):
        waits = {}

        def add(ev):
            if ev is None:
                return
            k, v = ev
            if waits.get(k, 0) < v:
                waits[k] = v
        for b in reads:
            add(b.w)
        for b in writes:
            add(b.w)
            for k, v in b.r.items():
                add((k, v))
        out = []
        kn = self.known[eng]
        for k, v in waits.items():
            if eng == "pe" and k == "c_pe":
                continue
            if kn.get(k, 0) >= v:
                continue
            kn[k] = v
            out.append((k, v))
        return out

    def _commit(self, ev, reads, writes):
        k, v = ev
        for b in reads:
            if b.r.get(k, 0) < v:
                b.r[k] = v
        for b in writes:
            b.w = ev
            b.r = {}

    def op(self, eng, fn, reads=(), writes=()):
        waits = self._collect(eng, reads, writes)
        self.cnt[eng] += 1
        ev = (f"c_{eng}", self.cnt[eng])
        self.ops[eng].append((waits, fn, (ev[0], 1)))
        self._commit(ev, reads, writes)
        return ev

    def dma(self, q, out, in_, reads=(), writes=(), **kw):
        pool = self.dma_pool[q]
        i = self.dma_idx[q]
        self.dma_idx[q] = i + 1
        sk = pool[i % len(pool)]
        prev = self.dma_tgt.get(sk, 0)
        waits = self._collect(q, reads, writes)
        if prev > 0 and self.known[q].get(sk, 0) < prev:
            self.known[q][sk] = prev
            waits.append((sk, prev))
        tgt = prev + 16
        self.dma_tgt[sk] = tgt
        ev = (sk, tgt)
        self.ops[q].append((waits, (lambda e, o=out, i_=in_, k=kw: e.dma_start(out=o, in_=i_, **k)), (sk, 16)))
        self._commit(ev, reads, writes)
        return ev

    def _all_events(self):
        waits = [(sk, t) for sk, t in self.dma_tgt.items() if t > 0]
        for e in ("pe", "act", "dve", "pool"):
            if self.cnt[e] > 0:
                waits.append((f"c_{e}", self.cnt[e]))
        return waits

    def barrier(self):
        allw = self._all_events()
        for e in self.ENGS:
            w = []
            for k, v in allw:
                if k == f"c_{e}" and e == "pe":
                    continue
                if self.known[e].get(k, 0) < v:
                    self.known[e][k] = v
                    w.append((k, v))
            if w:
                self.ops[e].append((w, None, None))

    def finish(self):
        self.ops["sp"].append((self._all_events(), None, None))

    def replay(self, block):
        sems = self.sems

        def run(engname):
            def f(e):
                for waits, fn, inc in self.ops[engname]:
                    for k, v in waits:
                        e.wait_ge(sems[k], v)
                    if fn is not None:
                        ins = fn(e)
                        if inc is not None:
                            ins.then_inc(sems[inc[0]], inc[1])
            return f
        block.tensor(run("pe"))
        block.scalar(run("act"))
        block.vector(run("dve"))
        block.gpsimd(run("pool"))
        block.sync(run("sp"))


class T:
    def __init__(self, t, name):
        self.t = t
        self.b = Buf(name)

    def __getitem__(self, idx):
        return self.t[idx]


class Ring:
    def __init__(self, items):
        self.items = items
        self.i = 0

    def nxt(self):
        x = self.items[self.i % len(self.items)]
        self.i += 1
        return x


def chunks(a, b, scol=None):
    out = []
    end = b if scol is None else min(b, scol)
    c = a
    while c < end:
        n = min(512, end - c)
        out.append((c, n, False))
        c += n
    if scol is not None and b > scol:
        out.append((scol, 128, True))
    return out


def build_program():
    nc = bass.Bass("TRN2", target_bir_lowering=False)

    def din(name, shape):
        return nc.dram_tensor(name, shape, F32, kind="ExternalInput").ap()

    def dout(name, shape):
        return nc.dram_tensor(name, shape, F32, kind="ExternalOutput").ap()

    xs = din("xs", [33 * 128, D])
    s0 = din("s0", [4, 16, 128, 128])
    ck = din("ck", [4, 512, D])
    cv = din("cv", [4, 512, D])
    cst = din("cst", [16, DFF])
    w_in = din("w_in", [D, 8192])
    w_oa = din("w_oa", [D, D])
    w_qkv = din("w_qkv", [D, 6144])
    w_ob = din("w_ob", [D, D])
    f_in = din("f_in", [2, D, 2 * DFF])
    f_dn = din("f_dn", [2, DFF, D])
    d_gains = din("gains", [128, 4, 16])
    d_gam = din("gam", [128, 3, 16])
    d_pvec = din("pvec", [128, 3])
    d_c0 = din("c0t", [128, 16])
    d_T3 = din("T3", [128, 16, 128])
    d_T4 = din("T4", [128, 16, 128])
    d_Tn = din("Tn", [128, 16, 32])
    d_cw = din("cw", [128, 2, 3, NFU])
    d_cb = din("cb", [128, 2, NFU])
    d_ident = din("ident", [128, 128])
    d_cmask = din("cmask", [128, 128])
    d_smask = din("smask", [128, 128])
    d_scanp = din("scanp", [128, 512])
    d_scans = din("scans", [128, 128])
    d_kneg = din("kneg", [128, 13])
    d_seqm = din("seqm", [128, 4])

    o_yp = dout("yp", [1024, D])
    o_ys = dout("ys", [128, D])
    o_Sp = dout("Sp", [16, 128, 128])
    o_Ss = dout("Ss", [4, 16, 128, 128])
    o_kp = dout("kp", [512, D])
    o_vp = dout("vp", [512, D])
    o_ks = dout("ks", [128, D])
    o_vs = dout("vs", [128, D])
    o_co = dout("co", [2, 10, DFF])

    kscr = nc.dram_tensor("kscr", [16, 128, 512], BF16, kind="Internal").ap()
    vscr = nc.dram_tensor("vscr", [512, D], BF16, kind="Internal").ap()
    Bkscr = Buf("kscr")
    Bvscr = Buf("vscr")

    P = Prog(nc)
    es = ExitStack()
    with es:
        for k in P.semkeys():
            P.sems[k] = es.enter_context(nc.semaphore(k))

        def sb(name, shape, dt):
            return T(es.enter_context(nc.sbuf_tensor("s_" + name, shape, dt)), name)

        def ps(name, shape, dt):
            return T(es.enter_context(nc.psum_tensor("p_" + name, shape, dt)), name)

        def bufs(lst):
            return [x.b if hasattr(x, 'b') else x for x in lst]

        def ACT(out, in_, func, R, W, **kw):
            P.op("act", lambda e: e.activation(out=out, in_=in_, func=func, **kw), bufs(R), bufs(W))

        def MM(out, lhsT, rhs, start, stop, R, W):
            P.op("pe", lambda e: e.matmul(out, lhsT, rhs, start=start, stop=stop), bufs(R), bufs(W))

        def TR(out, in_, ident, R, W):
            P.op("pe", lambda e: e.transpose(out, in_, ident), bufs(R), bufs(W))

        def TT(out, in0, in1, op, R, W):
            P.op("dve", lambda e: e.tensor_tensor(out=out, in0=in0, in1=in1, op=op), bufs(R), bufs(W))

        def TS(out, in0, s1, op0, R, W, s2=None, op1=None):
            if op1 is None:
                P.op("dve", lambda e: e.tensor_scalar(out=out, in0=in0, scalar1=s1, scalar2=None, op0=op0), bufs(R), bufs(W))
            else:
                P.op("dve", lambda e: e.tensor_scalar(out=out, in0=in0, scalar1=s1, scalar2=s2, op0=op0, op1=op1), bufs(R), bufs(W))

        def STT(out, in0, scalar, in1, op0, op1, R, W):
            P.op("dve", lambda e: e.scalar_tensor_tensor(out=out, in0=in0, scalar=scalar, in1=in1, op0=op0, op1=op1), bufs(R), bufs(W))

        def CPV(out, in_, R, W):
            P.op("dve", lambda e: e.tensor_copy(out=out, in_=in_), bufs(R), bufs(W))

        def RECIP(out, in_, R, W):
            P.op("dve", lambda e: e.reciprocal(out=out, in_=in_), bufs(R), bufs(W))

        def SCAN(out, d0, d1, R, W):
            P.op("dve", lambda e: e.tensor_tensor_scan(out=out, data0=d0, data1=d1, initial=0.0, op0=ALU.mult, op1=ALU.add), bufs(R), bufs(W))

        def MEMSET(ap, val, W):
            P.op("dve", lambda e: e.memset(ap, val), [], bufs(W))

        def CPA(out, in_, R, W):
            ACT(out, in_, AF.Copy, R, W)

        def DMA(out, in_, R, W, q="sp"):
            P.dma(q, out, in_, bufs(R), bufs(W))

        xT = sb("xT", [128, 16, 896], F32)
        hT = sb("hT", [128, 16, 896], BF16)
        wring = Ring([sb(f"wr{i}", [128, 4096], BF16) for i in range(3)])
        Vg = sb("Vg", [128, 12, 256], BF16)
        g4 = sb("g4", [128, 2, 896], BF16)
        g4b = sb("g4b", [128, 2, 896], BF16)
        oacc = sb("oacc", [128, 896], F32)
        tf = Ring([sb(f"tf{i}", [128, 512], F32) for i in range(6)])
        tb = Ring([sb(f"tb{i}", [128, 512], BF16) for i in range(4)])
        us = Ring([sb(f"us{i}", [128, 516], F32) for i in range(2)])
        S = sb("S", [128, 16, 128], F32)
        Sall = sb("Sall", [128, 8, 128], F32)
        tmpall = sb("tmpall", [128, 7, 128], F32)
        cmask4 = sb("cmask4", [128, 4, 128], F32)
        S0f = sb("S0f", [128, 4, 128], F32)
        arena = sb("arena", [128, 10240], BF16)
        identf = sb("identf", [128, 128], F32)
        identb = sb("identb", [128, 128], BF16)
        onesb = sb("onesb", [128, 128], BF16)
        cmask = sb("cmask", [128, 128], F32)
        smask = sb("smask", [128, 128], F32)
        scanp = sb("scanp", [128, 512], F32)
        scans = sb("scans", [128, 128], F32)
        gains = sb("gains", [128, 4, 16], F32)
        gam = sb("gam", [128, 3, 16], F32)
        lbt = sb("lbt", [128, 8, 16], F32)
        oml = sb("oml", [128, 16], F32)
        pvec = sb("pvec", [128, 3], F32)
        c0t = sb("c0t", [128, 16], F32)
        kneg = sb("kneg", [128, 13], F32)
        cbias = sb("cbias", [128, 13, 16], F32)
        seqm = sb("seqm", [128, 4], F32)
        cw = sb("cw", [128, 2, 3, NFU], F32)
        cb = sb("cb", [128, 2, NFU], F32)
        cstT = sb("cstT", [128, NFU, 16], F32)
        ucol = sb("ucol", [128, NFU, 10], F32)
        uH = sb("uH", [128, 2, NFU, 2], F32)
        onec = sb("onec", [128, 1], F32)
        epsc = sb("epsc", [128, 1], F32)
        Et = sb("Et", [128, 16], F32)
        E12t = sb("E12t", [128, 8], F32)
        Est = sb("Est", [128, 4], F32)
        Et2 = sb("Et2", [128, 16], F32)
        Est2 = sb("Est2", [128, 4], F32)
        T3g = sb("T3g", [128, 2, 128], F32)
        T4g = sb("T4g", [128, 2, 128], F32)
        Tng = sb("Tng", [128, 2, 32], F32)
        eT3g = sb("eT3g", [128, 2, 128], F32)
        eT4g = sb("eT4g", [128, 2, 128], F32)
        eTng = sb("eTng", [128, 2, 32], F32)

        pm = Ring([ps(f"pm{i}", [128, 512], F32) for i in range(4)])
        pss_ = Ring([ps(f"ps{i}", [128, 512], F32) for i in range(2)])
        ptr = Ring([ps(f"pt{i}", [128, 1024], BF16) for i in range(2)])

        class AV:
            def __init__(self, off, n, name):
                self.ap = arena.t[:, off:off + n]
                self.b = Buf(name)
                self.off = off

            def __getitem__(self, idx):
                return self.ap[idx]

        def carve(spec):
            off = 0
            d = {}
            for name, n in spec:
                d[name] = AV(off, n, name)
                off += n
            assert off <= 10240, off
            return d

        A_ = carve([("kT", 896), ("KpT", 896), ("qT", 896), ("Q2T", 896), ("gsT", 896), ("S0b", 512),
                    ("A0", 512), ("A1", 512), ("A2", 128), ("Km0", 128), ("Km1", 128), ("Ktall", 896), ("Sball", 896), ("kT2", 896), ("qT2", 896)])
        B_ = carve([("KTg", 3072), ("QTg", 1792), ("kc", 1024), ("vc", 1024), ("KcT", 1024),
                    ("E0a", 128), ("E0b", 128), ("E4a", 128), ("E4b", 128), ("E3a", 128), ("E3b", 128),
                    ("E1a", 128), ("E1b", 128), ("E2a", 128), ("E2b", 128), ("Eca", 160), ("Ecb", 160)])

        for t_, d_ in ((identf, d_ident), (cmask, d_cmask), (smask, d_smask), (scanp, d_scanp), (scans, d_scans),
                       (gains, d_gains), (gam, d_gam), (pvec, d_pvec), (c0t, d_c0), (kneg, d_kneg), (seqm, d_seqm),
                       (cw, d_cw), (cb, d_cb)):
            DMA(t_.t[:], d_, [], [t_])
        DMA(identb.t[:], d_ident, [], [identb], q="pool")
        MEMSET(onesb.t[:], 1.0, [onesb])
        MEMSET(onec.t[:], 1.0, [onec])
        MEMSET(epsc.t[:], EPS, [epsc])
        MEMSET(S.t[:], 0.0, [S])
        for i4 in range(4):
            DMA(cmask4.t[:, i4, :], d_cmask, [], [cmask4])
        MEMSET(uH.t[:], 0.0, [uH])
        g0, g1, g2 = gam.t[:, 0, :], gam.t[:, 1, :], gam.t[:, 2, :]
        L = lambda i: lbt.t[:, i, :]
        TT(L(0), g0, g1, ALU.max, [gam], [lbt])
        TT(L(0), L(0), g2, ALU.max, [gam, lbt], [lbt])
        for i, gi in enumerate((g0, g1, g2)):
            TT(L(1 + i), gi, L(0), ALU.subtract, [gam, lbt], [lbt])
            ACT(L(1 + i), L(1 + i), AF.Exp, [lbt], [lbt])
        TT(L(4), L(1), L(2), ALU.add, [lbt], [lbt])
        TT(L(4), L(4), L(3), ALU.add, [lbt], [lbt])
        RECIP(L(5), L(4), [lbt], [lbt])
        TT(L(6), L(1), L(5), ALU.mult, [lbt], [lbt])
        TS(oml.t[:], L(6), -1.0, ALU.mult, [lbt], [oml], s2=1.0, op1=ALU.add)
        for t in range(13):
            TS(cbias.t[:, t, :], c0t.t[:], kneg.t[:, t:t + 1], ALU.add, [c0t, kneg], [cbias])
        for pc in range(11):
            tt_ = tf.nxt()
            DMA(tt_.t[0:16, :], cst[:, pc * 512:(pc + 1) * 512], [], [tt_])
            pp = pss_.nxt()
            for i in range(4):
                TR(pp.t[:, i * 16:(i + 1) * 16], tt_.t[0:16, i * 128:(i + 1) * 128], identf.t[0:16, 0:16], [tt_, identf], [pp])
            CPV(cstT.t[:, pc * 4:(pc + 1) * 4, :], pp.t[:, 0:64].rearrange("p (a b) -> p a b", b=16), [pp], [cstT])

        def load_unit(src):
            w = wring.nxt()
            v = w.t[:, 0:4096].rearrange("p (k n) -> p k n", n=256)
            DMA(v, src.rearrange("(kc p) n -> p kc n", p=128), [], [w], q="pool")
            return w, v

        def load_rows(src):
            w = wring.nxt()
            v = w.t[:, 0:4096].rearrange("p (k n) -> p k n", n=2048)
            for kc_ in range(2):
                P.dma("pool", v[:, kc_, :], src[kc_ * 128:(kc_ + 1) * 128, :], [], [w.b], max_dma_last_dim=4096)
            return w, v

        def load_x(blk):
            tiles = blk["tiles"] + ([32] if blk["sample"] else [])
            for j, gt in enumerate(tiles):
                for q4 in range(4):
                    xt_ = tf.nxt()
                    DMA(xt_.t[:], xs[gt * 128:(gt + 1) * 128, q4 * 512:(q4 + 1) * 512], [], [xt_])
                    pp = pm.nxt()
                    for i in range(4):
                        TR(pp.t[:, i * 128:(i + 1) * 128], xt_.t[:, i * 128:(i + 1) * 128], identf.t[:], [xt_, identf], [pp])
                    CPA(xT.t[:, q4 * 4:(q4 + 1) * 4, j * 128:(j + 1) * 128], pp.t[:].rearrange("p (a b) -> p a b", b=128), [pp], [xT])

        def store_x(blk):
            tiles = blk["tiles"] + ([32] if blk["sample"] else [])
            for j, gt in enumerate(tiles):
                if gt < 24:
                    continue
                dst = o_ys if gt == 32 else o_yp[(gt - 24) * 128:(gt - 23) * 128, :]
                for q4 in range(4):
                    pp = pm.nxt()
                    for i in range(4):
                        TR(pp.t[:, i * 128:(i + 1) * 128], xT.t[:, q4 * 4 + i, j * 128:(j + 1) * 128], identf.t[:], [xT, identf], [pp])
                    st = tf.nxt()
                    CPA(st.t[:], pp.t[:], [pp], [st])
                    DMA(dst[:, q4 * 512:(q4 + 1) * 512], st.t[:], [st], [])

        def rstd_from(p, n, scale):
            rs = tf.nxt()
            ACT(rs.t[:, :n], p.t[:, :n], AF.Sqrt, [p, epsc], [rs], scale=scale, bias=epsc.t[:])
            rr = tf.nxt()
            RECIP(rr.t[:, :n], rs.t[:, :n], [rs], [rr])
            return rr

        def norm(gi, a, b):
            for (c0, n, _) in chunks(a, b):
                pssq = pm.nxt()
                for kc in range(16):
                    sq = tb.nxt()
                    ACT(sq.t[:, :n], xT.t[:, kc, c0:c0 + n], AF.Square, [xT], [sq])
                    MM(pssq.t[:, :n], onesb.t[:], sq.t[:, :n], kc == 0, kc == 15, [onesb, sq], [pssq])
                rr = rstd_from(pssq, n, 1.0 / D)
                for kc in range(16):
                    STT(hT.t[:, kc, c0:c0 + n], xT.t[:, kc, c0:c0 + n], gains.t[:, gi, kc:kc + 1], rr.t[:, :n],
                        ALU.mult, ALU.mult, [xT, gains, rr], [hT])

        def proj_T(Wt, Wv, hh, c0, n):
            p = pm.nxt()
            for kc in range(16):
                MM(p.t[:, :n], Wv[:, kc, hh * 128:(hh + 1) * 128], hT.t[:, kc, c0:c0 + n], kc == 0, kc == 15, [Wt, hT], [p])
            return p

        def wo_partial(Wt, Wv, a, b):
            for cu in range(16):
                for (c0, n, _) in chunks(a, b):
                    py = pm.nxt()
                    for hh in range(2):
                        MM(py.t[:, :n], Wv[:, hh, cu * 128:(cu + 1) * 128], g4.t[:, hh, c0:c0 + n], hh == 0, hh == 1, [Wt, g4], [py])
                    TT(xT.t[:, cu, c0:c0 + n], xT.t[:, cu, c0:c0 + n], py.t[:, :n], ALU.add, [xT, py], [xT])

        def stage_hgrn(blk):
            nt = len(blk["tiles"])
            samp = blk["sample"]
            full = blk["full"]
            scol = nt * 128 if samp else None
            ntt = nt + (1 if samp else 0)
            Tn_ = ntt * 128
            kT, KpT, qT, Q2T, gsT, S0b = A_["kT"], A_["KpT"], A_["qT"], A_["Q2T"], A_["gsT"], A_["S0b"]
            kTs = [A_["kT"], A_["kT2"]]
            qTs = [A_["qT"], A_["qT2"]]
            Ets = [Et, Et2]
            Ests = [Est, Est2]
            Aring = Ring([A_["A0"], A_["A1"]])
            Ktall, Sball = A_["Ktall"], A_["Sball"]
            Kmr = Ring([A_["Km0"], A_["Km1"]])

            class XV:
                def __init__(self, k):
                    self.t = xT.t[:, k, 0:512]
                    self.b = Buf(f"xv{k}")
            xtiles = Ring([XV(k) for k in range(12)])
            for a_ in (A_["A0"], A_["A1"]):
                MEMSET(a_.ap[:, :], 0.0, [a_])
            for g in range(8):
                Wi, Wiv = load_unit(w_in[:, 4096 + 256 * g:4096 + 256 * (g + 1)])
                Wf, Wfv = load_unit(w_in[:, 2048 + 256 * g:2048 + 256 * (g + 1)])
                for j in range(ntt):
                    pv = pm.nxt()
                    for kc in range(16):
                        MM(pv.t[:, :256], hT.t[:, kc, j * 128:(j + 1) * 128], Wiv[:, kc, :], kc == 0, kc == 15, [hT, Wi], [pv])
                    CPA(Vg.t[:, j, :], pv.t[:, :256], [pv], [Vg])
                if full:
                    Wq, Wqv = load_unit(w_in[:, 256 * g:256 * (g + 1)])
                    WgT, Wgv_ = load_unit(w_in[:, 6144 + 256 * g:6144 + 256 * (g + 1)])
                def p1(heads, tsrc):
                    allch = chunks(0, Tn_, scol)
                    for chs in ([c for c in allch if not c[2]], [c for c in allch if c[2]]):
                        if not chs:
                            continue
                        st = []
                        for hh in heads:
                            for (c0, n, is_s) in chs:
                                pf = proj_T(Wf, Wfv, hh, c0, n)
                                t1 = tsrc.nxt()
                                ACT(t1.t[:, :n], pf.t[:, :n], AF.Sigmoid, [pf], [t1], scale=-1.0)
                                st.append((hh, c0, n, is_s, t1, tsrc.nxt(), tsrc.nxt()))
                        for (hh, c0, n, is_s, t1, t2, t3) in st:
                            h = 2 * g + hh
                            TS(t1.t[:, :n], t1.t[:, :n], oml.t[:, h:h + 1], ALU.mult, [t1, oml], [t1])
                        for (hh, c0, n, is_s, t1, t2, t3) in st:
                            ACT(t2.t[:, :n], t1.t[:, :n], AF.Ln, [t1, onec], [t2], scale=-1.0, bias=onec.t[:])
                        for (hh, c0, n, is_s, t1, t2, t3) in st:
                            mk = scans.t[:, 0:n] if is_s else scanp.t[:, 0:n]
                            SCAN(t3.t[:, :n], mk, t2.t[:, :n], [t2, scans, scanp], [t3])
                        for (hh, c0, n, is_s, t1, t2, t3) in st:
                            ACT(t2.t[:, :n], t3.t[:, :n], AF.Exp, [t3], [t2], scale=-1.0)
                        for (hh, c0, n, is_s, t1, t2, t3) in st:
                            TT(kTs[hh].ap[:, c0:c0 + n], t1.t[:, :n], t2.t[:, :n], ALU.mult, [t1, t2], [kTs[hh]])
                        for (hh, c0, n, is_s, t1, t2, t3) in st:
                            if is_s:
                                ACT(Ests[hh].t[:, 0:4], t3.t[:, 0:128].rearrange("p (a b) -> p a b", b=32)[:, :, 31], AF.Exp, [t3], [Ests[hh]])
                            else:
                                ACT(Ets[hh].t[:, c0 // 64:(c0 + n) // 64], t3.t[:, 0:n].rearrange("p (a b) -> p a b", b=64)[:, :, 63],
                                    AF.Exp, [t3], [Ets[hh]])
                        if full:
                            for (hh, c0, n, is_s, t1, t2, t3) in st:
                                pq = proj_T(Wq, Wqv, hh, c0, n)
                                ACT(t2.t[:, :n], t3.t[:, :n], AF.Exp, [t3], [t2])
                                TT(qTs[hh].ap[:, c0:c0 + n], pq.t[:, :n], t2.t[:, :n], ALU.mult, [pq, t2], [qTs[hh]])

                def p2(hh):
                    h = 2 * g + hh
                    kT, qT, Et, Est = kTs[hh], qTs[hh], Ets[hh], Ests[hh]
                    if nt > 0:
                        v3 = lambda av, lo, hi: av.ap[:, 0:nt * 128].rearrange("p (a b) -> p a b", b=128)[:, :, lo:hi]
                        CPA(v3(KpT, 64, 128), v3(kT, 64, 128), [kT], [KpT])
                        if full:
                            CPA(v3(Q2T, 0, 64), v3(qT, 0, 64), [qT], [Q2T])
                        for j in range(nt):
                            e1 = Et.t[:, 2 * j:2 * j + 1]
                            TS(KpT.ap[:, j * 128:j * 128 + 64], kT.ap[:, j * 128:j * 128 + 64], e1, ALU.mult, [kT, Et], [KpT])
                            if full:
                                TS(Q2T.ap[:, j * 128 + 64:(j + 1) * 128], qT.ap[:, j * 128 + 64:(j + 1) * 128], e1, ALU.mult, [qT, Et], [Q2T])
                        ev = Et.t[:, 0:2 * nt].rearrange("p (a b) -> p a b", b=2)
                        TT(E12t.t[:, 0:nt], ev[:, :, 0], ev[:, :, 1], ALU.mult, [Et], [E12t])
                    if nt > 0:
                        hs_ = slice(hh * 128, (hh + 1) * 128)
                        pt = ptr.nxt()
                        for j in range(nt):
                            TR(pt.t[:, j * 128:(j + 1) * 128], KpT.ap[:, j * 128:(j + 1) * 128], identb.t[:], [KpT, identb], [pt])
                        CPA(Ktall.ap[:, 0:nt * 128], pt.t[:, 0:nt * 128], [pt], [Ktall])
                        for j0 in range(0, nt, 4):
                            jn = min(4, nt - j0)
                            px = pm.nxt()
                            for jj in range(jn):
                                j = j0 + jj
                                MM(px.t[:, jj * 128:(jj + 1) * 128], Ktall.ap[:, j * 128:(j + 1) * 128], Vg.t[:, j, hs_], True, True, [Ktall, Vg], [px])
                            for jj in range(jn):
                                j = j0 + jj
                                TS(tmpall.t[:, j, :], px.t[:, jj * 128:(jj + 1) * 128], Et.t[:, 2 * j + 1:2 * j + 2], ALU.mult, [px, Et], [tmpall])
                        CPV(Sall.t[:, 0, :], S.t[:, h, :], [S], [Sall])
                        for j in range(nt):
                            STT(Sall.t[:, j + 1, :], Sall.t[:, j, :], E12t.t[:, j:j + 1], tmpall.t[:, j, :], ALU.mult, ALU.add, [Sall, E12t, tmpall], [Sall])
                        CPV(S.t[:, h, :], Sall.t[:, nt, :], [Sall], [S])
                        if full:
                            CPA(Sball.ap[:, 0:nt * 128], Sall.t[:, 0:nt, :].rearrange("p a b -> p (a b)"), [Sall], [Sball])
                            for j0 in range(0, nt, 4):
                                jn = min(4, nt - j0)
                                psc = pm.nxt()
                                for jj in range(jn):
                                    j = j0 + jj
                                    MM(psc.t[0:64, jj * 128:jj * 128 + 64], kT.ap[:, j * 128:j * 128 + 64], qT.ap[:, j * 128:j * 128 + 64], True, True, [kT, qT], [psc])
                                    MM(psc.t[:, jj * 128 + 64:(jj + 1) * 128], KpT.ap[:, j * 128:(j + 1) * 128], qT.ap[:, j * 128 + 64:(j + 1) * 128], True, True, [KpT, qT], [psc])
                                A = Aring.nxt()
                                a3 = A.ap[:, 0:jn * 128].rearrange("p (a b) -> p a b", b=128)
                                p3 = psc.t[:, 0:jn * 128].rearrange("p (a b) -> p a b", b=128)
                                TT(a3[0:64, :, 0:64], p3[0:64, :, 0:64], cmask4.t[0:64, 0:jn, 0:64], ALU.mult, [psc, cmask4], [A])
                                TT(a3[:, :, 64:128], p3[:, :, 64:128], cmask4.t[:, 0:jn, 64:128], ALU.mult, [psc, cmask4], [A])
                                po = pm.nxt()
                                for jj in range(jn):
                                    j = j0 + jj
                                    MM(po.t[:, jj * 128:(jj + 1) * 128], Vg.t[:, j, hs_], A.ap[:, jj * 128:(jj + 1) * 128], True, False, [Vg, A], [po])
                                    MM(po.t[:, jj * 128:(jj + 1) * 128], Sball.ap[:, j * 128:(j + 1) * 128], Q2T.ap[:, j * 128:(j + 1) * 128], False, True, [Sball, Q2T], [po])
                                CPA(oacc.t[:, j0 * 128:(j0 + jn) * 128], po.t[:, 0:jn * 128], [po], [oacc])
                    if samp:
                        cs = slice(scol, scol + 128)
                        vj = Vg.t[:, nt, hh * 128:(hh + 1) * 128]
                        DMA(S0f.t[:], s0[:, h].rearrange("s k v -> k s v"), [], [S0f])
                        CPA(S0b.ap[:, :], S0f.t[:].rearrange("p a b -> p (a b)"), [S0f], [S0b])
                        psc = pss_.nxt()
                        MM(psc.t[:, 0:128], kT.ap[:, cs], qT.ap[:, cs], True, True, [kT, qT], [psc])
                        A2 = A_["A2"]
                        TT(A2.ap[:, :], psc.t[:, 0:128], smask.t[:], ALU.mult, [psc, smask], [A2])
                        po = pss_.nxt()
                        for s in range(4):
                            MM(po.t[:, 32 * s:32 * s + 32], vj, A2.ap[:, 32 * s:32 * s + 32], True, False, [Vg, A2], [po])
                            MM(po.t[:, 32 * s:32 * s + 32], S0b.ap[:, 128 * s:128 * s + 128], qT.ap[:, scol + 32 * s:scol + 32 * s + 32],
                               False, True, [S0b, qT], [po])
                        CPA(oacc.t[:, cs], po.t[:, 0:128], [po], [oacc])
                        pt = ptr.nxt()
                        TR(pt.t[:, 0:128], kT.ap[:, cs], identb.t[:], [kT, identb], [pt])
                        for s in range(4):
                            Km = Kmr.nxt()
                            TS(Km.ap[:, :], pt.t[:, 0:128], seqm.t[:, s:s + 1], ALU.mult, [pt, seqm], [Km])
                            px = pss_.nxt()
                            MM(px.t[:, 0:128], Km.ap[:, :], vj, True, True, [Km, Vg], [px])
                            t1 = tf.nxt()
                            TS(t1.t[:, :128], S0f.t[:, s, :], Est.t[:, s:s + 1], ALU.mult, [S0f, Est], [t1])
                            so = tf.nxt()
                            STT(so.t[:, :128], px.t[:, 0:128], Est.t[:, s:s + 1], t1.t[:, :128], ALU.mult, ALU.add, [px, Est, t1], [so])
                            DMA(o_Ss[s, h], so.t[:, :128], [so], [])
                    if full:
                        for (c0, n, _) in chunks(0, Tn_):
                            pg = proj_T(WgT, Wgv_, hh, c0, n)
                            ACT(gsT.ap[:, c0:c0 + n], pg.t[:, :n], AF.Silu, [pg], [gsT])
                            osq = tb.nxt()
                            ACT(osq.t[:, :n], oacc.t[:, c0:c0 + n], AF.Square, [oacc], [osq])
                            pq_ = pm.nxt()
                            MM(pq_.t[:, :n], onesb.t[:], osq.t[:, :n], True, True, [onesb, osq], [pq_])
                            rr = rstd_from(pq_, n, 1.0 / 128)
                            on = tf.nxt()
                            STT(on.t[:, :n], oacc.t[:, c0:c0 + n], pvec.t[:, 0:1], rr.t[:, :n], ALU.mult, ALU.mult, [oacc, pvec, rr], [on])
                            TT(g4.t[:, hh, c0:c0 + n], on.t[:, :n], gsT.ap[:, c0:c0 + n], ALU.mult, [on, gsT], [g4])
                if full:
                    p1([0], tf)
                    p1([1], tf)
                else:
                    p1([0, 1], xtiles)
                p2(0)
                p2(1)
                if full:
                    Wo, Wov = load_rows(w_oa[256 * g:256 * (g + 1), :])
                    wo_partial(Wo, Wov, 0, Tn_)
            if blk["last"]:
                DMA(o_Sp.rearrange("h k v -> k h v"), S.t[:], [S], [])

        def stage_ffn(l, blk, a):
            nt = len(blk["tiles"])
            samp = blk["sample"]
            scol = nt * 128 if samp else None
            Tn_ = (nt + (1 if samp else 0)) * 128
            norm(1 + 2 * l, a, Tn_)
            chs = chunks(a, Tn_, scol)
            first = blk["first"]
            def ffn_tail(cc, c2, pg, fu, c0, n, G):
                ACT(cc.t[:, :n], c2.t[:, :n], AF.Silu, [c2], [cc])
                TT(G.t[:, fu, c0:c0 + n], cc.t[:, :n], pg.t[:, :n], ALU.mult, [cc, pg], [G])

            pend = None
            for sl in range(22):
                Wu, Wuv = load_unit(f_in[l][:, 256 * sl:256 * (sl + 1)])
                Wg, Wgv = load_unit(f_in[l][:, DFF + 256 * sl:DFF + 256 * (sl + 1)])
                for fu in range(2):
                    U = 2 * sl + fu
                    prev = None
                    for (c0, n, is_s) in chs:
                        pu = proj_T(Wu, Wuv, fu, c0, n)
                        pg = proj_T(Wg, Wgv, fu, c0, n)
                        u_ = us.nxt()
                        w0 = cw.t[:, l, 0, U:U + 1]
                        w1 = cw.t[:, l, 1, U:U + 1]
                        w2 = cw.t[:, l, 2, U:U + 1]
                        cc = tf.nxt()
                        ACT(cc.t[:, :n], pu.t[:, :n], AF.Identity, [pu, cw, cb], [cc], scale=w2, bias=cb.t[:, l, U:U + 1])
                        c1 = tf.nxt()
                        c2 = tf.nxt()
                        if is_s:
                            u4 = u_.t[:, 0:136].rearrange("p (s c) -> p s c", c=34)
                            CPV(u4[:, :, 0:2], cstT.t[:, U, 8 * l:8 * l + 8].rearrange("p (s c) -> p s c", c=2), [cstT], [u_])
                            CPA(u4[:, :, 2:34], pu.t[:, 0:128].rearrange("p (s c) -> p s c", c=32), [pu], [u_])
                            v3 = lambda t_: t_.t[:, 0:128].rearrange("p (s c) -> p s c", c=32)
                            STT(v3(c1), u4[:, :, 1:33], w1, v3(cc), ALU.mult, ALU.add, [u_, cw, cc], [c1])
                            STT(v3(c2), u4[:, :, 0:32], w0, v3(c1), ALU.mult, ALU.add, [u_, cw, c1], [c2])
                            CPV(ucol.t[:, U, 2:10].rearrange("p (s c) -> p s c", c=2), u4[:, :, 32:34], [u_], [ucol])
                        else:
                            if prev is not None:
                                CPV(u_.t[:, 0:2], prev[0].t[:, prev[1]:prev[1] + 2], [prev[0]], [u_])
                            elif first:
                                MEMSET(u_.t[:, 0:2], 0.0, [u_])
                            else:
                                CPV(u_.t[:, 0:2], uH.t[:, l, U, :], [uH], [u_])
                            CPA(u_.t[:, 2:2 + n], pu.t[:, :n], [pu], [u_])
                            STT(c1.t[:, :n], u_.t[:, 1:1 + n], w1, cc.t[:, :n], ALU.mult, ALU.add, [u_, cw, cc], [c1])
                            STT(c2.t[:, :n], u_.t[:, 0:n], w0, c1.t[:, :n], ALU.mult, ALU.add, [u_, cw, c1], [c2])
                            prev = (u_, n)
                            last_prompt = (c0 + n == nt * 128)
                            if last_prompt:
                                if blk["last"]:
                                    CPV(ucol.t[:, U, 0:2], u_.t[:, n:n + 2], [u_], [ucol])
                                else:
                                    CPV(uH.t[:, l, U, :], u_.t[:, n:n + 2], [u_], [uH])
                        if pend is not None:
                            ffn_tail(*pend)
                        pend = (cc, c2, pg, fu, c0, n, [g4, g4b][sl % 2])
                if sl % 2 == 0:
                    continue
                if pend is not None:
                    ffn_tail(*pend)
                    pend = None
                Wd0, Wd0v = load_rows(f_dn[l][256 * (sl - 1):256 * sl, :])
                Wd1, Wd1v = load_rows(f_dn[l][256 * sl:256 * (sl + 1), :])
                for cu in range(16):
                    for (c0, n, _) in chunks(a, Tn_):
                        py = pm.nxt()
                        k4 = 0
                        for (Wt_, Wv_, G_) in ((Wd0, Wd0v, g4), (Wd1, Wd1v, g4b)):
                            for fu in range(2):
                                MM(py.t[:, :n], Wv_[:, fu, cu * 128:(cu + 1) * 128], G_.t[:, fu, c0:c0 + n], k4 == 0, k4 == 3, [Wt_, G_], [py])
                                k4 += 1
                        TT(xT.t[:, cu, c0:c0 + n], xT.t[:, cu, c0:c0 + n], py.t[:, :n], ALU.add, [xT, py], [xT])
            if blk["last"]:
                for q4 in range(11):
                    pp = pm.nxt()
                    for i in range(4):
                        TR(pp.t[0:10, i * 128:(i + 1) * 128], ucol.t[:, q4 * 4 + i, :], identf.t[:], [ucol, identf], [pp])
                    st = tf.nxt()
                    CPA(st.t[0:10, :], pp.t[0:10, :], [pp], [st])
                    DMA(o_co[l][:, q4 * 512:(q4 + 1) * 512], st.t[0:10, :], [st], [])

        import os
        KSUB = int(os.environ.get("KSUB", "99"))
        KSUB2 = int(os.environ.get("KSUB2", "99"))

        def stage_attn(blk):
            nt = len(blk["tiles"])
            samp = blk["sample"]
            scol = nt * 128 if samp else None
            ntt = nt + (1 if samp else 0)
            Tn_ = ntt * 128
            nh = blk["nh"]
            qj0 = blk["qj0"]
            qa = qj0 * 128
            nkt = nh + nt
            gt0 = blk["tiles"][0]
            KTg, QTg, kcb, vcb, KcT = B_["KTg"], B_["QTg"], B_["kc"], B_["vc"], B_["KcT"]
            KT3 = KTg.ap[:, :].rearrange("p (h c) -> p h c", h=2)
            QT3 = QTg.ap[:, :].rearrange("p (h c) -> p h c", h=2)
            E0r = Ring([B_["E0a"], B_["E0b"]])
            E4r = Ring([B_["E4a"], B_["E4b"]])
            E3r = Ring([B_["E3a"], B_["E3b"]])
            E1r = Ring([B_["E1a"], B_["E1b"]])
            E2r = Ring([B_["E2a"], B_["E2b"]])
            Ecr = Ring([B_["Eca"], B_["Ecb"]])
            for e_ in ("E0a", "E0b", "E4a", "E4b"):
                MEMSET(B_[e_].ap[:, :], 0.0, [B_[e_]])

            def normalize(p, n, col):
                sq = tb.nxt()
                ACT(sq.t[:, :n], p.t[:, :n], AF.Square, [p], [sq])
                p2 = pm.nxt()
                MM(p2.t[:, :n], onesb.t[:], sq.t[:, :n], True, True, [onesb, sq], [p2])
                rr = rstd_from(p2, n, 1.0 / 128)
                kn = tf.nxt()
                STT(kn.t[:, :n], p.t[:, :n], pvec.t[:, col:col + 1], rr.t[:, :n], ALU.mult, ALU.mult, [p, pvec, rr], [kn])
                return kn

            def out_tile_row(gt):
                if gt == 32:
                    return ("s", 0)
                if gt >= 28:
                    return ("p", (gt - 28) * 128)
                return None

            tiles = blk["tiles"] + ([32] if samp else [])
            def attn_inner(g):
                for hh in range(2):
                    h = 2 * g + hh
                    hs = slice(hh * 128, (hh + 1) * 128)
                    for j in range(qj0, nt):
                        qc = QT3[:, hh, j * 128:(j + 1) * 128]
                        ks0 = nh + j - 4
                        psA = pm.nxt()
                        psB = pss_.nxt()
                        for kt in range(5):
                            dst = psA.t[:, kt * 128:(kt + 1) * 128] if kt < 4 else psB.t[:, 0:128]
                            MM(dst, KT3[:, hh, (ks0 + kt) * 128:(ks0 + kt + 1) * 128], qc, True, True, [KTg, QTg], [psA if kt < 4 else psB])
                        if KSUB2 <= 1:
                            continue
                        kti = [gt0 - nh + ks0 + kt - 19 for kt in range(5)]
                        E0 = E0r.nxt()
                        ACT(E0.ap[:, 0:64], psA.t[:, 0:64], AF.Exp, [psA, cbias], [E0], scale=SC, bias=cbias.t[:, kti[0], h:h + 1])
                        ACT(E0.ap[64:128, 64:128], psA.t[64:128, 64:128], AF.Exp, [psA, cbias], [E0], scale=SC, bias=cbias.t[64:128, kti[0], h:h + 1])
                        E1 = E1r.nxt()
                        ACT(E1.ap[:, :], psA.t[:, 128:256], AF.Exp, [psA, cbias], [E1], scale=SC, bias=cbias.t[:, kti[1], h:h + 1])
                        E2 = E2r.nxt()
                        ACT(E2.ap[:, :], psA.t[:, 256:384], AF.Exp, [psA, cbias], [E2], scale=SC, bias=cbias.t[:, kti[2], h:h + 1])
                        t3 = tf.nxt()
                        ACT(t3.t[:, :128], psA.t[:, 384:512], AF.Exp, [psA, kneg], [t3], scale=SC, bias=kneg.t[:, kti[3]:kti[3] + 1])
                        E3 = E3r.nxt()
                        TT(E3.ap[:, :], t3.t[:, :128], eT3g.t[:, hh, :], ALU.mult, [t3, eT3g], [E3])
                        t4 = tf.nxt()
                        ACT(t4.t[0:64, :128], psB.t[0:64, 0:128], AF.Exp, [psB, kneg], [t4], scale=SC, bias=kneg.t[0:64, kti[4]:kti[4] + 1])
                        ACT(t4.t[64:128, 64:128], psB.t[64:128, 64:128], AF.Exp, [psB, kneg], [t4], scale=SC, bias=kneg.t[64:128, kti[4]:kti[4] + 1])
                        E4 = E4r.nxt()
                        TT(E4.ap[0:64, :], t4.t[0:64, :128], eT4g.t[0:64, hh, :], ALU.mult, [t4, eT4g], [E4])
                        TT(E4.ap[64:128, 64:128], t4.t[64:128, 64:128], eT4g.t[64:128, hh, 64:128], ALU.mult, [t4, eT4g], [E4])
                        Es = [E0, E1, E2, E3, E4]
                        if KSUB2 <= 2:
                            continue
                        pden = pss_.nxt()
                        KSUB3 = int(os.environ.get("KSUB3", "3"))
                        for kt in range(5):
                            if KSUB3 & 1:
                                MM(pden.t[:, 0:128], onesb.t[:], Es[kt].ap[:, :], kt == 0, kt == 4, [onesb, Es[kt]], [pden])
                            if KSUB3 & 16:
                                MM(pden.t[:, 0:128], onesb.t[:], g4.t[:, 0, kt * 128:(kt + 1) * 128], kt == 0, kt == 4, [onesb, g4], [pden])
                            if KSUB3 & 32:
                                MM(pden.t[:, 0:128], onesb.t[:], Es[kt].ap[:, :], kt == 0, kt == 4, [onesb], [pden])
                            if KSUB3 & 4:
                                MM(pden.t[:, 0:128], onesb.t[:], Es[1].ap[:, :], kt == 0, kt == 4, [onesb, Es[1]], [pden])
                            if KSUB3 & 8:
                                MM(pden.t[:, 0:128], onesb.t[:], Es[kt].ap[:, :], True, True, [onesb, Es[kt]], [pden])
                        po = pss_.nxt()
                        for kt in range(5):
                            if KSUB3 & 2:
                                MM(po.t[:, 0:128], Vg.t[:, ks0 + kt, hs], Es[kt].ap[:, :], kt == 0, kt == 4, [Vg, Es[kt]], [po])
                        if KSUB2 <= 3:
                            continue
                        rd = tf.nxt()
                        TS(rd.t[:, :128], pden.t[:, 0:128], 1e-30, ALU.add, [pden], [rd])
                        rr = tf.nxt()
                        RECIP(rr.t[:, :128], rd.t[:, :128], [rd], [rr])
                        TT(g4.t[:, hh, j * 128:(j + 1) * 128], po.t[:, 0:128], rr.t[:, :128], ALU.mult, [po, rr], [g4])
                if samp:
                    kc3 = kcb.ap[:, :].rearrange("p (t c) -> p t c", c=256)
                    vc3 = vcb.ap[:, :].rearrange("p (t c) -> p t c", c=256)
                    for s in range(4):
                        DMA(kc3, ck[s][:, 256 * g:256 * (g + 1)].rearrange("(t p) c -> p t c", p=128), [], [kcb], q="pool")
                        DMA(vc3, cv[s][:, 256 * g:256 * (g + 1)].rearrange("(t p) c -> p t c", p=128), [], [vcb], q="pool")
                        pt = ptr.nxt()
                        for hh in range(2):
                            for kt in range(4):
                                i = hh * 4 + kt
                                TR(pt.t[:, i * 128:(i + 1) * 128], kc3[:, kt, hh * 128:(hh + 1) * 128], identb.t[:], [kcb, identb], [pt])
                        CPA(KcT.ap[:, :], pt.t[:, :], [pt], [KcT])
                        for hh in range(2):
                            h = 2 * g + hh
                            hs = slice(hh * 128, (hh + 1) * 128)
                            qc = QT3[:, hh, scol + 32 * s:scol + 32 * s + 32]
                            psc = pss_.nxt()
                            for kt in range(4):
                                i = hh * 4 + kt
                                MM(psc.t[:, kt * 32:(kt + 1) * 32], KcT.ap[:, i * 128:(i + 1) * 128], qc, True, True, [KcT, QTg], [psc])
                            MM(psc.t[:, 128:160], KT3[:, hh, nkt * 128:(nkt + 1) * 128], qc, True, True, [KTg, QTg], [psc])
                            Ec = Ecr.nxt()
                            ACT(Ec.ap[:, 0:96], psc.t[:, 0:96], AF.Exp, [psc, c0t], [Ec], scale=SC, bias=c0t.t[:, h:h + 1])
                            t3 = tf.nxt()
                            ACT(t3.t[:, :32], psc.t[:, 96:128], AF.Exp, [psc], [t3], scale=SC)
                            TT(Ec.ap[:, 96:128], t3.t[:, :32], eT3g.t[:, hh, 0:32], ALU.mult, [t3, eT3g], [Ec])
                            tn = tf.nxt()
                            ACT(tn.t[:, :32], psc.t[:, 128:160], AF.Exp, [psc], [tn], scale=SC)
                            STT(Ec.ap[:, 128:160], tn.t[:, :32], seqm.t[:, s:s + 1], eTng.t[:, hh, :], ALU.mult, ALU.mult, [tn, seqm, eTng], [Ec])
                            pden = pss_.nxt()
                            po = pss_.nxt()
                            for kt in range(5):
                                ec = Ec.ap[:, 128:160] if kt == 4 else Ec.ap[:, kt * 32:(kt + 1) * 32]
                                MM(pden.t[:, 0:32], onesb.t[:], ec, kt == 0, kt == 4, [onesb, Ec], [pden])
                            for kt in range(5):
                                ec = Ec.ap[:, 128:160] if kt == 4 else Ec.ap[:, kt * 32:(kt + 1) * 32]
                                lv = Vg.t[:, nkt, hs] if kt == 4 else vc3[:, kt, hs]
                                MM(po.t[:, 0:32], lv, ec, kt == 0, kt == 4, [Vg, vcb, Ec], [po])
                            rd = tf.nxt()
                            TS(rd.t[:, :32], pden.t[:, 0:32], 1e-30, ALU.add, [pden], [rd])
                            rr = tf.nxt()
                            RECIP(rr.t[:, :32], rd.t[:, :32], [rd], [rr])
                            TT(g4.t[:, hh, scol + 32 * s:scol + 32 * s + 32], po.t[:, 0:32], rr.t[:, :32], ALU.mult, [po, rr], [g4])
                if KSUB <= 3:
                    return
                Wo, Wov = load_rows(w_ob[256 * g:256 * (g + 1), :])
                wo_partial(Wo, Wov, qa, Tn_)


            KONLY = int(os.environ.get("KONLY", "0"))
            for g in range(8):
                if KONLY:
                    break
                Wv, Wvv = load_unit(w_qkv[:, 4096 + 256 * g:4096 + 256 * (g + 1)])
                Wk, Wkv = load_unit(w_qkv[:, 2048 + 256 * g:2048 + 256 * (g + 1)])
                DMA(T3g.t[:], d_T3[:, 2 * g:2 * g + 2, :], [], [T3g])
                DMA(T4g.t[:], d_T4[:, 2 * g:2 * g + 2, :], [], [T4g])
                DMA(Tng.t[:], d_Tn[:, 2 * g:2 * g + 2, :], [], [Tng])
                ACT(eT3g.t[:], T3g.t[:], AF.Exp, [T3g], [eT3g])
                ACT(eT4g.t[:], T4g.t[:], AF.Exp, [T4g], [eT4g])
                ACT(eTng.t[:], Tng.t[:], AF.Exp, [Tng], [eTng])
                if nh:
                    DMA(Vg.t[:, 0:4, :], vscr[:, 256 * g:256 * (g + 1)].rearrange("(t p) c -> p t c", p=128), [Bvscr], [Vg])
                    for hh in range(2):
                        DMA(KT3[:, hh, 0:512], kscr[2 * g + hh], [Bkscr], [KTg])
                for j, gt in enumerate(tiles):
                    pv = pm.nxt()
                    for kc in range(16):
                        MM(pv.t[:, :256], hT.t[:, kc, j * 128:(j + 1) * 128], Wvv[:, kc, :], kc == 0, kc == 15, [hT, Wv], [pv])
                    CPA(Vg.t[:, nh + j, :], pv.t[:, :256], [pv], [Vg])
                    orow = out_tile_row(gt)
                    if orow is not None:
                        st = tf.nxt()
                        CPA(st.t[:, :256], pv.t[:, :256], [pv], [st])
                        dst = o_vs if orow[0] == "s" else o_vp[orow[1]:orow[1] + 128, :]
                        DMA(dst[:, 256 * g:256 * (g + 1)], st.t[:, :256], [st], [])
                Wq, Wqv = load_unit(w_qkv[:, 256 * g:256 * (g + 1)])
                for hh in range(2):
                    h = 2 * g + hh
                    for (c0, n, _) in chunks(0, Tn_):
                        pk = proj_T(Wk, Wkv, hh, c0, n)
                        kn = normalize(pk, n, 2)
                        CPA(KT3[:, hh, nh * 128 + c0:nh * 128 + c0 + n], kn.t[:, :n], [kn], [KTg])
                        for jj in range(n // 128):
                            gt = tiles[c0 // 128 + jj]
                            orow = out_tile_row(gt)
                            if orow is None:
                                continue
                            pso = pss_.nxt()
                            TR(pso.t[:, 0:128], kn.t[:, jj * 128:(jj + 1) * 128], identf.t[:], [kn, identf], [pso])
                            st = tf.nxt()
                            CPV(st.t[:, :128], pso.t[:, 0:128], [pso], [st])
                            dst = o_ks if orow[0] == "s" else o_kp[orow[1]:orow[1] + 128, :]
                            DMA(dst[:, h * 128:(h + 1) * 128], st.t[:, :128], [st], [])
                    for (c0, n, _) in chunks(qa, Tn_):
                        pq = proj_T(Wq, Wqv, hh, c0, n)
                        qn = normalize(pq, n, 1)
                        CPA(QT3[:, hh, c0:c0 + n], qn.t[:, :n], [qn], [QTg])
                if KSUB <= 1:
                    continue
                if nh == 0:
                    DMA(vscr[:, 256 * g:256 * (g + 1)].rearrange("(t p) c -> p t c", p=128), Vg.t[:, nt - 4:nt, :], [Vg], [Bvscr])
                    for hh in range(2):
                        DMA(kscr[2 * g + hh], KT3[:, hh, (nt - 4) * 128:nt * 128], [KTg], [Bkscr])
                if KSUB <= 2:
                    continue
                attn_inner(g)
            if KONLY:
                for g in range(KONLY):
                    attn_inner(g)

        blocks = [
            dict(tiles=list(range(0, 7)), full=False, sample=False, last=False, first=True),
            dict(tiles=list(range(7, 14)), full=False, sample=False, last=False, first=False),
            dict(tiles=list(range(14, 19)), full=False, sample=False, last=False, first=False),
            dict(tiles=list(range(19, 26)), full=True, sample=False, last=False, first=True, nh=0, qj0=4),
            dict(tiles=list(range(26, 32)), full=True, sample=True, last=True, first=False, nh=4, qj0=0),
        ]
        import os
        STOP = int(os.environ.get("KSTOP", "99"))
        for bi, blk in enumerate(blocks):
            nt = len(blk["tiles"])
            Tn_ = (nt + (1 if blk["sample"] else 0)) * 128
            if STOP <= 1:
                break
            KONLY = int(os.environ.get("KONLY", "0"))
            if KONLY:
                if bi != 3:
                    continue
                stage_attn(blk)
                break
            P.barrier()
            load_x(blk)
            norm(0, 0, Tn_)
            if STOP <= 2:
                break
            if not blk["full"]:
                P.barrier()
            stage_hgrn(blk)
            if STOP <= 3:
                break
            if STOP <= 4 and bi == 2:
                break
            if blk["full"]:
                qa = blk["qj0"] * 128
                if STOP <= 5:
                    break
                stage_ffn(0, blk, 0)
                if STOP <= 6:
                    break
                norm(2, 0, Tn_)
                P.barrier()
                stage_attn(blk)
                if STOP <= 7:
                    break
                stage_ffn(1, blk, qa)
                store_x(blk)
                if STOP <= 8:
                    break
        P.finish()
        with nc.Block() as block:
            P.replay(block)
    return nc


_CACHE = {}


def _host_consts():
    ident = np.eye(128, dtype=np.float32)
    s = np.arange(128)[:, None]
    t = np.arange(128)[None, :]
    cmask = (s <= t).astype(np.float32)
    smask = ((s <= t) & (s // 32 == t // 32)).astype(np.float32)
    scanp = np.ones((128, 512), np.float32)
    scanp[:, ::64] = 0.0
    scans = np.ones((128, 128), np.float32)
    scans[:, ::32] = 0.0
    seqm = (np.arange(128)[:, None] // 32 == np.arange(4)[None, :]).astype(np.float32)
    return dict(ident=ident, cmask=cmask, smask=smask, scanp=scanp, scans=scans, seqm=seqm)


def kernel(x_prompt, x_sample, state_a_S, cache_b_k, cache_b_v, state_ffn_conv,
           norm_mix, norm_ffn, a_w_in, a_gamma_lb, a_norm_o, a_w_o,
           b_w_qkv, b_q_norm, b_k_norm, b_rel_bias, b_w_o,
           f_w_in, f_conv_w, f_conv_b, f_w_down):
    f32 = lambda a: np.ascontiguousarray(np.asarray(a, dtype=np.float32))
    x_prompt, x_sample = f32(x_prompt), f32(x_sample)
    if "nc" not in _CACHE:
        _CACHE["nc"] = build_program()
    nc = _CACHE["nc"]
    cst_ = _host_consts()
    gains = np.stack([f32(norm_mix)[0], f32(norm_ffn)[0], f32(norm_mix)[1], f32(norm_ffn)[1]], 0)
    gains = np.ascontiguousarray(gains.reshape(4, 16, 128).transpose(2, 0, 1))
    gam = np.ascontiguousarray(f32(a_gamma_lb).reshape(3, 16, 128).transpose(2, 0, 1))
    pvec = np.ascontiguousarray(np.stack([f32(a_norm_o)[0], f32(b_q_norm)[0], f32(b_k_norm)[0]], 1))
    rb = f32(b_rel_bias)[0]
    c0t = np.ascontiguousarray(np.broadcast_to(rb[0][None, :], (128, 16)))
    kk = np.arange(128)[:, None]
    qq = np.arange(128)[None, :]
    T3 = np.ascontiguousarray(rb[np.maximum(kk - qq, 0)].transpose(0, 2, 1))
    T4 = np.ascontiguousarray(rb[np.minimum(kk - qq, 63) + 128].transpose(0, 2, 1))
    ii = np.arange(32)[None, :]
    Tn = np.ascontiguousarray(rb[(kk % 32) - ii + 128].transpose(0, 2, 1))
    cw = np.ascontiguousarray(f32(f_conv_w).reshape(2, 3, NFU, 128).transpose(3, 0, 1, 2))
    cb = np.ascontiguousarray(f32(f_conv_b).reshape(2, NFU, 128).transpose(2, 0, 1))
    shared = dict(w_in=f32(a_w_in)[0], w_oa=f32(a_w_o)[0], w_qkv=f32(b_w_qkv)[0], w_ob=f32(b_w_o)[0],
                  f_in=f32(f_w_in), f_dn=f32(f_w_down), gains=gains, gam=gam, pvec=pvec, c0t=c0t,
                  T3=T3, T4=T4, Tn=Tn, cw=cw, cb=cb, **cst_)
    sS = f32(state_a_S)[0]
    cK = f32(cache_b_k)[0].reshape(32, 512, D)
    cV = f32(cache_b_v)[0].reshape(32, 512, D)
    cst = f32(state_ffn_conv)
    in_maps = []
    for c in range(8):
        b, q = c // 4, c % 4
        base = 1024 * (q - 3)
        xs = np.zeros((33 * 128, D), np.float32)
        lo = max(base, 0)
        xs[lo - base:4096] = x_prompt[b, lo:base + 4096]
        xs[4096:] = x_sample[4 * c:4 * c + 4].reshape(128, D)
        kneg = np.zeros((128, 13), np.float32)
        for t in range(13):
            if base + (19 + t) * 128 < 0:
                kneg[:, t] = NEG
        m = dict(shared)
        m.update(xs=xs, s0=np.ascontiguousarray(sS[4 * c:4 * c + 4]), ck=np.ascontiguousarray(cK[4 * c:4 * c + 4]),
                 cv=np.ascontiguousarray(cV[4 * c:4 * c + 4]),
                 cst=np.ascontiguousarray(cst[:, 4 * c:4 * c + 4].reshape(16, DFF)), kneg=kneg)
        in_maps.append(m)
    import os
    ncores = int(os.environ.get("KCORES", "8"))
    res = run_bass_kernel_spmd(nc, in_maps[:ncores], core_ids=list(range(ncores)))
    R = list(res.results) + [res.results[0]] * (8 - ncores)
    y_p = np.zeros((2, 4096, D), np.float32)
    y_s = np.zeros((32, 32, D), np.float32)
    Sp = np.zeros((1, 2, 16, 128, 128), np.float32)
    Ss = np.zeros((1, 32, 16, 128, 128), np.float32)
    kp = np.zeros((1, 2, 512, 16, 128), np.float32)
    vp = np.zeros((1, 2, 512, 16, 128), np.float32)
    ks = np.zeros((1, 32, 32, 16, 128), np.float32)
    vs = np.zeros((1, 32, 32, 16, 128), np.float32)
    cp = np.zeros((2, 2, 2, DFF), np.float32)
    cs = np.zeros((2, 32, 2, DFF), np.float32)
    for c in range(8):
        b, q = c // 4, c % 4
        r = R[c]
        y_p[b, 1024 * q:1024 * (q + 1)] = r["yp"]
        y_s[4 * c:4 * c + 4] = r["ys"].reshape(4, 32, D)
        Ss[0, 4 * c:4 * c + 4] = r["Ss"]
        ks[0, 4 * c:4 * c + 4] = r["ks"].reshape(4, 32, 16, 128)
        vs[0, 4 * c:4 * c + 4] = r["vs"].reshape(4, 32, 16, 128)
        cs[:, 4 * c:4 * c + 4] = r["co"][:, 2:10].reshape(2, 4, 2, DFF)
        if q == 3:
            Sp[0, b] = r["Sp"]
            kp[0, b] = r["kp"].reshape(512, 16, 128)
            vp[0, b] = r["vp"].reshape(512, 16, 128)
            cp[:, b] = r["co"][:, 0:2]
    return (y_p, y_s, Sp, Ss, kp, vp, ks, vs, cp, cs)
```

```python
import numpy as np
from contextlib import ExitStack
import concourse.bass as bass
import concourse.mybir as mybir
from concourse.bass_utils import run_bass_kernel_spmd

F32 = mybir.dt.float32
BF16 = mybir.dt.bfloat16
AF = mybir.ActivationFunctionType
ALU = mybir.AluOpType

D = 2048
DFF = 5632
NFU = 44
EPS = 1e-6
NEG = -30000.0
SC = 128.0 ** -0.5


class Buf:
    __slots__ = ("name", "w", "r")

    def __init__(self, name="b"):
        self.name = name
        self.w = None
        self.r = {}


class Prog:
    ENGS = ("pe", "act", "dve", "pool", "sp")

    def __init__(self, nc, n_dma_sp=8, n_dma_pool=6):
        self.nc = nc
        self.ops = {e: [] for e in self.ENGS}
        self.known = {e: {} for e in self.ENGS}
        self.cnt = {e: 0 for e in self.ENGS}
        self.sems = {}
        self.dma_pool = {"sp": [f"dsp{i}" for i in range(n_dma_sp)],
                         "pool": [f"dpl{i}" for i in range(n_dma_pool)]}
        self.dma_tgt = {}
        self.dma_idx = {"sp": 0, "pool": 0}

    def semkeys(self):
        ks = [f"c_{e}" for e in ("pe", "act", "dve", "pool")]
        for q in self.dma_pool.values():
            ks += q
        return ks

    def _collect(self, eng, reads, writes):
        waits = {}

        def add(ev):
            if ev is None:
                return
            k, v = ev
            if waits.get(k, 0) < v:
                waits[k] = v
        for b in reads:
            add(b.w)
        for b in writes:
            add(b.w)
            for k, v in b.r.items():
                add((k, v))
        out = []
        kn = self.known[eng]
        for k, v in waits.items():
            if eng == "pe" and k == "c_pe":
                continue
            if kn.get(k, 0) >= v:
                continue
            kn[k] = v
            out.append((k, v))
        return out

    def _commit(self, ev, reads, writes):
        k, v = ev
        for b in reads:
            if b.r.get(k, 0) < v:
                b.r[k] = v
        for b in writes:
            b.w = ev
            b.r = {}

    def op(self, eng, fn, reads=(), writes=()):
        waits = self._collect(eng, reads, writes)
        self.cnt[eng] += 1
        ev = (f"c_{eng}", self.cnt[eng])
        self.ops[eng].append((waits, fn, (ev[0], 1)))
        self._commit(ev, reads, writes)
        return ev

    def dma(self, q, out, in_, reads=(), writes=(), **kw):
        pool = self.dma_pool[q]
        i = self.dma_idx[q]
        self.dma_idx[q] = i + 1
        sk = pool[i % len(pool)]
        prev = self.dma_tgt.get(sk, 0)
        waits = self._collect(q, reads, writes)
        if prev > 0 and self.known[q].get(sk, 0) < prev:
            self.known[q][sk] = prev
            waits.append((sk, prev))
        tgt = prev + 16
        self.dma_tgt[sk] = tgt
        ev = (sk, tgt)
        self.ops[q].append((waits, (lambda e, o=out, i_=in_, k=kw: e.dma_start(out=o, in_=i_, **k)), (sk, 16)))
        self._commit(ev, reads, writes)
        return ev

    def _all_events(self):
        waits = [(sk, t) for sk, t in self.dma_tgt.items() if t > 0]
        for e in ("pe", "act", "dve", "pool"):
            if self.cnt[e] > 0:
                waits.append((f"c_{e}", self.cnt[e]))
        return waits

    def barrier(self):
        allw = self._all_events()
        for e in self.ENGS:
            w = []
            for k, v in allw:
                if k == f"c_{e}" and e == "pe":
                    continue
                if self.known[e].get(k, 0) < v:
                    self.known[e][k] = v
                    w.append((k, v))
            if w:
                self.ops[e].append((w, None, None))

    def finish(self):
        self.ops["sp"].append((self._all_events(), None, None))

    def replay(self, block):
        sems = self.sems

        def run(engname):
            def f(e):
                for waits, fn, inc in self.ops[engname]:
                    for k, v in waits:
                        e.wait_ge(sems[k], v)
                    if fn is not None:
                        ins = fn(e)
                        if inc is not None:
                            ins.then_inc(sems[inc[0]], inc[1])
            return f
        block.tensor(run("pe"))
        block.scalar(run("act"))
        block.vector(run("dve"))
        block.gpsimd(run("pool"))
        block.sync(run("sp"))


class T:
    def __init__(self, t, name):
        self.t = t
        self.b = Buf(name)

    def __getitem__(self, idx):
        return self.t[idx]


class Ring:
    def __init__(self, items):
        self.items = items
        self.i = 0

    def nxt(self):
        x = self.items[self.i % len(self.items)]
        self.i += 1
        return x


def chunks(a, b, scol=None):
    out = []
    end = b if scol is None else min(b, scol)
    c = a
    while c < end:
        n = min(512, end - c)
        out.append((c, n, False))
        c += n
    if scol is not None and b > scol:
        out.append((scol, 128, True))
    return out


def build_program():
    nc = bass.Bass("TRN2", target_bir_lowering=False)

    def din(name, shape):
        return nc.dram_tensor(name, shape, F32, kind="ExternalInput").ap()

    def dout(name, shape):
        return nc.dram_tensor(name, shape, F32, kind="ExternalOutput").ap()

    xs = din("xs", [33 * 128, D])
    s0 = din("s0", [4, 16, 128, 128])
    ck = din("ck", [4, 512, D])
    cv = din("cv", [4, 512, D])
    cst = din("cst", [16, DFF])
    w_in = din("w_in", [D, 8192])
    w_oa = din("w_oa", [D, D])
    w_qkv = din("w_qkv", [D, 6144])
    w_ob = din("w_ob", [D, D])
    f_in = din("f_in", [2, D, 2 * DFF])
    f_dn = din("f_dn", [2, DFF, D])
    d_gains = din("gains", [128, 4, 16])
    d_gam = din("gam", [128, 3, 16])
    d_pvec = din("pvec", [128, 3])
    d_c0 = din("c0t", [128, 16])
    d_T3 = din("T3", [128, 16, 128])
    d_T4 = din("T4", [128, 16, 128])
    d_Tn = din("Tn", [128, 16, 32])
    d_cw = din("cw", [128, 2, 3, NFU])
    d_cb = din("cb", [128, 2, NFU])
    d_ident = din("ident", [128, 128])
    d_cmask = din("cmask", [128, 128])
    d_smask = din("smask", [128, 128])
    d_scanp = din("scanp", [128, 512])
    d_scans = din("scans", [128, 128])
    d_kneg = din("kneg", [128, 13])
    d_seqm = din("seqm", [128, 4])

    o_yp = dout("yp", [1024, D])
    o_ys = dout("ys", [128, D])
    o_Sp = dout("Sp", [16, 128, 128])
    o_Ss = dout("Ss", [4, 16, 128, 128])
    o_kp = dout("kp", [512, D])
    o_vp = dout("vp", [512, D])
    o_ks = dout("ks", [128, D])
    o_vs = dout("vs", [128, D])
    o_co = dout("co", [2, 10, DFF])

    kscr = nc.dram_tensor("kscr", [16, 128, 512], BF16, kind="Internal").ap()
    vscr = nc.dram_tensor("vscr", [512, D], BF16, kind="Internal").ap()
    Bkscr = Buf("kscr")
    Bvscr = Buf("vscr")

    P = Prog(nc)
    es = ExitStack()
    with es:
        for k in P.semkeys():
            P.sems[k] = es.enter_context(nc.semaphore(k))

        def sb(name, shape, dt):
            return T(es.enter_context(nc.sbuf_tensor("s_" + name, shape, dt)), name)

        def ps(name, shape, dt):
            return T(es.enter_context(nc.psum_tensor("p_" + name, shape, dt)), name)

        def bufs(lst):
            return [x.b if hasattr(x, 'b') else x for x in lst]

        def ACT(out, in_, func, R, W, **kw):
            P.op("act", lambda e: e.activation(out=out, in_=in_, func=func, **kw), bufs(R), bufs(W))

        def MM(out, lhsT, rhs, start, stop, R, W):
            P.op("pe", lambda e: e.matmul(out, lhsT, rhs, start=start, stop=stop), bufs(R), bufs(W))

        def TR(out, in_, ident, R, W):
            P.op("pe", lambda e: e.transpose(out, in_, ident), bufs(R), bufs(W))

        def TT(out, in0, in1, op, R, W):
            P.op("dve", lambda e: e.tensor_tensor(out=out, in0=in0, in1=in1, op=op), bufs(R), bufs(W))

        def TS(out, in0, s1, op0, R, W, s2=None, op1=None):
            if op1 is None:
                P.op("dve", lambda e: e.tensor_scalar(out=out, in0=in0, scalar1=s1, scalar2=None, op0=op0), bufs(R), bufs(W))
            else:
                P.op("dve", lambda e: e.tensor_scalar(out=out, in0=in0, scalar1=s1, scalar2=s2, op0=op0, op1=op1), bufs(R), bufs(W))

        def STT(out, in0, scalar, in1, op0, op1, R, W):
            P.op("dve", lambda e: e.scalar_tensor_tensor(out=out, in0=in0, scalar=scalar, in1=in1, op0=op0, op1=op1), bufs(R), bufs(W))

        def CPV(out, in_, R, W):
            P.op("dve", lambda e: e.tensor_copy(out=out, in_=in_), bufs(R), bufs(W))

        def RECIP(out, in_, R, W):
            P.op("dve", lambda e: e.reciprocal(out=out, in_=in_), bufs(R), bufs(W))

        def SCAN(out, d0, d1, R, W):
            P.op("dve", lambda e: e.tensor_tensor_scan(out=out, data0=d0, data1=d1, initial=0.0, op0=ALU.mult, op1=ALU.add), bufs(R), bufs(W))

        def MEMSET(ap, val, W):
            P.op("dve", lambda e: e.memset(ap, val), [], bufs(W))

        def CPA(out, in_, R, W):
            ACT(out, in_, AF.Copy, R, W)

        def DMA(out, in_, R, W, q="sp"):
            P.dma(q, out, in_, bufs(R), bufs(W))

        xT = sb("xT", [128, 16, 896], F32)
        hT = sb("hT", [128, 16, 896], BF16)
        wring = Ring([sb(f"wr{i}", [128, 4096], BF16) for i in range(3)])
        Vg = sb("Vg", [128, 12, 256], BF16)
        g4 = sb("g4", [128, 2, 896], BF16)
        g4b = sb("g4b", [128, 2, 896], BF16)
        g4c = sb("g4c", [128, 2, 896], BF16)
        oacc = sb("oacc", [128, 896], F32)
        tf = Ring([sb(f"tf{i}", [128, 512], F32) for i in range(6)])
        tb = Ring([sb(f"tb{i}", [128, 512], BF16) for i in range(4)])
        us = Ring([sb(f"us{i}", [128, 516], F32) for i in range(2)])
        S = sb("S", [128, 16, 128], F32)
        Sall = sb("Sall", [128, 8, 128], F32)
        tmpall = sb("tmpall", [128, 7, 128], F32)
        cmask4 = sb("cmask4", [128, 4, 128], F32)
        S0f = sb("S0f", [128, 4, 128], F32)
        arena = sb("arena", [128, 10240], BF16)
        identf = sb("identf", [128, 128], F32)
        identb = sb("identb", [128, 128], BF16)
        onesb = sb("onesb", [128, 128], BF16)
        cmask = sb("cmask", [128, 128], F32)
        smask = sb("smask", [128, 128], F32)
        scanp = sb("scanp", [128, 512], F32)
        scans = sb("scans", [128, 128], F32)
        gains = sb("gains", [128, 4, 16], F32)
        gam = sb("gam", [128, 3, 16], F32)
        lbt = sb("lbt", [128, 8, 16], F32)
        oml = sb("oml", [128, 16], F32)
        pvec = sb("pvec", [128, 3], F32)
        c0t = sb("c0t", [128, 16], F32)
        kneg = sb("kneg", [128, 13], F32)
        cbias = sb("cbias", [128, 13, 16], F32)
        seqm = sb("seqm", [128, 4], F32)
        cw = sb("cw", [128, 2, 3, NFU], F32)
        cb = sb("cb", [128, 2, NFU], F32)
        cstT = sb("cstT", [128, NFU, 16], F32)
        ucol = sb("ucol", [128, NFU, 10], F32)
        uH = sb("uH", [128, 2, NFU, 2], F32)
        onec = sb("onec", [128, 1], F32)
        epsc = sb("epsc", [128, 1], F32)
        Et = sb("Et", [128, 16], F32)
        E12t = sb("E12t", [128, 8], F32)
        Est = sb("Est", [128, 4], F32)
        Et2 = sb("Et2", [128, 16], F32)
        Est2 = sb("Est2", [128, 4], F32)
        T3g = sb("T3g", [128, 2, 128], F32)
        T4g = sb("T4g", [128, 2, 128], F32)
        Tng = sb("Tng", [128, 2, 32], F32)
        eT3g = sb("eT3g", [128, 2, 128], F32)
        eT4g = sb("eT4g", [128, 2, 128], F32)
        eTng = sb("eTng", [128, 2, 32], F32)

        pm = Ring([ps(f"pm{i}", [128, 512], F32) for i in range(4)])
        pss_ = Ring([ps(f"ps{i}", [128, 512], F32) for i in range(2)])
        ptr = Ring([ps(f"pt{i}", [128, 1024], BF16) for i in range(2)])

        class AV:
            def __init__(self, off, n, name):
                self.ap = arena.t[:, off:off + n]
                self.b = Buf(name)
                self.off = off

            def __getitem__(self, idx):
                return self.ap[idx]

        def carve(spec):
            off = 0
            d = {}
            for name, n in spec:
                d[name] = AV(off, n, name)
                off += n
            assert off <= 10240, off
            return d

        A_ = carve([("kT", 896), ("KpT", 896), ("qT", 896), ("Q2T", 896), ("gsT", 896), ("S0b", 512),
                    ("A0", 512), ("A1", 512), ("A2", 128), ("Km0", 128), ("Km1", 128), ("Ktall", 896), ("Sball", 896), ("kT2", 896), ("qT2", 896)])
        B_ = carve([("KTg", 3072), ("QTg", 1792), ("kc", 1024), ("vc", 1024), ("KcT", 1024),
                    ("E0a", 128), ("E0b", 128), ("E4a", 128), ("E4b", 128), ("E3a", 128), ("E3b", 128),
                    ("E1a", 128), ("E1b", 128), ("E2a", 128), ("E2b", 128), ("Eca", 160), ("Ecb", 160)])

        for t_, d_ in ((identf, d_ident), (cmask, d_cmask), (smask, d_smask), (scanp, d_scanp), (scans, d_scans),
                       (gains, d_gains), (gam, d_gam), (pvec, d_pvec), (c0t, d_c0), (kneg, d_kneg), (seqm, d_seqm),
                       (cw, d_cw), (cb, d_cb)):
            DMA(t_.t[:], d_, [], [t_])
        DMA(identb.t[:], d_ident, [], [identb], q="pool")
        MEMSET(onesb.t[:], 1.0, [onesb])
        MEMSET(onec.t[:], 1.0, [onec])
        MEMSET(epsc.t[:], EPS, [epsc])
        MEMSET(S.t[:], 0.0, [S])
        for i4 in range(4):
            DMA(cmask4.t[:, i4, :], d_cmask, [], [cmask4])
        MEMSET(uH.t[:], 0.0, [uH])
        g0, g1, g2 = gam.t[:, 0, :], gam.t[:, 1, :], gam.t[:, 2, :]
        L = lambda i: lbt.t[:, i, :]
        TT(L(0), g0, g1, ALU.max, [gam], [lbt])
        TT(L(0), L(0), g2, ALU.max, [gam, lbt], [lbt])
        for i, gi in enumerate((g0, g1, g2)):
            TT(L(1 + i), gi, L(0), ALU.subtract, [gam, lbt], [lbt])
            ACT(L(1 + i), L(1 + i), AF.Exp, [lbt], [lbt])
        TT(L(4), L(1), L(2), ALU.add, [lbt], [lbt])
        TT(L(4), L(4), L(3), ALU.add, [lbt], [lbt])
        RECIP(L(5), L(4), [lbt], [lbt])
        TT(L(6), L(1), L(5), ALU.mult, [lbt], [lbt])
        TS(oml.t[:], L(6), -1.0, ALU.mult, [lbt], [oml], s2=1.0, op1=ALU.add)
        for t in range(13):
            TS(cbias.t[:, t, :], c0t.t[:], kneg.t[:, t:t + 1], ALU.add, [c0t, kneg], [cbias])
        for pc in range(11):
            tt_ = tf.nxt()
            DMA(tt_.t[0:16, :], cst[:, pc * 512:(pc + 1) * 512], [], [tt_])
            pp = pss_.nxt()
            for i in range(4):
                TR(pp.t[:, i * 16:(i + 1) * 16], tt_.t[0:16, i * 128:(i + 1) * 128], identf.t[0:16, 0:16], [tt_, identf], [pp])
            CPV(cstT.t[:, pc * 4:(pc + 1) * 4, :], pp.t[:, 0:64].rearrange("p (a b) -> p a b", b=16), [pp], [cstT])

        def load_unit(src):
            w = wring.nxt()
            v = w.t[:, 0:4096].rearrange("p (k n) -> p k n", n=256)
            DMA(v, src.rearrange("(kc p) n -> p kc n", p=128), [], [w], q="pool")
            return w, v

        def load_rows(src):
            w = wring.nxt()
            v = w.t[:, 0:4096].rearrange("p (k n) -> p k n", n=2048)
            for kc_ in range(2):
                P.dma("pool", v[:, kc_, :], src[kc_ * 128:(kc_ + 1) * 128, :], [], [w.b], max_dma_last_dim=4096)
            return w, v

        def load_x(blk):
            tiles = blk["tiles"] + ([32] if blk["sample"] else [])
            for j, gt in enumerate(tiles):
                for q4 in range(4):
                    xt_ = tf.nxt()
                    DMA(xt_.t[:], xs[gt * 128:(gt + 1) * 128, q4 * 512:(q4 + 1) * 512], [], [xt_])
                    pp = pm.nxt()
                    for i in range(4):
                        TR(pp.t[:, i * 128:(i + 1) * 128], xt_.t[:, i * 128:(i + 1) * 128], identf.t[:], [xt_, identf], [pp])
                    CPA(xT.t[:, q4 * 4:(q4 + 1) * 4, j * 128:(j + 1) * 128], pp.t[:].rearrange("p (a b) -> p a b", b=128), [pp], [xT])

        def store_x(blk):
            tiles = blk["tiles"] + ([32] if blk["sample"] else [])
            for j, gt in enumerate(tiles):
                if gt < 24:
                    continue
                dst = o_ys if gt == 32 else o_yp[(gt - 24) * 128:(gt - 23) * 128, :]
                for q4 in range(4):
                    pp = pm.nxt()
                    for i in range(4):
                        TR(pp.t[:, i * 128:(i + 1) * 128], xT.t[:, q4 * 4 + i, j * 128:(j + 1) * 128], identf.t[:], [xT, identf], [pp])
                    st = tf.nxt()
                    CPA(st.t[:], pp.t[:], [pp], [st])
                    DMA(dst[:, q4 * 512:(q4 + 1) * 512], st.t[:], [st], [])

        def rstd_from(p, n, scale):
            rs = tf.nxt()
            ACT(rs.t[:, :n], p.t[:, :n], AF.Sqrt, [p, epsc], [rs], scale=scale, bias=epsc.t[:])
            rr = tf.nxt()
            RECIP(rr.t[:, :n], rs.t[:, :n], [rs], [rr])
            return rr

        def norm(gi, a, b):
            for (c0, n, _) in chunks(a, b):
                pssq = pm.nxt()
                for kc in range(16):
                    sq = tb.nxt()
                    ACT(sq.t[:, :n], xT.t[:, kc, c0:c0 + n], AF.Square, [xT], [sq])
                    MM(pssq.t[:, :n], onesb.t[:], sq.t[:, :n], kc == 0, kc == 15, [onesb, sq], [pssq])
                rr = rstd_from(pssq, n, 1.0 / D)
                for kc in range(16):
                    STT(hT.t[:, kc, c0:c0 + n], xT.t[:, kc, c0:c0 + n], gains.t[:, gi, kc:kc + 1], rr.t[:, :n],
                        ALU.mult, ALU.mult, [xT, gains, rr], [hT])

        def proj_T(Wt, Wv, hh, c0, n):
            p = pm.nxt()
            for kc in range(16):
                MM(p.t[:, :n], Wv[:, kc, hh * 128:(hh + 1) * 128], hT.t[:, kc, c0:c0 + n], kc == 0, kc == 15, [Wt, hT], [p])
            return p

        def wo_partial(Wt, Wv, a, b):
            for cu in range(16):
                for (c0, n, _) in chunks(a, b):
                    py = pm.nxt()
                    for hh in range(2):
                        MM(py.t[:, :n], Wv[:, hh, cu * 128:(cu + 1) * 128], g4.t[:, hh, c0:c0 + n], hh == 0, hh == 1, [Wt, g4], [py])
                    TT(xT.t[:, cu, c0:c0 + n], xT.t[:, cu, c0:c0 + n], py.t[:, :n], ALU.add, [xT, py], [xT])

        def stage_hgrn(blk):
            nt = len(blk["tiles"])
            samp = blk["sample"]
            full = blk["full"]
            scol = nt * 128 if samp else None
            ntt = nt + (1 if samp else 0)
            Tn_ = ntt * 128
            kT, KpT, qT, Q2T, gsT, S0b = A_["kT"], A_["KpT"], A_["qT"], A_["Q2T"], A_["gsT"], A_["S0b"]
            kTs = [A_["kT"], A_["kT2"]]
            qTs = [A_["qT"], A_["qT2"]]
            Ets = [Et, Et2]
            Ests = [Est, Est2]
            Aring = Ring([A_["A0"], A_["A1"]])
            Ktall, Sball = A_["Ktall"], A_["Sball"]
            Kmr = Ring([A_["Km0"], A_["Km1"]])
            for a_ in (A_["A0"], A_["A1"]):
                MEMSET(a_.ap[:, :], 0.0, [a_])
            for g in range(8):
                Wi, Wiv = load_unit(w_in[:, 4096 + 256 * g:4096 + 256 * (g + 1)])
                Wf, Wfv = load_unit(w_in[:, 2048 + 256 * g:2048 + 256 * (g + 1)])
                for j in range(ntt):
                    pv = pm.nxt()
                    for kc in range(16):
                        MM(pv.t[:, :256], hT.t[:, kc, j * 128:(j + 1) * 128], Wiv[:, kc, :], kc == 0, kc == 15, [hT, Wi], [pv])
                    CPA(Vg.t[:, j, :], pv.t[:, :256], [pv], [Vg])
                if full:
                    Wq, Wqv = load_unit(w_in[:, 256 * g:256 * (g + 1)])
                    WgT, Wgv_ = load_unit(w_in[:, 6144 + 256 * g:6144 + 256 * (g + 1)])
                def p1(hh):
                    h = 2 * g + hh
                    kT, qT, Et, Est = kTs[hh], qTs[hh], Ets[hh], Ests[hh]
                    allch = chunks(0, Tn_, scol)
                    for chs in ([c for c in allch if not c[2]], [c for c in allch if c[2]]):
                        if not chs:
                            continue
                        st = []
                        for (c0, n, is_s) in chs:
                            pf = proj_T(Wf, Wfv, hh, c0, n)
                            t1 = tf.nxt()
                            ACT(t1.t[:, :n], pf.t[:, :n], AF.Sigmoid, [pf], [t1], scale=-1.0)
                            st.append((c0, n, is_s, t1, tf.nxt(), tf.nxt()))
                        for (c0, n, is_s, t1, t2, t3) in st:
                            TS(t1.t[:, :n], t1.t[:, :n], oml.t[:, h:h + 1], ALU.mult, [t1, oml], [t1])
                        for (c0, n, is_s, t1, t2, t3) in st:
                            ACT(t2.t[:, :n], t1.t[:, :n], AF.Ln, [t1, onec], [t2], scale=-1.0, bias=onec.t[:])
                        for (c0, n, is_s, t1, t2, t3) in st:
                            mk = scans.t[:, 0:n] if is_s else scanp.t[:, 0:n]
                            SCAN(t3.t[:, :n], mk, t2.t[:, :n], [t2, scans, scanp], [t3])
                        for (c0, n, is_s, t1, t2, t3) in st:
                            ACT(t2.t[:, :n], t3.t[:, :n], AF.Exp, [t3], [t2], scale=-1.0)
                        for (c0, n, is_s, t1, t2, t3) in st:
                            TT(kT.ap[:, c0:c0 + n], t1.t[:, :n], t2.t[:, :n], ALU.mult, [t1, t2], [kT])
                        for (c0, n, is_s, t1, t2, t3) in st:
                            if is_s:
                                ACT(Est.t[:, 0:4], t3.t[:, 0:128].rearrange("p (a b) -> p a b", b=32)[:, :, 31], AF.Exp, [t3], [Est])
                            else:
                                ACT(Et.t[:, c0 // 64:(c0 + n) // 64], t3.t[:, 0:n].rearrange("p (a b) -> p a b", b=64)[:, :, 63],
                                    AF.Exp, [t3], [Et])
                        if full:
                            for (c0, n, is_s, t1, t2, t3) in st:
                                pq = proj_T(Wq, Wqv, hh, c0, n)
                                ACT(t2.t[:, :n], t3.t[:, :n], AF.Exp, [t3], [t2])
                                TT(qT.ap[:, c0:c0 + n], pq.t[:, :n], t2.t[:, :n], ALU.mult, [pq, t2], [qT])

                def p2(hh):
                    h = 2 * g + hh
                    kT, qT, Et, Est = kTs[hh], qTs[hh], Ets[hh], Ests[hh]
                    if nt > 0:
                        v3 = lambda av, lo, hi: av.ap[:, 0:nt * 128].rearrange("p (a b) -> p a b", b=128)[:, :, lo:hi]
                        CPA(v3(KpT, 64, 128), v3(kT, 64, 128), [kT], [KpT])
                        if full:
                            CPA(v3(Q2T, 0, 64), v3(qT, 0, 64), [qT], [Q2T])
                        for j in range(nt):
                            e1 = Et.t[:, 2 * j:2 * j + 1]
                            TS(KpT.ap[:, j * 128:j * 128 + 64], kT.ap[:, j * 128:j * 128 + 64], e1, ALU.mult, [kT, Et], [KpT])
                            if full:
                                TS(Q2T.ap[:, j * 128 + 64:(j + 1) * 128], qT.ap[:, j * 128 + 64:(j + 1) * 128], e1, ALU.mult, [qT, Et], [Q2T])
                        ev = Et.t[:, 0:2 * nt].rearrange("p (a b) -> p a b", b=2)
                        TT(E12t.t[:, 0:nt], ev[:, :, 0], ev[:, :, 1], ALU.mult, [Et], [E12t])
                    if nt > 0:
                        hs_ = slice(hh * 128, (hh + 1) * 128)
                        pt = ptr.nxt()
                        for j in range(nt):
                            TR(pt.t[:, j * 128:(j + 1) * 128], KpT.ap[:, j * 128:(j + 1) * 128], identb.t[:], [KpT, identb], [pt])
                        CPA(Ktall.ap[:, 0:nt * 128], pt.t[:, 0:nt * 128], [pt], [Ktall])
                        for j0 in range(0, nt, 4):
                            jn = min(4, nt - j0)
                            px = pm.nxt()
                            for jj in range(jn):
                                j = j0 + jj
                                MM(px.t[:, jj * 128:(jj + 1) * 128], Ktall.ap[:, j * 128:(j + 1) * 128], Vg.t[:, j, hs_], True, True, [Ktall, Vg], [px])
                            for jj in range(jn):
                                j = j0 + jj
                                TS(tmpall.t[:, j, :], px.t[:, jj * 128:(jj + 1) * 128], Et.t[:, 2 * j + 1:2 * j + 2], ALU.mult, [px, Et], [tmpall])
                        CPV(Sall.t[:, 0, :], S.t[:, h, :], [S], [Sall])
                        for j in range(nt):
                            STT(Sall.t[:, j + 1, :], Sall.t[:, j, :], E12t.t[:, j:j + 1], tmpall.t[:, j, :], ALU.mult, ALU.add, [Sall, E12t, tmpall], [Sall])
                        CPV(S.t[:, h, :], Sall.t[:, nt, :], [Sall], [S])
                        if full:
                            CPA(Sball.ap[:, 0:nt * 128], Sall.t[:, 0:nt, :].rearrange("p a b -> p (a b)"), [Sall], [Sball])
                            for j0 in range(0, nt, 4):
                                jn = min(4, nt - j0)
                                psc = pm.nxt()
                                for jj in range(jn):
                                    j = j0 + jj
                                    MM(psc.t[0:64, jj * 128:jj * 128 + 64], kT.ap[:, j * 128:j * 128 + 64], qT.ap[:, j * 128:j * 128 + 64], True, True, [kT, qT], [psc])
                                    MM(psc.t[:, jj * 128 + 64:(jj + 1) * 128], KpT.ap[:, j * 128:(j + 1) * 128], qT.ap[:, j * 128 + 64:(j + 1) * 128], True, True, [KpT, qT], [psc])
                                A = Aring.nxt()
                                a3 = A.ap[:, 0:jn * 128].rearrange("p (a b) -> p a b", b=128)
                                p3 = psc.t[:, 0:jn * 128].rearrange("p (a b) -> p a b", b=128)
                                TT(a3[0:64, :, 0:64], p3[0:64, :, 0:64], cmask4.t[0:64, 0:jn, 0:64], ALU.mult, [psc, cmask4], [A])
                                TT(a3[:, :, 64:128], p3[:, :, 64:128], cmask4.t[:, 0:jn, 64:128], ALU.mult, [psc, cmask4], [A])
                                po = pm.nxt()
                                for jj in range(jn):
                                    j = j0 + jj
                                    MM(po.t[:, jj * 128:(jj + 1) * 128], Vg.t[:, j, hs_], A.ap[:, jj * 128:(jj + 1) * 128], True, False, [Vg, A], [po])
                                    MM(po.t[:, jj * 128:(jj + 1) * 128], Sball.ap[:, j * 128:(j + 1) * 128], Q2T.ap[:, j * 128:(j + 1) * 128], False, True, [Sball, Q2T], [po])
                                CPA(oacc.t[:, j0 * 128:(j0 + jn) * 128], po.t[:, 0:jn * 128], [po], [oacc])
                    if samp:
                        cs = slice(scol, scol + 128)
                        vj = Vg.t[:, nt, hh * 128:(hh + 1) * 128]
                        DMA(S0f.t[:], s0[:, h].rearrange("s k v -> k s v"), [], [S0f])
                        CPA(S0b.ap[:, :], S0f.t[:].rearrange("p a b -> p (a b)"), [S0f], [S0b])
                        psc = pss_.nxt()
                        MM(psc.t[:, 0:128], kT.ap[:, cs], qT.ap[:, cs], True, True, [kT, qT], [psc])
                        A2 = A_["A2"]
                        TT(A2.ap[:, :], psc.t[:, 0:128], smask.t[:], ALU.mult, [psc, smask], [A2])
                        po = pss_.nxt()
                        for s in range(4):
                            MM(po.t[:, 32 * s:32 * s + 32], vj, A2.ap[:, 32 * s:32 * s + 32], True, False, [Vg, A2], [po])
                            MM(po.t[:, 32 * s:32 * s + 32], S0b.ap[:, 128 * s:128 * s + 128], qT.ap[:, scol + 32 * s:scol + 32 * s + 32],
                               False, True, [S0b, qT], [po])
                        CPA(oacc.t[:, cs], po.t[:, 0:128], [po], [oacc])
                        pt = ptr.nxt()
                        TR(pt.t[:, 0:128], kT.ap[:, cs], identb.t[:], [kT, identb], [pt])
                        for s in range(4):
                            Km = Kmr.nxt()
                            TS(Km.ap[:, :], pt.t[:, 0:128], seqm.t[:, s:s + 1], ALU.mult, [pt, seqm], [Km])
                            px = pss_.nxt()
                            MM(px.t[:, 0:128], Km.ap[:, :], vj, True, True, [Km, Vg], [px])
                            t1 = tf.nxt()
                            TS(t1.t[:, :128], S0f.t[:, s, :], Est.t[:, s:s + 1], ALU.mult, [S0f, Est], [t1])
                            so = tf.nxt()
                            STT(so.t[:, :128], px.t[:, 0:128], Est.t[:, s:s + 1], t1.t[:, :128], ALU.mult, ALU.add, [px, Est, t1], [so])
                            DMA(o_Ss[s, h], so.t[:, :128], [so], [])
                    if full:
                        for (c0, n, _) in chunks(0, Tn_):
                            pg = proj_T(WgT, Wgv_, hh, c0, n)
                            ACT(gsT.ap[:, c0:c0 + n], pg.t[:, :n], AF.Silu, [pg], [gsT])
                            osq = tb.nxt()
                            ACT(osq.t[:, :n], oacc.t[:, c0:c0 + n], AF.Square, [oacc], [osq])
                            pq_ = pm.nxt()
                            MM(pq_.t[:, :n], onesb.t[:], osq.t[:, :n], True, True, [onesb, osq], [pq_])
                            rr = rstd_from(pq_, n, 1.0 / 128)
                            on = tf.nxt()
                            STT(on.t[:, :n], oacc.t[:, c0:c0 + n], pvec.t[:, 0:1], rr.t[:, :n], ALU.mult, ALU.mult, [oacc, pvec, rr], [on])
                            TT(g4.t[:, hh, c0:c0 + n], on.t[:, :n], gsT.ap[:, c0:c0 + n], ALU.mult, [on, gsT], [g4])
                p1(0)
                p1(1)
                p2(0)
                p2(1)
                if full:
                    Wo, Wov = load_rows(w_oa[256 * g:256 * (g + 1), :])
                    wo_partial(Wo, Wov, 0, Tn_)
            if blk["last"]:
                DMA(o_Sp.rearrange("h k v -> k h v"), S.t[:], [S], [])

        def stage_ffn(l, blk, a):
            nt = len(blk["tiles"])
            samp = blk["sample"]
            scol = nt * 128 if samp else None
            Tn_ = (nt + (1 if samp else 0)) * 128
            norm(1 + 2 * l, a, Tn_)
            chs = chunks(a, Tn_, scol)
            first = blk["first"]
            G3 = [g4, g4b, g4c]
            pst = {"pend": None}

            def ffn_tail(cc, c2, pg, fu, c0, n, G):
                ACT(cc.t[:, :n], c2.t[:, :n], AF.Silu, [c2], [cc])
                TT(G.t[:, fu, c0:c0 + n], cc.t[:, :n], pg.t[:, :n], ALU.mult, [cc, pg], [G])

            def flush():
                if pst["pend"] is not None:
                    ffn_tail(*pst["pend"])
                    pst["pend"] = None

            def down(p):
                for half in range(2):
                    w = wring.nxt()
                    wv = w.t[:, 0:4096].rearrange("p (k n) -> p k n", n=1024)
                    for k_ in range(4):
                        P.dma("pool", wv[:, k_, :], f_dn[l][512 * p + 128 * k_:512 * p + 128 * (k_ + 1), 1024 * half:1024 * (half + 1)],
                              [], [w.b], max_dma_last_dim=4096)
                    for cu in range(8):
                        cuu = 8 * half + cu
                        for (c0, n, _) in chunks(a, Tn_):
                            py = pm.nxt()
                            for k4 in range(4):
                                G_ = G3[(2 * p + k4 // 2) % 3]
                                MM(py.t[:, :n], wv[:, k4, cu * 128:(cu + 1) * 128], G_.t[:, k4 % 2, c0:c0 + n], k4 == 0, k4 == 3, [w, G_], [py])
                            TT(xT.t[:, cuu, c0:c0 + n], xT.t[:, cuu, c0:c0 + n], py.t[:, :n], ALU.add, [xT, py], [xT])

            def up(sl):
                Wu, Wuv = load_unit(f_in[l][:, 256 * sl:256 * (sl + 1)])
                Wg, Wgv = load_unit(f_in[l][:, DFF + 256 * sl:DFF + 256 * (sl + 1)])
                for fu in range(2):
                    U = 2 * sl + fu
                    prev = None
                    for (c0, n, is_s) in chs:
                        pu = proj_T(Wu, Wuv, fu, c0, n)
                        pg = proj_T(Wg, Wgv, fu, c0, n)
                        u_ = us.nxt()
                        w0 = cw.t[:, l, 0, U:U + 1]
                        w1 = cw.t[:, l, 1, U:U + 1]
                        w2 = cw.t[:, l, 2, U:U + 1]
                        cc = tf.nxt()
                        ACT(cc.t[:, :n], pu.t[:, :n], AF.Identity, [pu, cw, cb], [cc], scale=w2, bias=cb.t[:, l, U:U + 1])
                        c1 = tf.nxt()
                        c2 = tf.nxt()
                        if is_s:
                            u4 = u_.t[:, 0:136].rearrange("p (s c) -> p s c", c=34)
                            CPV(u4[:, :, 0:2], cstT.t[:, U, 8 * l:8 * l + 8].rearrange("p (s c) -> p s c", c=2), [cstT], [u_])
                            CPA(u4[:, :, 2:34], pu.t[:, 0:128].rearrange("p (s c) -> p s c", c=32), [pu], [u_])
                            v3 = lambda t_: t_.t[:, 0:128].rearrange("p (s c) -> p s c", c=32)
                            STT(v3(c1), u4[:, :, 1:33], w1, v3(cc), ALU.mult, ALU.add, [u_, cw, cc], [c1])
                            STT(v3(c2), u4[:, :, 0:32], w0, v3(c1), ALU.mult, ALU.add, [u_, cw, c1], [c2])
                            CPV(ucol.t[:, U, 2:10].rearrange("p (s c) -> p s c", c=2), u4[:, :, 32:34], [u_], [ucol])
                        else:
                            if prev is not None:
                                CPV(u_.t[:, 0:2], prev[0].t[:, prev[1]:prev[1] + 2], [prev[0]], [u_])
                            elif first:
                                MEMSET(u_.t[:, 0:2], 0.0, [u_])
                            else:
                                CPV(u_.t[:, 0:2], uH.t[:, l, U, :], [uH], [u_])
                            CPA(u_.t[:, 2:2 + n], pu.t[:, :n], [pu], [u_])
                            STT(c1.t[:, :n], u_.t[:, 1:1 + n], w1, cc.t[:, :n], ALU.mult, ALU.add, [u_, cw, cc], [c1])
                            STT(c2.t[:, :n], u_.t[:, 0:n], w0, c1.t[:, :n], ALU.mult, ALU.add, [u_, cw, c1], [c2])
                            prev = (u_, n)
                            last_prompt = (c0 + n == nt * 128)
                            if last_prompt:
                                if blk["last"]:
                                    CPV(ucol.t[:, U, 0:2], u_.t[:, n:n + 2], [u_], [ucol])
                                else:
                                    CPV(uH.t[:, l, U, :], u_.t[:, n:n + 2], [u_], [uH])
                        flush()
                        pst["pend"] = (cc, c2, pg, fu, c0, n, G3[sl % 3])
                flush()

            up(0)
            up(1)
            for p in range(11):
                if 2 * p + 2 < 22:
                    up(2 * p + 2)
                else:
                    flush()
                down(p)
                if 2 * p + 3 < 22:
                    up(2 * p + 3)
            if blk["last"]:
                for q4 in range(11):
                    pp = pm.nxt()
                    for i in range(4):
                        TR(pp.t[0:10, i * 128:(i + 1) * 128], ucol.t[:, q4 * 4 + i, :], identf.t[:], [ucol, identf], [pp])
                    st = tf.nxt()
                    CPA(st.t[0:10, :], pp.t[0:10, :], [pp], [st])
                    DMA(o_co[l][:, q4 * 512:(q4 + 1) * 512], st.t[0:10, :], [st], [])

        import os
        KSUB = int(os.environ.get("KSUB", "99"))
        KSUB2 = int(os.environ.get("KSUB2", "99"))

        def stage_attn(blk):
            nt = len(blk["tiles"])
            samp = blk["sample"]
            scol = nt * 128 if samp else None
            ntt = nt + (1 if samp else 0)
            Tn_ = ntt * 128
            nh = blk["nh"]
            qj0 = blk["qj0"]
            qa = qj0 * 128
            nkt = nh + nt
            gt0 = blk["tiles"][0]
            KTg, QTg, kcb, vcb, KcT = B_["KTg"], B_["QTg"], B_["kc"], B_["vc"], B_["KcT"]
            KT3 = KTg.ap[:, :].rearrange("p (h c) -> p h c", h=2)
            QT3 = QTg.ap[:, :].rearrange("p (h c) -> p h c", h=2)
            E0r = Ring([B_["E0a"], B_["E0b"]])
            E4r = Ring([B_["E4a"], B_["E4b"]])
            E3r = Ring([B_["E3a"], B_["E3b"]])
            E1r = Ring([B_["E1a"], B_["E1b"]])
            E2r = Ring([B_["E2a"], B_["E2b"]])
            Ecr = Ring([B_["Eca"], B_["Ecb"]])
            for e_ in ("E0a", "E0b", "E4a", "E4b"):
                MEMSET(B_[e_].ap[:, :], 0.0, [B_[e_]])

            def normalize(p, n, col):
                sq = tb.nxt()
                ACT(sq.t[:, :n], p.t[:, :n], AF.Square, [p], [sq])
                p2 = pm.nxt()
                MM(p2.t[:, :n], onesb.t[:], sq.t[:, :n], True, True, [onesb, sq], [p2])
                rr = rstd_from(p2, n, 1.0 / 128)
                kn = tf.nxt()
                STT(kn.t[:, :n], p.t[:, :n], pvec.t[:, col:col + 1], rr.t[:, :n], ALU.mult, ALU.mult, [p, pvec, rr], [kn])
                return kn

            def out_tile_row(gt):
                if gt == 32:
                    return ("s", 0)
                if gt >= 28:
                    return ("p", (gt - 28) * 128)
                return None

            tiles = blk["tiles"] + ([32] if samp else [])
            def attn_inner(g):
                for hh in range(2):
                    h = 2 * g + hh
                    hs = slice(hh * 128, (hh + 1) * 128)
                    for j in range(qj0, nt):
                        qc = QT3[:, hh, j * 128:(j + 1) * 128]
                        ks0 = nh + j - 4
                        psA = pm.nxt()
                        psB = pss_.nxt()
                        for kt in range(5):
                            dst = psA.t[:, kt * 128:(kt + 1) * 128] if kt < 4 else psB.t[:, 0:128]
                            MM(dst, KT3[:, hh, (ks0 + kt) * 128:(ks0 + kt + 1) * 128], qc, True, True, [KTg, QTg], [psA if kt < 4 else psB])
                        if KSUB2 <= 1:
                            continue
                        kti = [gt0 - nh + ks0 + kt - 19 for kt in range(5)]
                        E0 = E0r.nxt()
                        ACT(E0.ap[:, 0:64], psA.t[:, 0:64], AF.Exp, [psA, cbias], [E0], scale=SC, bias=cbias.t[:, kti[0], h:h + 1])
                        ACT(E0.ap[64:128, 64:128], psA.t[64:128, 64:128], AF.Exp, [psA, cbias], [E0], scale=SC, bias=cbias.t[64:128, kti[0], h:h + 1])
                        E1 = E1r.nxt()
                        ACT(E1.ap[:, :], psA.t[:, 128:256], AF.Exp, [psA, cbias], [E1], scale=SC, bias=cbias.t[:, kti[1], h:h + 1])
                        E2 = E2r.nxt()
                        ACT(E2.ap[:, :], psA.t[:, 256:384], AF.Exp, [psA, cbias], [E2], scale=SC, bias=cbias.t[:, kti[2], h:h + 1])
                        t3 = tf.nxt()
                        ACT(t3.t[:, :128], psA.t[:, 384:512], AF.Exp, [psA, kneg], [t3], scale=SC, bias=kneg.t[:, kti[3]:kti[3] + 1])
                        E3 = E3r.nxt()
                        TT(E3.ap[:, :], t3.t[:, :128], eT3g.t[:, hh, :], ALU.mult, [t3, eT3g], [E3])
                        t4 = tf.nxt()
                        ACT(t4.t[0:64, :128], psB.t[0:64, 0:128], AF.Exp, [psB, kneg], [t4], scale=SC, bias=kneg.t[0:64, kti[4]:kti[4] + 1])
                        ACT(t4.t[64:128, 64:128], psB.t[64:128, 64:128], AF.Exp, [psB, kneg], [t4], scale=SC, bias=kneg.t[64:128, kti[4]:kti[4] + 1])
                        E4 = E4r.nxt()
                        TT(E4.ap[0:64, :], t4.t[0:64, :128], eT4g.t[0:64, hh, :], ALU.mult, [t4, eT4g], [E4])
                        TT(E4.ap[64:128, 64:128], t4.t[64:128, 64:128], eT4g.t[64:128, hh, 64:128], ALU.mult, [t4, eT4g], [E4])
                        Es = [E0, E1, E2, E3, E4]
                        if KSUB2 <= 2:
                            continue
                        pden = pss_.nxt()
                        KSUB3 = int(os.environ.get("KSUB3", "3"))
                        for kt in range(5):
                            if KSUB3 & 1:
                                MM(pden.t[:, 0:128], onesb.t[:], Es[kt].ap[:, :], kt == 0, kt == 4, [onesb, Es[kt]], [pden])
                            if KSUB3 & 16:
                                MM(pden.t[:, 0:128], onesb.t[:], g4.t[:, 0, kt * 128:(kt + 1) * 128], kt == 0, kt == 4, [onesb, g4], [pden])
                            if KSUB3 & 32:
                                MM(pden.t[:, 0:128], onesb.t[:], Es[kt].ap[:, :], kt == 0, kt == 4, [onesb], [pden])
                            if KSUB3 & 4:
                                MM(pden.t[:, 0:128], onesb.t[:], Es[1].ap[:, :], kt == 0, kt == 4, [onesb, Es[1]], [pden])
                            if KSUB3 & 8:
                                MM(pden.t[:, 0:128], onesb.t[:], Es[kt].ap[:, :], True, True, [onesb, Es[kt]], [pden])
                        po = pss_.nxt()
                        for kt in range(5):
                            if KSUB3 & 2:
                                MM(po.t[:, 0:128], Vg.t[:, ks0 + kt, hs], Es[kt].ap[:, :], kt == 0, kt == 4, [Vg, Es[kt]], [po])
                        if KSUB2 <= 3:
                            continue
                        rd = tf.nxt()
                        TS(rd.t[:, :128], pden.t[:, 0:128], 1e-30, ALU.add, [pden], [rd])
                        rr = tf.nxt()
                        RECIP(rr.t[:, :128], rd.t[:, :128], [rd], [rr])
                        TT(g4.t[:, hh, j * 128:(j + 1) * 128], po.t[:, 0:128], rr.t[:, :128], ALU.mult, [po, rr], [g4])
                if samp:
                    kc3 = kcb.ap[:, :].rearrange("p (t c) -> p t c", c=256)
                    vc3 = vcb.ap[:, :].rearrange("p (t c) -> p t c", c=256)
                    for s in range(4):
                        DMA(kc3, ck[s][:, 256 * g:256 * (g + 1)].rearrange("(t p) c -> p t c", p=128), [], [kcb], q="pool")
                        DMA(vc3, cv[s][:, 256 * g:256 * (g + 1)].rearrange("(t p) c -> p t c", p=128), [], [vcb], q="pool")
                        pt = ptr.nxt()
                        for hh in range(2):
                            for kt in range(4):
                                i = hh * 4 + kt
                                TR(pt.t[:, i * 128:(i + 1) * 128], kc3[:, kt, hh * 128:(hh + 1) * 128], identb.t[:], [kcb, identb], [pt])
                        CPA(KcT.ap[:, :], pt.t[:, :], [pt], [KcT])
                        for hh in range(2):
                            h = 2 * g + hh
                            hs = slice(hh * 128, (hh + 1) * 128)
                            qc = QT3[:, hh, scol + 32 * s:scol + 32 * s + 32]
                            psc = pss_.nxt()
                            for kt in range(4):
                                i = hh * 4 + kt
                                MM(psc.t[:, kt * 32:(kt + 1) * 32], KcT.ap[:, i * 128:(i + 1) * 128], qc, True, True, [KcT, QTg], [psc])
                            MM(psc.t[:, 128:160], KT3[:, hh, nkt * 128:(nkt + 1) * 128], qc, True, True, [KTg, QTg], [psc])
                            Ec = Ecr.nxt()
                            ACT(Ec.ap[:, 0:96], psc.t[:, 0:96], AF.Exp, [psc, c0t], [Ec], scale=SC, bias=c0t.t[:, h:h + 1])
                            t3 = tf.nxt()
                            ACT(t3.t[:, :32], psc.t[:, 96:128], AF.Exp, [psc], [t3], scale=SC)
                            TT(Ec.ap[:, 96:128], t3.t[:, :32], eT3g.t[:, hh, 0:32], ALU.mult, [t3, eT3g], [Ec])
                            tn = tf.nxt()
                            ACT(tn.t[:, :32], psc.t[:, 128:160], AF.Exp, [psc], [tn], scale=SC)
                            STT(Ec.ap[:, 128:160], tn.t[:, :32], seqm.t[:, s:s + 1], eTng.t[:, hh, :], ALU.mult, ALU.mult, [tn, seqm, eTng], [Ec])
                            pden = pss_.nxt()
                            po = pss_.nxt()
                            for kt in range(5):
                                ec = Ec.ap[:, 128:160] if kt == 4 else Ec.ap[:, kt * 32:(kt + 1) * 32]
                                MM(pden.t[:, 0:32], onesb.t[:], ec, kt == 0, kt == 4, [onesb, Ec], [pden])
                            for kt in range(5):
                                ec = Ec.ap[:, 128:160] if kt == 4 else Ec.ap[:, kt * 32:(kt + 1) * 32]
                                lv = Vg.t[:, nkt, hs] if kt == 4 else vc3[:, kt, hs]
                                MM(po.t[:, 0:32], lv, ec, kt == 0, kt == 4, [Vg, vcb, Ec], [po])
                            rd = tf.nxt()
                            TS(rd.t[:, :32], pden.t[:, 0:32], 1e-30, ALU.add, [pden], [rd])
                            rr = tf.nxt()
                            RECIP(rr.t[:, :32], rd.t[:, :32], [rd], [rr])
                            TT(g4.t[:, hh, scol + 32 * s:scol + 32 * s + 32], po.t[:, 0:32], rr.t[:, :32], ALU.mult, [po, rr], [g4])
                if KSUB <= 3:
                    return
                Wo, Wov = load_rows(w_ob[256 * g:256 * (g + 1), :])
                wo_partial(Wo, Wov, qa, Tn_)


            KONLY = int(os.environ.get("KONLY", "0"))
            for g in range(8):
                if KONLY:
                    break
                Wv, Wvv = load_unit(w_qkv[:, 4096 + 256 * g:4096 + 256 * (g + 1)])
                Wk, Wkv = load_unit(w_qkv[:, 2048 + 256 * g:2048 + 256 * (g + 1)])
                DMA(T3g.t[:], d_T3[:, 2 * g:2 * g + 2, :], [], [T3g])
                DMA(T4g.t[:], d_T4[:, 2 * g:2 * g + 2, :], [], [T4g])
                DMA(Tng.t[:], d_Tn[:, 2 * g:2 * g + 2, :], [], [Tng])
                ACT(eT3g.t[:], T3g.t[:], AF.Exp, [T3g], [eT3g])
                ACT(eT4g.t[:], T4g.t[:], AF.Exp, [T4g], [eT4g])
                ACT(eTng.t[:], Tng.t[:], AF.Exp, [Tng], [eTng])
                if nh:
                    DMA(Vg.t[:, 0:4, :], vscr[:, 256 * g:256 * (g + 1)].rearrange("(t p) c -> p t c", p=128), [Bvscr], [Vg])
                    for hh in range(2):
                        DMA(KT3[:, hh, 0:512], kscr[2 * g + hh], [Bkscr], [KTg])
                for j, gt in enumerate(tiles):
                    pv = pm.nxt()
                    for kc in range(16):
                        MM(pv.t[:, :256], hT.t[:, kc, j * 128:(j + 1) * 128], Wvv[:, kc, :], kc == 0, kc == 15, [hT, Wv], [pv])
                    CPA(Vg.t[:, nh + j, :], pv.t[:, :256], [pv], [Vg])
                    orow = out_tile_row(gt)
                    if orow is not None:
                        st = tf.nxt()
                        CPA(st.t[:, :256], pv.t[:, :256], [pv], [st])
                        dst = o_vs if orow[0] == "s" else o_vp[orow[1]:orow[1] + 128, :]
                        DMA(dst[:, 256 * g:256 * (g + 1)], st.t[:, :256], [st], [])
                Wq, Wqv = load_unit(w_qkv[:, 256 * g:256 * (g + 1)])
                for hh in range(2):
                    h = 2 * g + hh
                    for (c0, n, _) in chunks(0, Tn_):
                        pk = proj_T(Wk, Wkv, hh, c0, n)
                        kn = normalize(pk, n, 2)
                        CPA(KT3[:, hh, nh * 128 + c0:nh * 128 + c0 + n], kn.t[:, :n], [kn], [KTg])
                        for jj in range(n // 128):
                            gt = tiles[c0 // 128 + jj]
                            orow = out_tile_row(gt)
                            if orow is None:
                                continue
                            pso = pss_.nxt()
                            TR(pso.t[:, 0:128], kn.t[:, jj * 128:(jj + 1) * 128], identf.t[:], [kn, identf], [pso])
                            st = tf.nxt()
                            CPV(st.t[:, :128], pso.t[:, 0:128], [pso], [st])
                            dst = o_ks if orow[0] == "s" else o_kp[orow[1]:orow[1] + 128, :]
                            DMA(dst[:, h * 128:(h + 1) * 128], st.t[:, :128], [st], [])
                    for (c0, n, _) in chunks(qa, Tn_):
                        pq = proj_T(Wq, Wqv, hh, c0, n)
                        qn = normalize(pq, n, 1)
                        CPA(QT3[:, hh, c0:c0 + n], qn.t[:, :n], [qn], [QTg])
                if KSUB <= 1:
                    continue
                if nh == 0:
                    DMA(vscr[:, 256 * g:256 * (g + 1)].rearrange("(t p) c -> p t c", p=128), Vg.t[:, nt - 4:nt, :], [Vg], [Bvscr])
                    for hh in range(2):
                        DMA(kscr[2 * g + hh], KT3[:, hh, (nt - 4) * 128:nt * 128], [KTg], [Bkscr])
                if KSUB <= 2:
                    continue
                attn_inner(g)
            if KONLY:
                for g in range(KONLY):
                    attn_inner(g)

        blocks = [
            dict(tiles=list(range(0, 7)), full=False, sample=False, last=False, first=True),
            dict(tiles=list(range(7, 14)), full=False, sample=False, last=False, first=False),
            dict(tiles=list(range(14, 19)), full=False, sample=False, last=False, first=False),
            dict(tiles=list(range(19, 26)), full=True, sample=False, last=False, first=True, nh=0, qj0=4),
            dict(tiles=list(range(26, 32)), full=True, sample=True, last=True, first=False, nh=4, qj0=0),
        ]
        import os
        STOP = int(os.environ.get("KSTOP", "99"))
        for bi, blk in enumerate(blocks):
            nt = len(blk["tiles"])
            Tn_ = (nt + (1 if blk["sample"] else 0)) * 128
            if STOP <= 1:
                break
            KONLY = int(os.environ.get("KONLY", "0"))
            if KONLY:
                if bi != 3:
                    continue
                stage_attn(blk)
                break
            P.barrier()
            load_x(blk)
            norm(0, 0, Tn_)
            if STOP <= 2:
                break
            stage_hgrn(blk)
            if STOP <= 3:
                break
            if STOP <= 4 and bi == 2:
                break
            if blk["full"]:
                qa = blk["qj0"] * 128
                if STOP <= 5:
                    break
                stage_ffn(0, blk, 0)
                if STOP <= 6:
                    break
                norm(2, 0, Tn_)
                P.barrier()
                stage_attn(blk)
                if STOP <= 7:
                    break
                stage_ffn(1, blk, qa)
                store_x(blk)
                if STOP <= 8:
                    break
        P.finish()
        with nc.Block() as block:
            P.replay(block)
    return nc


_CACHE = {}


def _host_consts():
    ident = np.eye(128, dtype=np.float32)
    s = np.arange(128)[:, None]
    t = np.arange(128)[None, :]
    cmask = (s <= t).astype(np.float32)
    smask = ((s <= t) & (s // 32 == t // 32)).astype(np.float32)
    scanp = np.ones((128, 512), np.float32)
    scanp[:, ::64] = 0.0
    scans = np.ones((128, 128), np.float32)
    scans[:, ::32] = 0.0
    seqm = (np.arange(128)[:, None] // 32 == np.arange(4)[None, :]).astype(np.float32)
    return dict(ident=ident, cmask=cmask, smask=smask, scanp=scanp, scans=scans, seqm=seqm)


def kernel(x_prompt, x_sample, state_a_S, cache_b_k, cache_b_v, state_ffn_conv,
           norm_mix, norm_ffn, a_w_in, a_gamma_lb, a_norm_o, a_w_o,
           b_w_qkv, b_q_norm, b_k_norm, b_rel_bias, b_w_o,
           f_w_in, f_conv_w, f_conv_b, f_w_down):
    f32 = lambda a: np.ascontiguousarray(np.asarray(a, dtype=np.float32))
    x_prompt, x_sample = f32(x_prompt), f32(x_sample)
    if "nc" not in _CACHE:
        _CACHE["nc"] = build_program()
    nc = _CACHE["nc"]
    cst_ = _host_consts()
    gains = np.stack([f32(norm_mix)[0], f32(norm_ffn)[0], f32(norm_mix)[1], f32(norm_ffn)[1]], 0)
    gains = np.ascontiguousarray(gains.reshape(4, 16, 128).transpose(2, 0, 1))
    gam = np.ascontiguousarray(f32(a_gamma_lb).reshape(3, 16, 128).transpose(2, 0, 1))
    pvec = np.ascontiguousarray(np.stack([f32(a_norm_o)[0], f32(b_q_norm)[0], f32(b_k_norm)[0]], 1))
    rb = f32(b_rel_bias)[0]
    c0t = np.ascontiguousarray(np.broadcast_to(rb[0][None, :], (128, 16)))
    kk = np.arange(128)[:, None]
    qq = np.arange(128)[None, :]
    T3 = np.ascontiguousarray(rb[np.maximum(kk - qq, 0)].transpose(0, 2, 1))
    T4 = np.ascontiguousarray(rb[np.minimum(kk - qq, 63) + 128].transpose(0, 2, 1))
    ii = np.arange(32)[None, :]
    Tn = np.ascontiguousarray(rb[(kk % 32) - ii + 128].transpose(0, 2, 1))
    cw = np.ascontiguousarray(f32(f_conv_w).reshape(2, 3, NFU, 128).transpose(3, 0, 1, 2))
    cb = np.ascontiguousarray(f32(f_conv_b).reshape(2, NFU, 128).transpose(2, 0, 1))
    shared = dict(w_in=f32(a_w_in)[0], w_oa=f32(a_w_o)[0], w_qkv=f32(b_w_qkv)[0], w_ob=f32(b_w_o)[0],
                  f_in=f32(f_w_in), f_dn=f32(f_w_down), gains=gains, gam=gam, pvec=pvec, c0t=c0t,
                  T3=T3, T4=T4, Tn=Tn, cw=cw, cb=cb, **cst_)
    sS = f32(state_a_S)[0]
    cK = f32(cache_b_k)[0].reshape(32, 512, D)
    cV = f32(cache_b_v)[0].reshape(32, 512, D)
    cst = f32(state_ffn_conv)
    in_maps = []
    for c in range(8):
        b, q = c // 4, c % 4
        base = 1024 * (q - 3)
        xs = np.zeros((33 * 128, D), np.float32)
        lo = max(base, 0)
        xs[lo - base:4096] = x_prompt[b, lo:base + 4096]
        xs[4096:] = x_sample[4 * c:4 * c + 4].reshape(128, D)
        kneg = np.zeros((128, 13), np.float32)
        for t in range(13):
            if base + (19 + t) * 128 < 0:
                kneg[:, t] = NEG
        m = dict(shared)
        m.update(xs=xs, s0=np.ascontiguousarray(sS[4 * c:4 * c + 4]), ck=np.ascontiguousarray(cK[4 * c:4 * c + 4]),
                 cv=np.ascontiguousarray(cV[4 * c:4 * c + 4]),
                 cst=np.ascontiguousarray(cst[:, 4 * c:4 * c + 4].reshape(16, DFF)), kneg=kneg)
        in_maps.append(m)
    import os
    ncores = int(os.environ.get("KCORES", "8"))
    res = run_bass_kernel_spmd(nc, in_maps[:ncores], core_ids=list(range(ncores)))
    R = list(res.results) + [res.results[0]] * (8 - ncores)
    y_p = np.zeros((2, 4096, D), np.float32)
    y_s = np.zeros((32, 32, D), np.float32)
    Sp = np.zeros((1, 2, 16, 128, 128), np.float32)
    Ss = np.zeros((1, 32, 16, 128, 128), np.float32)
    kp = np.zeros((1, 2, 512, 16, 128), np.float32)
    vp = np.zeros((1, 2, 512, 16, 128), np.float32)
    ks = np.zeros((1, 32, 32, 16, 128), np.float32)
    vs = np.zeros((1, 32, 32, 16, 128), np.float32)
    cp = np.zeros((2, 2, 2, DFF), np.float32)
    cs = np.zeros((2, 32, 2, DFF), np.float32)
    for c in range(8):
        b, q = c // 4, c % 4
        r = R[c]
        y_p[b, 1024 * q:1024 * (q + 1)] = r["yp"]
        y_s[4 * c:4 * c + 4] = r["ys"].reshape(4, 32, D)
        Ss[0, 4 * c:4 * c + 4] = r["Ss"]
        ks[0, 4 * c:4 * c + 4] = r["ks"].reshape(4, 32, 16, 128)
        vs[0, 4 * c:4 * c + 4] = r["vs"].reshape(4, 32, 16, 128)
        cs[:, 4 * c:4 * c + 4] = r["co"][:, 2:10].reshape(2, 4, 2, DFF)
        if q == 3:
            Sp[0, b] = r["Sp"]
            kp[0, b] = r["kp"].reshape(512, 16, 128)
            vp[0, b] = r["vp"].reshape(512, 16, 128)
            cp[:, b] = r["co"][:, 0:2]
    return (y_p, y_s, Sp, Ss, kp, vp, ks, vs, cp, cs)
```

```python
import numpy as np
from contextlib import ExitStack
import concourse.bass as bass
import concourse.mybir as mybir
from concourse.bass_utils import run_bass_kernel_spmd

F32 = mybir.dt.float32
BF16 = mybir.dt.bfloat16
AF = mybir.ActivationFunctionType
ALU = mybir.AluOpType

D = 2048
DFF = 5632
NFU = 44
EPS = 1e-6
NEG = -30000.0
SC = 128.0 ** -0.5


class Buf:
    __slots__ = ("name", "w", "r")

    def __init__(self, name="b"):
        self.name = name
        self.w = None
        self.r = {}


class Prog:
    ENGS = ("pe", "act", "dve", "pool", "sp")

    def __init__(self, nc, n_dma_sp=8, n_dma_pool=6):
        self.nc = nc
        self.ops = {e: [] for e in self.ENGS}
        self.known = {e: {} for e in self.ENGS}
        self.cnt = {e: 0 for e in self.ENGS}
        self.sems = {}
        self.dma_pool = {"sp": [f"dsp{i}" for i in range(n_dma_sp)],
                         "pool": [f"dpl{i}" for i in range(n_dma_pool)]}
        self.dma_tgt = {}
        self.dma_idx = {"sp": 0, "pool": 0}

    def semkeys(self):
        ks = [f"c_{e}" for e in ("pe", "act", "dve", "pool")]
        for q in self.dma_pool.values():
            ks += q
        return ks

    def _collect(self, eng, reads, writes):
        waits = {}

        def add(ev):
            if ev is None:
                return
            k, v = ev
            if waits.get(k, 0) < v:
                waits[k] = v
        for b in reads:
            add(b.w)
        for b in writes:
            add(b.w)
            for k, v in b.r.items():
                add((k, v))
        out = []
        kn = self.known[eng]
        for k, v in waits.items():
            if eng == "pe" and k == "c_pe":
                continue
            if kn.get(k, 0) >= v:
                continue
            kn[k] = v
            out.append((k, v))
        return out

    def _commit(self, ev, reads, writes):
        k, v = ev
        for b in reads:
            if b.r.get(k, 0) < v:
                b.r[k] = v
        for b in writes:
            b.w = ev
            b.r = {}

    def op(self, eng, fn, reads=(), writes=()):
        waits = self._collect(eng, reads, writes)
        self.cnt[eng] += 1
        ev = (f"c_{eng}", self.cnt[eng])
        self.ops[eng].append((waits, fn, (ev[0], 1)))
        self._commit(ev, reads, writes)
        return ev

    def dma(self, q, out, in_, reads=(), writes=(), **kw):
        pool = self.dma_pool[q]
        i = self.dma_idx[q]
        self.dma_idx[q] = i + 1
        sk = pool[i % len(pool)]
        prev = self.dma_tgt.get(sk, 0)
        waits = self._collect(q, reads, writes)
        if prev > 0 and self.known[q].get(sk, 0) < prev:
            self.known[q][sk] = prev
            waits.append((sk, prev))
        tgt = prev + 16
        self.dma_tgt[sk] = tgt
        ev = (sk, tgt)
        self.ops[q].append((waits, (lambda e, o=out, i_=in_, k=kw: e.dma_start(out=o, in_=i_, **k)), (sk, 16)))
        self._commit(ev, reads, writes)
        return ev

    def _all_events(self):
        waits = [(sk, t) for sk, t in self.dma_tgt.items() if t > 0]
        for e in ("pe", "act", "dve", "pool"):
            if self.cnt[e] > 0:
                waits.append((f"c_{e}", self.cnt[e]))
        return waits

    def barrier(self):
        allw = self._all_events()
        for e in self.ENGS:
            w = []
            for k, v in allw:
                if k == f"c_{e}" and e == "pe":
                    continue
                if self.known[e].get(k, 0) < v:
                    self.known[e][k] = v
                    w.append((k, v))
            if w:
                self.ops[e].append((w, None, None))

    def finish(self):
        self.ops["sp"].append((self._all_events(), None, None))

    def replay(self, block):
        sems = self.sems

        def run(engname):
            def f(e):
                for waits, fn, inc in self.ops[engname]:
                    for k, v in waits:
                        e.wait_ge(sems[k], v)
                    if fn is not None:
                        ins = fn(e)
                        if inc is not None:
                            ins.then_inc(sems[inc[0]], inc[1])
            return f
        block.tensor(run("pe"))
        block.scalar(run("act"))
        block.vector(run("dve"))
        block.gpsimd(run("pool"))
        block.sync(run("sp"))


class T:
    def __init__(self, t, name):
        self.t = t
        self.b = Buf(name)

    def __getitem__(self, idx):
        return self.t[idx]


class Ring:
    def __init__(self, items):
        self.items = items
        self.i = 0

    def nxt(self):
        x = self.items[self.i % len(self.items)]
        self.i += 1
        return x


def chunks(a, b, scol=None):
    out = []
    end = b if scol is None else min(b, scol)
    c = a
    while c < end:
        n = min(512, end - c)
        out.append((c, n, False))
        c += n
    if scol is not None and b > scol:
        out.append((scol, 128, True))
    return out


def build_program():
    nc = bass.Bass("TRN2", target_bir_lowering=False)

    def din(name, shape):
        return nc.dram_tensor(name, shape, F32, kind="ExternalInput").ap()

    def dout(name, shape):
        return nc.dram_tensor(name, shape, F32, kind="ExternalOutput").ap()

    xs = din("xs", [33 * 128, D])
    s0 = din("s0", [4, 16, 128, 128])
    ck = din("ck", [4, 512, D])
    cv = din("cv", [4, 512, D])
    cst = din("cst", [16, DFF])
    w_in = din("w_in", [32, 128, 4096])
    w_oa = din("w_oa", [D, D])
    w_qkv = din("w_qkv", [24, 128, 4096])
    w_ob = din("w_ob", [D, D])
    f_in = din("f_in", [2, 44, 128, 4096])
    f_dn = din("f_dn", [2, DFF, D])
    d_gains = din("gains", [128, 4, 16])
    d_gam = din("gam", [128, 3, 16])
    d_pvec = din("pvec", [128, 3])
    d_c0 = din("c0t", [128, 16])
    d_T3 = din("T3", [128, 16, 128])
    d_T4 = din("T4", [128, 16, 128])
    d_Tn = din("Tn", [128, 16, 32])
    d_cw = din("cw", [128, 2, 3, NFU])
    d_cb = din("cb", [128, 2, NFU])
    d_ident = din("ident", [128, 128])
    d_cmask = din("cmask", [128, 128])
    d_smask = din("smask", [128, 128])
    d_scanp = din("scanp", [128, 512])
    d_scans = din("scans", [128, 128])
    d_kneg = din("kneg", [128, 13])
    d_seqm = din("seqm", [128, 4])

    o_yp = dout("yp", [1024, D])
    o_ys = dout("ys", [128, D])
    o_Sp = dout("Sp", [16, 128, 128])
    o_Ss = dout("Ss", [4, 16, 128, 128])
    o_kp = dout("kp", [512, D])
    o_vp = dout("vp", [512, D])
    o_ks = dout("ks", [128, D])
    o_vs = dout("vs", [128, D])
    o_co = dout("co", [2, 10, DFF])

    kscr = nc.dram_tensor("kscr", [16, 128, 512], BF16, kind="Internal").ap()
    vscr = nc.dram_tensor("vscr", [512, D], BF16, kind="Internal").ap()
    Bkscr = Buf("kscr")
    Bvscr = Buf("vscr")

    P = Prog(nc)
    es = ExitStack()
    with es:
        for k in P.semkeys():
            P.sems[k] = es.enter_context(nc.semaphore(k))

        def sb(name, shape, dt):
            return T(es.enter_context(nc.sbuf_tensor("s_" + name, shape, dt)), name)

        def ps(name, shape, dt):
            return T(es.enter_context(nc.psum_tensor("p_" + name, shape, dt)), name)

        def bufs(lst):
            return [x.b if hasattr(x, 'b') else x for x in lst]

        def ACT(out, in_, func, R, W, **kw):
            P.op("act", lambda e: e.activation(out=out, in_=in_, func=func, **kw), bufs(R), bufs(W))

        def MM(out, lhsT, rhs, start, stop, R, W):
            P.op("pe", lambda e: e.matmul(out, lhsT, rhs, start=start, stop=stop), bufs(R), bufs(W))

        def TR(out, in_, ident, R, W):
            P.op("pe", lambda e: e.transpose(out, in_, ident), bufs(R), bufs(W))

        def TT(out, in0, in1, op, R, W):
            P.op("dve", lambda e: e.tensor_tensor(out=out, in0=in0, in1=in1, op=op), bufs(R), bufs(W))

        def TS(out, in0, s1, op0, R, W, s2=None, op1=None):
            if op1 is None:
                P.op("dve", lambda e: e.tensor_scalar(out=out, in0=in0, scalar1=s1, scalar2=None, op0=op0), bufs(R), bufs(W))
            else:
                P.op("dve", lambda e: e.tensor_scalar(out=out, in0=in0, scalar1=s1, scalar2=s2, op0=op0, op1=op1), bufs(R), bufs(W))

        def STT(out, in0, scalar, in1, op0, op1, R, W):
            P.op("dve", lambda e: e.scalar_tensor_tensor(out=out, in0=in0, scalar=scalar, in1=in1, op0=op0, op1=op1), bufs(R), bufs(W))

        def CPV(out, in_, R, W):
            P.op("dve", lambda e: e.tensor_copy(out=out, in_=in_), bufs(R), bufs(W))

        def RECIP(out, in_, R, W):
            P.op("dve", lambda e: e.reciprocal(out=out, in_=in_), bufs(R), bufs(W))

        def SCAN(out, d0, d1, R, W):
            P.op("dve", lambda e: e.tensor_tensor_scan(out=out, data0=d0, data1=d1, initial=0.0, op0=ALU.mult, op1=ALU.add), bufs(R), bufs(W))

        def MEMSET(ap, val, W):
            P.op("dve", lambda e: e.memset(ap, val), [], bufs(W))

        def CPA(out, in_, R, W):
            ACT(out, in_, AF.Copy, R, W)

        def DMA(out, in_, R, W, q="sp"):
            P.dma(q, out, in_, bufs(R), bufs(W))

        xT = sb("xT", [128, 16, 896], F32)
        hT = sb("hT", [128, 16, 896], BF16)
        wring = Ring([sb(f"wr{i}", [128, 4096], BF16) for i in range(3)])
        Vg = sb("Vg", [128, 12, 256], BF16)
        g4 = sb("g4", [128, 2, 896], BF16)
        g4b = sb("g4b", [128, 2, 896], BF16)
        g4c = sb("g4c", [128, 2, 896], BF16)
        oacc = sb("oacc", [128, 896], F32)
        tf = Ring([sb(f"tf{i}", [128, 512], F32) for i in range(6)])
        tb = Ring([sb(f"tb{i}", [128, 512], BF16) for i in range(4)])
        us = Ring([sb(f"us{i}", [128, 516], F32) for i in range(2)])
        S = sb("S", [128, 16, 128], F32)
        Sall = sb("Sall", [128, 8, 128], F32)
        tmpall = sb("tmpall", [128, 7, 128], F32)
        cmask4 = sb("cmask4", [128, 4, 128], F32)
        S0f = sb("S0f", [128, 4, 128], F32)
        arena = sb("arena", [128, 10240], BF16)
        identf = sb("identf", [128, 128], F32)
        identb = sb("identb", [128, 128], BF16)
        onesb = sb("onesb", [128, 128], BF16)
        cmask = sb("cmask", [128, 128], F32)
        smask = sb("smask", [128, 128], F32)
        scanp = sb("scanp", [128, 512], F32)
        scans = sb("scans", [128, 128], F32)
        gains = sb("gains", [128, 4, 16], F32)
        gam = sb("gam", [128, 3, 16], F32)
        lbt = sb("lbt", [128, 8, 16], F32)
        oml = sb("oml", [128, 16], F32)
        pvec = sb("pvec", [128, 3], F32)
        c0t = sb("c0t", [128, 16], F32)
        kneg = sb("kneg", [128, 13], F32)
        cbias = sb("cbias", [128, 13, 16], F32)
        seqm = sb("seqm", [128, 4], F32)
        cw = sb("cw", [128, 2, 3, NFU], F32)
        cb = sb("cb", [128, 2, NFU], F32)
        cstT = sb("cstT", [128, NFU, 16], F32)
        ucol = sb("ucol", [128, NFU, 10], F32)
        uH = sb("uH", [128, 2, NFU, 2], F32)
        onec = sb("onec", [128, 1], F32)
        epsc = sb("epsc", [128, 1], F32)
        Et = sb("Et", [128, 16], F32)
        E12t = sb("E12t", [128, 8], F32)
        Est = sb("Est", [128, 4], F32)
        Et2 = sb("Et2", [128, 16], F32)
        Est2 = sb("Est2", [128, 4], F32)
        T3g = sb("T3g", [128, 2, 128], F32)
        T4g = sb("T4g", [128, 2, 128], F32)
        Tng = sb("Tng", [128, 2, 32], F32)
        eT3g = sb("eT3g", [128, 2, 128], F32)
        eT4g = sb("eT4g", [128, 2, 128], F32)
        eTng = sb("eTng", [128, 2, 32], F32)

        pm = Ring([ps(f"pm{i}", [128, 512], F32) for i in range(4)])
        pss_ = Ring([ps(f"ps{i}", [128, 512], F32) for i in range(2)])
        ptr = Ring([ps(f"pt{i}", [128, 1024], BF16) for i in range(2)])

        class AV:
            def __init__(self, off, n, name):
                self.ap = arena.t[:, off:off + n]
                self.b = Buf(name)
                self.off = off

            def __getitem__(self, idx):
                return self.ap[idx]

        def carve(spec):
            off = 0
            d = {}
            for name, n in spec:
                d[name] = AV(off, n, name)
                off += n
            assert off <= 10240, off
            return d

        A_ = carve([("kT", 896), ("KpT", 896), ("qT", 896), ("Q2T", 896), ("gsT", 896), ("S0b", 512),
                    ("A0", 512), ("A1", 512), ("A2", 128), ("Km0", 128), ("Km1", 128), ("Ktall", 896), ("Sball", 896), ("kT2", 896), ("qT2", 896)])
        B_ = carve([("KTg", 3072), ("QTg", 1792), ("kc", 1024), ("vc", 1024), ("KcT", 1024),
                    ("E0a", 128), ("E0b", 128), ("E4a", 128), ("E4b", 128), ("E3a", 128), ("E3b", 128),
                    ("E1a", 128), ("E1b", 128), ("E2a", 128), ("E2b", 128), ("Eca", 160), ("Ecb", 160)])

        for t_, d_ in ((identf, d_ident), (cmask, d_cmask), (smask, d_smask), (scanp, d_scanp), (scans, d_scans),
                       (gains, d_gains), (gam, d_gam), (pvec, d_pvec), (c0t, d_c0), (kneg, d_kneg), (seqm, d_seqm),
                       (cw, d_cw), (cb, d_cb)):
            DMA(t_.t[:], d_, [], [t_])
        DMA(identb.t[:], d_ident, [], [identb], q="pool")
        MEMSET(onesb.t[:], 1.0, [onesb])
        MEMSET(onec.t[:], 1.0, [onec])
        MEMSET(epsc.t[:], EPS, [epsc])
        MEMSET(S.t[:], 0.0, [S])
        for i4 in range(4):
            DMA(cmask4.t[:, i4, :], d_cmask, [], [cmask4])
        MEMSET(uH.t[:], 0.0, [uH])
        g0, g1, g2 = gam.t[:, 0, :], gam.t[:, 1, :], gam.t[:, 2, :]
        L = lambda i: lbt.t[:, i, :]
        TT(L(0), g0, g1, ALU.max, [gam], [lbt])
        TT(L(0), L(0), g2, ALU.max, [gam, lbt], [lbt])
        for i, gi in enumerate((g0, g1, g2)):
            TT(L(1 + i), gi, L(0), ALU.subtract, [gam, lbt], [lbt])
            ACT(L(1 + i), L(1 + i), AF.Exp, [lbt], [lbt])
        TT(L(4), L(1), L(2), ALU.add, [lbt], [lbt])
        TT(L(4), L(4), L(3), ALU.add, [lbt], [lbt])
        RECIP(L(5), L(4), [lbt], [lbt])
        TT(L(6), L(1), L(5), ALU.mult, [lbt], [lbt])
        TS(oml.t[:], L(6), -1.0, ALU.mult, [lbt], [oml], s2=1.0, op1=ALU.add)
        for t in range(13):
            TS(cbias.t[:, t, :], c0t.t[:], kneg.t[:, t:t + 1], ALU.add, [c0t, kneg], [cbias])
        for pc in range(11):
            tt_ = tf.nxt()
            DMA(tt_.t[0:16, :], cst[:, pc * 512:(pc + 1) * 512], [], [tt_])
            pp = pss_.nxt()
            for i in range(4):
                TR(pp.t[:, i * 16:(i + 1) * 16], tt_.t[0:16, i * 128:(i + 1) * 128], identf.t[0:16, 0:16], [tt_, identf], [pp])
            CPV(cstT.t[:, pc * 4:(pc + 1) * 4, :], pp.t[:, 0:64].rearrange("p (a b) -> p a b", b=16), [pp], [cstT])

        def load_unit(src):
            w = wring.nxt()
            v = w.t[:, 0:4096].rearrange("p (k n) -> p k n", n=256)
            P.dma("pool", w.t[:, 0:4096], src, [], [w.b], max_dma_last_dim=8192)
            return w, v

        def load_rows(src):
            w = wring.nxt()
            v = w.t[:, 0:4096].rearrange("p (k n) -> p k n", n=2048)
            for kc_ in range(2):
                P.dma("pool", v[:, kc_, :], src[kc_ * 128:(kc_ + 1) * 128, :], [], [w.b], max_dma_last_dim=4096)
            return w, v

        def load_x(blk):
            tiles = blk["tiles"] + ([32] if blk["sample"] else [])
            for j, gt in enumerate(tiles):
                for q4 in range(4):
                    xt_ = tf.nxt()
                    DMA(xt_.t[:], xs[gt * 128:(gt + 1) * 128, q4 * 512:(q4 + 1) * 512], [], [xt_])
                    pp = pm.nxt()
                    for i in range(4):
                        TR(pp.t[:, i * 128:(i + 1) * 128], xt_.t[:, i * 128:(i + 1) * 128], identf.t[:], [xt_, identf], [pp])
                    CPA(xT.t[:, q4 * 4:(q4 + 1) * 4, j * 128:(j + 1) * 128], pp.t[:].rearrange("p (a b) -> p a b", b=128), [pp], [xT])

        def store_x(blk):
            tiles = blk["tiles"] + ([32] if blk["sample"] else [])
            for j, gt in enumerate(tiles):
                if gt < 24:
                    continue
                dst = o_ys if gt == 32 else o_yp[(gt - 24) * 128:(gt - 23) * 128, :]
                for q4 in range(4):
                    pp = pm.nxt()
                    for i in range(4):
                        TR(pp.t[:, i * 128:(i + 1) * 128], xT.t[:, q4 * 4 + i, j * 128:(j + 1) * 128], identf.t[:], [xT, identf], [pp])
                    st = tf.nxt()
                    CPA(st.t[:], pp.t[:], [pp], [st])
                    DMA(dst[:, q4 * 512:(q4 + 1) * 512], st.t[:], [st], [])

        def rstd_from(p, n, scale):
            rs = tf.nxt()
            ACT(rs.t[:, :n], p.t[:, :n], AF.Sqrt, [p, epsc], [rs], scale=scale, bias=epsc.t[:])
            rr = tf.nxt()
            RECIP(rr.t[:, :n], rs.t[:, :n], [rs], [rr])
            return rr

        def norm(gi, a, b):
            for (c0, n, _) in chunks(a, b):
                pssq = pm.nxt()
                for kc in range(16):
                    sq = tb.nxt()
                    ACT(sq.t[:, :n], xT.t[:, kc, c0:c0 + n], AF.Square, [xT], [sq])
                    MM(pssq.t[:, :n], onesb.t[:], sq.t[:, :n], kc == 0, kc == 15, [onesb, sq], [pssq])
                rr = rstd_from(pssq, n, 1.0 / D)
                for kc in range(16):
                    STT(hT.t[:, kc, c0:c0 + n], xT.t[:, kc, c0:c0 + n], gains.t[:, gi, kc:kc + 1], rr.t[:, :n],
                        ALU.mult, ALU.mult, [xT, gains, rr], [hT])

        def proj_T(Wt, Wv, hh, c0, n):
            p = pm.nxt()
            for kc in range(16):
                MM(p.t[:, :n], Wv[:, kc, hh * 128:(hh + 1) * 128], hT.t[:, kc, c0:c0 + n], kc == 0, kc == 15, [Wt, hT], [p])
            return p

        def wo_partial(Wt, Wv, a, b):
            for cu in range(16):
                for (c0, n, _) in chunks(a, b):
                    py = pm.nxt()
                    for hh in range(2):
                        MM(py.t[:, :n], Wv[:, hh, cu * 128:(cu + 1) * 128], g4.t[:, hh, c0:c0 + n], hh == 0, hh == 1, [Wt, g4], [py])
                    TT(xT.t[:, cu, c0:c0 + n], xT.t[:, cu, c0:c0 + n], py.t[:, :n], ALU.add, [xT, py], [xT])

        def stage_hgrn(blk):
            nt = len(blk["tiles"])
            samp = blk["sample"]
            full = blk["full"]
            scol = nt * 128 if samp else None
            ntt = nt + (1 if samp else 0)
            Tn_ = ntt * 128
            kT, KpT, qT, Q2T, gsT, S0b = A_["kT"], A_["KpT"], A_["qT"], A_["Q2T"], A_["gsT"], A_["S0b"]
            kTs = [A_["kT"], A_["kT2"]]
            qTs = [A_["qT"], A_["qT2"]]
            Ets = [Et, Et2]
            Ests = [Est, Est2]
            Aring = Ring([A_["A0"], A_["A1"]])
            Ktall, Sball = A_["Ktall"], A_["Sball"]
            Kmr = Ring([A_["Km0"], A_["Km1"]])
            for a_ in (A_["A0"], A_["A1"]):
                MEMSET(a_.ap[:, :], 0.0, [a_])
            for g in range(8):
                Wi, Wiv = load_unit(w_in[16 + g])
                Wf, Wfv = load_unit(w_in[8 + g])
                for j in range(ntt):
                    pv = pm.nxt()
                    for kc in range(16):
                        MM(pv.t[:, :256], hT.t[:, kc, j * 128:(j + 1) * 128], Wiv[:, kc, :], kc == 0, kc == 15, [hT, Wi], [pv])
                    CPA(Vg.t[:, j, :], pv.t[:, :256], [pv], [Vg])
                if full:
                    Wq, Wqv = load_unit(w_in[g])
                    WgT, Wgv_ = load_unit(w_in[24 + g])
                def p1(hh):
                    h = 2 * g + hh
                    kT, qT, Et, Est = kTs[hh], qTs[hh], Ets[hh], Ests[hh]
                    allch = chunks(0, Tn_, scol)
                    for chs in ([c for c in allch if not c[2]], [c for c in allch if c[2]]):
                        if not chs:
                            continue
                        st = []
                        for (c0, n, is_s) in chs:
                            pf = proj_T(Wf, Wfv, hh, c0, n)
                            t1 = tf.nxt()
                            ACT(t1.t[:, :n], pf.t[:, :n], AF.Sigmoid, [pf], [t1], scale=-1.0)
                            st.append((c0, n, is_s, t1, tf.nxt(), tf.nxt()))
                        for (c0, n, is_s, t1, t2, t3) in st:
                            TS(t1.t[:, :n], t1.t[:, :n], oml.t[:, h:h + 1], ALU.mult, [t1, oml], [t1])
                        for (c0, n, is_s, t1, t2, t3) in st:
                            ACT(t2.t[:, :n], t1.t[:, :n], AF.Ln, [t1, onec], [t2], scale=-1.0, bias=onec.t[:])
                        for (c0, n, is_s, t1, t2, t3) in st:
                            mk = scans.t[:, 0:n] if is_s else scanp.t[:, 0:n]
                            SCAN(t3.t[:, :n], mk, t2.t[:, :n], [t2, scans, scanp], [t3])
                        for (c0, n, is_s, t1, t2, t3) in st:
                            ACT(t2.t[:, :n], t3.t[:, :n], AF.Exp, [t3], [t2], scale=-1.0)
                        for (c0, n, is_s, t1, t2, t3) in st:
                            TT(kT.ap[:, c0:c0 + n], t1.t[:, :n], t2.t[:, :n], ALU.mult, [t1, t2], [kT])
                        for (c0, n, is_s, t1, t2, t3) in st:
                            if is_s:
                                ACT(Est.t[:, 0:4], t3.t[:, 0:128].rearrange("p (a b) -> p a b", b=32)[:, :, 31], AF.Exp, [t3], [Est])
                            else:
                                ACT(Et.t[:, c0 // 64:(c0 + n) // 64], t3.t[:, 0:n].rearrange("p (a b) -> p a b", b=64)[:, :, 63],
                                    AF.Exp, [t3], [Et])
                        if full:
                            for (c0, n, is_s, t1, t2, t3) in st:
                                pq = proj_T(Wq, Wqv, hh, c0, n)
                                ACT(t2.t[:, :n], t3.t[:, :n], AF.Exp, [t3], [t2])
                                TT(qT.ap[:, c0:c0 + n], pq.t[:, :n], t2.t[:, :n], ALU.mult, [pq, t2], [qT])

                def p2(hh):
                    h = 2 * g + hh
                    kT, qT, Et, Est = kTs[hh], qTs[hh], Ets[hh], Ests[hh]
                    if nt > 0:
                        v3 = lambda av, lo, hi: av.ap[:, 0:nt * 128].rearrange("p (a b) -> p a b", b=128)[:, :, lo:hi]
                        CPA(v3(KpT, 64, 128), v3(kT, 64, 128), [kT], [KpT])
                        if full:
                            CPA(v3(Q2T, 0, 64), v3(qT, 0, 64), [qT], [Q2T])
                        for j in range(nt):
                            e1 = Et.t[:, 2 * j:2 * j + 1]
                            TS(KpT.ap[:, j * 128:j * 128 + 64], kT.ap[:, j * 128:j * 128 + 64], e1, ALU.mult, [kT, Et], [KpT])
                            if full:
                                TS(Q2T.ap[:, j * 128 + 64:(j + 1) * 128], qT.ap[:, j * 128 + 64:(j + 1) * 128], e1, ALU.mult, [qT, Et], [Q2T])
                        ev = Et.t[:, 0:2 * nt].rearrange("p (a b) -> p a b", b=2)
                        TT(E12t.t[:, 0:nt], ev[:, :, 0], ev[:, :, 1], ALU.mult, [Et], [E12t])
                    if nt > 0:
                        hs_ = slice(hh * 128, (hh + 1) * 128)
                        pt = ptr.nxt()
                        for j in range(nt):
                            TR(pt.t[:, j * 128:(j + 1) * 128], KpT.ap[:, j * 128:(j + 1) * 128], identb.t[:], [KpT, identb], [pt])
                        CPA(Ktall.ap[:, 0:nt * 128], pt.t[:, 0:nt * 128], [pt], [Ktall])
                        for j0 in range(0, nt, 4):
                            jn = min(4, nt - j0)
                            px = pm.nxt()
                            for jj in range(jn):
                                j = j0 + jj
                                MM(px.t[:, jj * 128:(jj + 1) * 128], Ktall.ap[:, j * 128:(j + 1) * 128], Vg.t[:, j, hs_], True, True, [Ktall, Vg], [px])
                            for jj in range(jn):
                                j = j0 + jj
                                TS(tmpall.t[:, j, :], px.t[:, jj * 128:(jj + 1) * 128], Et.t[:, 2 * j + 1:2 * j + 2], ALU.mult, [px, Et], [tmpall])
                        CPV(Sall.t[:, 0, :], S.t[:, h, :], [S], [Sall])
                        for j in range(nt):
                            STT(Sall.t[:, j + 1, :], Sall.t[:, j, :], E12t.t[:, j:j + 1], tmpall.t[:, j, :], ALU.mult, ALU.add, [Sall, E12t, tmpall], [Sall])
                        CPV(S.t[:, h, :], Sall.t[:, nt, :], [Sall], [S])
                        if full:
                            CPA(Sball.ap[:, 0:nt * 128], Sall.t[:, 0:nt, :].rearrange("p a b -> p (a b)"), [Sall], [Sball])
                            for j0 in range(0, nt, 4):
                                jn = min(4, nt - j0)
                                psc = pm.nxt()
                                for jj in range(jn):
                                    j = j0 + jj
                                    MM(psc.t[0:64, jj * 128:jj * 128 + 64], kT.ap[:, j * 128:j * 128 + 64], qT.ap[:, j * 128:j * 128 + 64], True, True, [kT, qT], [psc])
                                    MM(psc.t[:, jj * 128 + 64:(jj + 1) * 128], KpT.ap[:, j * 128:(j + 1) * 128], qT.ap[:, j * 128 + 64:(j + 1) * 128], True, True, [KpT, qT], [psc])
                                A = Aring.nxt()
                                a3 = A.ap[:, 0:jn * 128].rearrange("p (a b) -> p a b", b=128)
                                p3 = psc.t[:, 0:jn * 128].rearrange("p (a b) -> p a b", b=128)
                                TT(a3[0:64, :, 0:64], p3[0:64, :, 0:64], cmask4.t[0:64, 0:jn, 0:64], ALU.mult, [psc, cmask4], [A])
                                TT(a3[:, :, 64:128], p3[:, :, 64:128], cmask4.t[:, 0:jn, 64:128], ALU.mult, [psc, cmask4], [A])
                                po = pm.nxt()
                                for jj in range(jn):
                                    j = j0 + jj
                                    MM(po.t[:, jj * 128:(jj + 1) * 128], Vg.t[:, j, hs_], A.ap[:, jj * 128:(jj + 1) * 128], True, False, [Vg, A], [po])
                                    MM(po.t[:, jj * 128:(jj + 1) * 128], Sball.ap[:, j * 128:(j + 1) * 128], Q2T.ap[:, j * 128:(j + 1) * 128], False, True, [Sball, Q2T], [po])
                                CPA(oacc.t[:, j0 * 128:(j0 + jn) * 128], po.t[:, 0:jn * 128], [po], [oacc])
                    if samp:
                        cs = slice(scol, scol + 128)
                        vj = Vg.t[:, nt, hh * 128:(hh + 1) * 128]
                        DMA(S0f.t[:], s0[:, h].rearrange("s k v -> k s v"), [], [S0f])
                        CPA(S0b.ap[:, :], S0f.t[:].rearrange("p a b -> p (a b)"), [S0f], [S0b])
                        psc = pss_.nxt()
                        MM(psc.t[:, 0:128], kT.ap[:, cs], qT.ap[:, cs], True, True, [kT, qT], [psc])
                        A2 = A_["A2"]
                        TT(A2.ap[:, :], psc.t[:, 0:128], smask.t[:], ALU.mult, [psc, smask], [A2])
                        po = pss_.nxt()
                        for s in range(4):
                            MM(po.t[:, 32 * s:32 * s + 32], vj, A2.ap[:, 32 * s:32 * s + 32], True, False, [Vg, A2], [po])
                            MM(po.t[:, 32 * s:32 * s + 32], S0b.ap[:, 128 * s:128 * s + 128], qT.ap[:, scol + 32 * s:scol + 32 * s + 32],
                               False, True, [S0b, qT], [po])
                        CPA(oacc.t[:, cs], po.t[:, 0:128], [po], [oacc])
                        pt = ptr.nxt()
                        TR(pt.t[:, 0:128], kT.ap[:, cs], identb.t[:], [kT, identb], [pt])
                        for s in range(4):
                            Km = Kmr.nxt()
                            TS(Km.ap[:, :], pt.t[:, 0:128], seqm.t[:, s:s + 1], ALU.mult, [pt, seqm], [Km])
                            px = pss_.nxt()
                            MM(px.t[:, 0:128], Km.ap[:, :], vj, True, True, [Km, Vg], [px])
                            t1 = tf.nxt()
                            TS(t1.t[:, :128], S0f.t[:, s, :], Est.t[:, s:s + 1], ALU.mult, [S0f, Est], [t1])
                            so = tf.nxt()
                            STT(so.t[:, :128], px.t[:, 0:128], Est.t[:, s:s + 1], t1.t[:, :128], ALU.mult, ALU.add, [px, Est, t1], [so])
                            DMA(o_Ss[s, h], so.t[:, :128], [so], [])
                    if full:
                        for (c0, n, _) in chunks(0, Tn_):
                            pg = proj_T(WgT, Wgv_, hh, c0, n)
                            ACT(gsT.ap[:, c0:c0 + n], pg.t[:, :n], AF.Silu, [pg], [gsT])
                            osq = tb.nxt()
                            ACT(osq.t[:, :n], oacc.t[:, c0:c0 + n], AF.Square, [oacc], [osq])
                            pq_ = pm.nxt()
                            MM(pq_.t[:, :n], onesb.t[:], osq.t[:, :n], True, True, [onesb, osq], [pq_])
                            rr = rstd_from(pq_, n, 1.0 / 128)
                            on = tf.nxt()
                            STT(on.t[:, :n], oacc.t[:, c0:c0 + n], pvec.t[:, 0:1], rr.t[:, :n], ALU.mult, ALU.mult, [oacc, pvec, rr], [on])
                            TT(g4.t[:, hh, c0:c0 + n], on.t[:, :n], gsT.ap[:, c0:c0 + n], ALU.mult, [on, gsT], [g4])
                p1(0)
                p1(1)
                p2(0)
                p2(1)
                if full:
                    Wo, Wov = load_rows(w_oa[256 * g:256 * (g + 1), :])
                    wo_partial(Wo, Wov, 0, Tn_)
            if blk["last"]:
                DMA(o_Sp.rearrange("h k v -> k h v"), S.t[:], [S], [])

        def stage_ffn(l, blk, a):
            nt = len(blk["tiles"])
            samp = blk["sample"]
            scol = nt * 128 if samp else None
            Tn_ = (nt + (1 if samp else 0)) * 128
            norm(1 + 2 * l, a, Tn_)
            chs = chunks(a, Tn_, scol)
            first = blk["first"]
            G3 = [g4, g4b, g4c]
            pst = {"pend": None}

            def ffn_tail(cc, c2, pg, fu, c0, n, G):
                ACT(cc.t[:, :n], c2.t[:, :n], AF.Silu, [c2], [cc])
                TT(G.t[:, fu, c0:c0 + n], cc.t[:, :n], pg.t[:, :n], ALU.mult, [cc, pg], [G])

            def flush():
                if pst["pend"] is not None:
                    ffn_tail(*pst["pend"])
                    pst["pend"] = None

            def down(p):
                for half in range(2):
                    w = wring.nxt()
                    wv = w.t[:, 0:4096].rearrange("p (k n) -> p k n", n=1024)
                    for k_ in range(4):
                        P.dma("pool", wv[:, k_, :], f_dn[l][512 * p + 128 * k_:512 * p + 128 * (k_ + 1), 1024 * half:1024 * (half + 1)],
                              [], [w.b], max_dma_last_dim=4096)
                    for cu in range(8):
                        cuu = 8 * half + cu
                        for (c0, n, _) in chunks(a, Tn_):
                            py = pm.nxt()
                            for k4 in range(4):
                                G_ = G3[(2 * p + k4 // 2) % 3]
                                MM(py.t[:, :n], wv[:, k4, cu * 128:(cu + 1) * 128], G_.t[:, k4 % 2, c0:c0 + n], k4 == 0, k4 == 3, [w, G_], [py])
                            TT(xT.t[:, cuu, c0:c0 + n], xT.t[:, cuu, c0:c0 + n], py.t[:, :n], ALU.add, [xT, py], [xT])

            def up(sl):
                Wu, Wuv = load_unit(f_in[l][sl])
                Wg, Wgv = load_unit(f_in[l][22 + sl])
                for fu in range(2):
                    U = 2 * sl + fu
                    prev = None
                    for (c0, n, is_s) in chs:
                        pu = proj_T(Wu, Wuv, fu, c0, n)
                        pg = proj_T(Wg, Wgv, fu, c0, n)
                        u_ = us.nxt()
                        w0 = cw.t[:, l, 0, U:U + 1]
                        w1 = cw.t[:, l, 1, U:U + 1]
                        w2 = cw.t[:, l, 2, U:U + 1]
                        cc = tf.nxt()
                        ACT(cc.t[:, :n], pu.t[:, :n], AF.Identity, [pu, cw, cb], [cc], scale=w2, bias=cb.t[:, l, U:U + 1])
                        c1 = tf.nxt()
                        c2 = tf.nxt()
                        if is_s:
                            u4 = u_.t[:, 0:136].rearrange("p (s c) -> p s c", c=34)
                            CPV(u4[:, :, 0:2], cstT.t[:, U, 8 * l:8 * l + 8].rearrange("p (s c) -> p s c", c=2), [cstT], [u_])
                            CPA(u4[:, :, 2:34], pu.t[:, 0:128].rearrange("p (s c) -> p s c", c=32), [pu], [u_])
                            v3 = lambda t_: t_.t[:, 0:128].rearrange("p (s c) -> p s c", c=32)
                            STT(v3(c1), u4[:, :, 1:33], w1, v3(cc), ALU.mult, ALU.add, [u_, cw, cc], [c1])
                            STT(v3(c2), u4[:, :, 0:32], w0, v3(c1), ALU.mult, ALU.add, [u_, cw, c1], [c2])
                            CPV(ucol.t[:, U, 2:10].rearrange("p (s c) -> p s c", c=2), u4[:, :, 32:34], [u_], [ucol])
                        else:
                            if prev is not None:
                                CPV(u_.t[:, 0:2], prev[0].t[:, prev[1]:prev[1] + 2], [prev[0]], [u_])
                            elif first:
                                MEMSET(u_.t[:, 0:2], 0.0, [u_])
                            else:
                                CPV(u_.t[:, 0:2], uH.t[:, l, U, :], [uH], [u_])
                            CPA(u_.t[:, 2:2 + n], pu.t[:, :n], [pu], [u_])
                            STT(c1.t[:, :n], u_.t[:, 1:1 + n], w1, cc.t[:, :n], ALU.mult, ALU.add, [u_, cw, cc], [c1])
                            STT(c2.t[:, :n], u_.t[:, 0:n], w0, c1.t[:, :n], ALU.mult, ALU.add, [u_, cw, c1], [c2])
                            prev = (u_, n)
                            last_prompt = (c0 + n == nt * 128)
                            if last_prompt:
                                if blk["last"]:
                                    CPV(ucol.t[:, U, 0:2], u_.t[:, n:n + 2], [u_], [ucol])
                                else:
                                    CPV(uH.t[:, l, U, :], u_.t[:, n:n + 2], [u_], [uH])
                        flush()
                        pst["pend"] = (cc, c2, pg, fu, c0, n, G3[sl % 3])
                flush()

            up(0)
            up(1)
            for p in range(11):
                if 2 * p + 2 < 22:
                    up(2 * p + 2)
                else:
                    flush()
                down(p)
                if 2 * p + 3 < 22:
                    up(2 * p + 3)
            if blk["last"]:
                for q4 in range(11):
                    pp = pm.nxt()
                    for i in range(4):
                        TR(pp.t[0:10, i * 128:(i + 1) * 128], ucol.t[:, q4 * 4 + i, :], identf.t[:], [ucol, identf], [pp])
                    st = tf.nxt()
                    CPA(st.t[0:10, :], pp.t[0:10, :], [pp], [st])
                    DMA(o_co[l][:, q4 * 512:(q4 + 1) * 512], st.t[0:10, :], [st], [])

        import os
        KSUB = int(os.environ.get("KSUB", "99"))
        KSUB2 = int(os.environ.get("KSUB2", "99"))

        def stage_attn(blk):
            nt = len(blk["tiles"])
            samp = blk["sample"]
            scol = nt * 128 if samp else None
            ntt = nt + (1 if samp else 0)
            Tn_ = ntt * 128
            nh = blk["nh"]
            qj0 = blk["qj0"]
            qa = qj0 * 128
            nkt = nh + nt
            gt0 = blk["tiles"][0]
            KTg, QTg, kcb, vcb, KcT = B_["KTg"], B_["QTg"], B_["kc"], B_["vc"], B_["KcT"]
            KT3 = KTg.ap[:, :].rearrange("p (h c) -> p h c", h=2)
            QT3 = QTg.ap[:, :].rearrange("p (h c) -> p h c", h=2)
            E0r = Ring([B_["E0a"], B_["E0b"]])
            E4r = Ring([B_["E4a"], B_["E4b"]])
            E3r = Ring([B_["E3a"], B_["E3b"]])
            E1r = Ring([B_["E1a"], B_["E1b"]])
            E2r = Ring([B_["E2a"], B_["E2b"]])
            Ecr = Ring([B_["Eca"], B_["Ecb"]])
            for e_ in ("E0a", "E0b", "E4a", "E4b"):
                MEMSET(B_[e_].ap[:, :], 0.0, [B_[e_]])

            def normalize(p, n, col):
                sq = tb.nxt()
                ACT(sq.t[:, :n], p.t[:, :n], AF.Square, [p], [sq])
                p2 = pm.nxt()
                MM(p2.t[:, :n], onesb.t[:], sq.t[:, :n], True, True, [onesb, sq], [p2])
                rr = rstd_from(p2, n, 1.0 / 128)
                kn = tf.nxt()
                STT(kn.t[:, :n], p.t[:, :n], pvec.t[:, col:col + 1], rr.t[:, :n], ALU.mult, ALU.mult, [p, pvec, rr], [kn])
                return kn

            def out_tile_row(gt):
                if gt == 32:
                    return ("s", 0)
                if gt >= 28:
                    return ("p", (gt - 28) * 128)
                return None

            tiles = blk["tiles"] + ([32] if samp else [])
            def attn_inner(g):
                for hh in range(2):
                    h = 2 * g + hh
                    hs = slice(hh * 128, (hh + 1) * 128)
                    for j in range(qj0, nt):
                        qc = QT3[:, hh, j * 128:(j + 1) * 128]
                        ks0 = nh + j - 4
                        psA = pm.nxt()
                        psB = pss_.nxt()
                        for kt in range(5):
                            dst = psA.t[:, kt * 128:(kt + 1) * 128] if kt < 4 else psB.t[:, 0:128]
                            MM(dst, KT3[:, hh, (ks0 + kt) * 128:(ks0 + kt + 1) * 128], qc, True, True, [KTg, QTg], [psA if kt < 4 else psB])
                        if KSUB2 <= 1:
                            continue
                        kti = [gt0 - nh + ks0 + kt - 19 for kt in range(5)]
                        E0 = E0r.nxt()
                        ACT(E0.ap[:, 0:64], psA.t[:, 0:64], AF.Exp, [psA, cbias], [E0], scale=SC, bias=cbias.t[:, kti[0], h:h + 1])
                        ACT(E0.ap[64:128, 64:128], psA.t[64:128, 64:128], AF.Exp, [psA, cbias], [E0], scale=SC, bias=cbias.t[64:128, kti[0], h:h + 1])
                        E1 = E1r.nxt()
                        ACT(E1.ap[:, :], psA.t[:, 128:256], AF.Exp, [psA, cbias], [E1], scale=SC, bias=cbias.t[:, kti[1], h:h + 1])
                        E2 = E2r.nxt()
                        ACT(E2.ap[:, :], psA.t[:, 256:384], AF.Exp, [psA, cbias], [E2], scale=SC, bias=cbias.t[:, kti[2], h:h + 1])
                        t3 = tf.nxt()
                        ACT(t3.t[:, :128], psA.t[:, 384:512], AF.Exp, [psA, kneg], [t3], scale=SC, bias=kneg.t[:, kti[3]:kti[3] + 1])
                        E3 = E3r.nxt()
                        TT(E3.ap[:, :], t3.t[:, :128], eT3g.t[:, hh, :], ALU.mult, [t3, eT3g], [E3])
                        t4 = tf.nxt()
                        ACT(t4.t[0:64, :128], psB.t[0:64, 0:128], AF.Exp, [psB, kneg], [t4], scale=SC, bias=kneg.t[0:64, kti[4]:kti[4] + 1])
                        ACT(t4.t[64:128, 64:128], psB.t[64:128, 64:128], AF.Exp, [psB, kneg], [t4], scale=SC, bias=kneg.t[64:128, kti[4]:kti[4] + 1])
                        E4 = E4r.nxt()
                        TT(E4.ap[0:64, :], t4.t[0:64, :128], eT4g.t[0:64, hh, :], ALU.mult, [t4, eT4g], [E4])
                        TT(E4.ap[64:128, 64:128], t4.t[64:128, 64:128], eT4g.t[64:128, hh, 64:128], ALU.mult, [t4, eT4g], [E4])
                        Es = [E0, E1, E2, E3, E4]
                        if KSUB2 <= 2:
                            continue
                        pden = pss_.nxt()
                        KSUB3 = int(os.environ.get("KSUB3", "3"))
                        for kt in range(5):
                            if KSUB3 & 1:
                                MM(pden.t[:, 0:128], onesb.t[:], Es[kt].ap[:, :], kt == 0, kt == 4, [onesb, Es[kt]], [pden])
                            if KSUB3 & 16:
                                MM(pden.t[:, 0:128], onesb.t[:], g4.t[:, 0, kt * 128:(kt + 1) * 128], kt == 0, kt == 4, [onesb, g4], [pden])
                            if KSUB3 & 32:
                                MM(pden.t[:, 0:128], onesb.t[:], Es[kt].ap[:, :], kt == 0, kt == 4, [onesb], [pden])
                            if KSUB3 & 4:
                                MM(pden.t[:, 0:128], onesb.t[:], Es[1].ap[:, :], kt == 0, kt == 4, [onesb, Es[1]], [pden])
                            if KSUB3 & 8:
                                MM(pden.t[:, 0:128], onesb.t[:], Es[kt].ap[:, :], True, True, [onesb, Es[kt]], [pden])
                        po = pss_.nxt()
                        for kt in range(5):
                            if KSUB3 & 2:
                                MM(po.t[:, 0:128], Vg.t[:, ks0 + kt, hs], Es[kt].ap[:, :], kt == 0, kt == 4, [Vg, Es[kt]], [po])
                        if KSUB2 <= 3:
                            continue
                        rd = tf.nxt()
                        TS(rd.t[:, :128], pden.t[:, 0:128], 1e-30, ALU.add, [pden], [rd])
                        rr = tf.nxt()
                        RECIP(rr.t[:, :128], rd.t[:, :128], [rd], [rr])
                        TT(g4.t[:, hh, j * 128:(j + 1) * 128], po.t[:, 0:128], rr.t[:, :128], ALU.mult, [po, rr], [g4])
                if samp:
                    kc3 = kcb.ap[:, :].rearrange("p (t c) -> p t c", c=256)
                    vc3 = vcb.ap[:, :].rearrange("p (t c) -> p t c", c=256)
                    for s in range(4):
                        DMA(kc3, ck[s][:, 256 * g:256 * (g + 1)].rearrange("(t p) c -> p t c", p=128), [], [kcb], q="pool")
                        DMA(vc3, cv[s][:, 256 * g:256 * (g + 1)].rearrange("(t p) c -> p t c", p=128), [], [vcb], q="pool")
                        pt = ptr.nxt()
                        for hh in range(2):
                            for kt in range(4):
                                i = hh * 4 + kt
                                TR(pt.t[:, i * 128:(i + 1) * 128], kc3[:, kt, hh * 128:(hh + 1) * 128], identb.t[:], [kcb, identb], [pt])
                        CPA(KcT.ap[:, :], pt.t[:, :], [pt], [KcT])
                        for hh in range(2):
                            h = 2 * g + hh
                            hs = slice(hh * 128, (hh + 1) * 128)
                            qc = QT3[:, hh, scol + 32 * s:scol + 32 * s + 32]
                            psc = pss_.nxt()
                            for kt in range(4):
                                i = hh * 4 + kt
                                MM(psc.t[:, kt * 32:(kt + 1) * 32], KcT.ap[:, i * 128:(i + 1) * 128], qc, True, True, [KcT, QTg], [psc])
                            MM(psc.t[:, 128:160], KT3[:, hh, nkt * 128:(nkt + 1) * 128], qc, True, True, [KTg, QTg], [psc])
                            Ec = Ecr.nxt()
                            ACT(Ec.ap[:, 0:96], psc.t[:, 0:96], AF.Exp, [psc, c0t], [Ec], scale=SC, bias=c0t.t[:, h:h + 1])
                            t3 = tf.nxt()
                            ACT(t3.t[:, :32], psc.t[:, 96:128], AF.Exp, [psc], [t3], scale=SC)
                            TT(Ec.ap[:, 96:128], t3.t[:, :32], eT3g.t[:, hh, 0:32], ALU.mult, [t3, eT3g], [Ec])
                            tn = tf.nxt()
                            ACT(tn.t[:, :32], psc.t[:, 128:160], AF.Exp, [psc], [tn], scale=SC)
                            STT(Ec.ap[:, 128:160], tn.t[:, :32], seqm.t[:, s:s + 1], eTng.t[:, hh, :], ALU.mult, ALU.mult, [tn, seqm, eTng], [Ec])
                            pden = pss_.nxt()
                            po = pss_.nxt()
                            for kt in range(5):
                                ec = Ec.ap[:, 128:160] if kt == 4 else Ec.ap[:, kt * 32:(kt + 1) * 32]
                                MM(pden.t[:, 0:32], onesb.t[:], ec, kt == 0, kt == 4, [onesb, Ec], [pden])
                            for kt in range(5):
                                ec = Ec.ap[:, 128:160] if kt == 4 else Ec.ap[:, kt * 32:(kt + 1) * 32]
                                lv = Vg.t[:, nkt, hs] if kt == 4 else vc3[:, kt, hs]
                                MM(po.t[:, 0:32], lv, ec, kt == 0, kt == 4, [Vg, vcb, Ec], [po])
                            rd = tf.nxt()
                            TS(rd.t[:, :32], pden.t[:, 0:32], 1e-30, ALU.add, [pden], [rd])
                            rr = tf.nxt()
                            RECIP(rr.t[:, :32], rd.t[:, :32], [rd], [rr])
                            TT(g4.t[:, hh, scol + 32 * s:scol + 32 * s + 32], po.t[:, 0:32], rr.t[:, :32], ALU.mult, [po, rr], [g4])
                if KSUB <= 3:
                    return
                Wo, Wov = load_rows(w_ob[256 * g:256 * (g + 1), :])
                wo_partial(Wo, Wov, qa, Tn_)


            KONLY = int(os.environ.get("KONLY", "0"))
            for g in range(8):
                if KONLY:
                    break
                Wv, Wvv = load_unit(w_qkv[16 + g])
                Wk, Wkv = load_unit(w_qkv[8 + g])
                DMA(T3g.t[:], d_T3[:, 2 * g:2 * g + 2, :], [], [T3g])
                DMA(T4g.t[:], d_T4[:, 2 * g:2 * g + 2, :], [], [T4g])
                DMA(Tng.t[:], d_Tn[:, 2 * g:2 * g + 2, :], [], [Tng])
                ACT(eT3g.t[:], T3g.t[:], AF.Exp, [T3g], [eT3g])
                ACT(eT4g.t[:], T4g.t[:], AF.Exp, [T4g], [eT4g])
                ACT(eTng.t[:], Tng.t[:], AF.Exp, [Tng], [eTng])
                if nh:
                    DMA(Vg.t[:, 0:4, :], vscr[:, 256 * g:256 * (g + 1)].rearrange("(t p) c -> p t c", p=128), [Bvscr], [Vg])
                    for hh in range(2):
                        DMA(KT3[:, hh, 0:512], kscr[2 * g + hh], [Bkscr], [KTg])
                for j, gt in enumerate(tiles):
                    pv = pm.nxt()
                    for kc in range(16):
                        MM(pv.t[:, :256], hT.t[:, kc, j * 128:(j + 1) * 128], Wvv[:, kc, :], kc == 0, kc == 15, [hT, Wv], [pv])
                    CPA(Vg.t[:, nh + j, :], pv.t[:, :256], [pv], [Vg])
                    orow = out_tile_row(gt)
                    if orow is not None:
                        st = tf.nxt()
                        CPA(st.t[:, :256], pv.t[:, :256], [pv], [st])
                        dst = o_vs if orow[0] == "s" else o_vp[orow[1]:orow[1] + 128, :]
                        DMA(dst[:, 256 * g:256 * (g + 1)], st.t[:, :256], [st], [])
                Wq, Wqv = load_unit(w_qkv[g])
                for hh in range(2):
                    h = 2 * g + hh
                    for (c0, n, _) in chunks(0, Tn_):
                        pk = proj_T(Wk, Wkv, hh, c0, n)
                        kn = normalize(pk, n, 2)
                        CPA(KT3[:, hh, nh * 128 + c0:nh * 128 + c0 + n], kn.t[:, :n], [kn], [KTg])
                        for jj in range(n // 128):
                            gt = tiles[c0 // 128 + jj]
                            orow = out_tile_row(gt)
                            if orow is None:
                                continue
                            pso = pss_.nxt()
                            TR(pso.t[:, 0:128], kn.t[:, jj * 128:(jj + 1) * 128], identf.t[:], [kn, identf], [pso])
                            st = tf.nxt()
                            CPV(st.t[:, :128], pso.t[:, 0:128], [pso], [st])
                            dst = o_ks if orow[0] == "s" else o_kp[orow[1]:orow[1] + 128, :]
                            DMA(dst[:, h * 128:(h + 1) * 128], st.t[:, :128], [st], [])
                    for (c0, n, _) in chunks(qa, Tn_):
                        pq = proj_T(Wq, Wqv, hh, c0, n)
                        qn = normalize(pq, n, 1)
                        CPA(QT3[:, hh, c0:c0 + n], qn.t[:, :n], [qn], [QTg])
                if KSUB <= 1:
                    continue
                if nh == 0:
                    DMA(vscr[:, 256 * g:256 * (g + 1)].rearrange("(t p) c -> p t c", p=128), Vg.t[:, nt - 4:nt, :], [Vg], [Bvscr])
                    for hh in range(2):
                        DMA(kscr[2 * g + hh], KT3[:, hh, (nt - 4) * 128:nt * 128], [KTg], [Bkscr])
                if KSUB <= 2:
                    continue
                attn_inner(g)
            if KONLY:
                for g in range(KONLY):
                    attn_inner(g)

        blocks = [
            dict(tiles=list(range(0, 7)), full=False, sample=False, last=False, first=True),
            dict(tiles=list(range(7, 14)), full=False, sample=False, last=False, first=False),
            dict(tiles=list(range(14, 19)), full=False, sample=False, last=False, first=False),
            dict(tiles=list(range(19, 26)), full=True, sample=False, last=False, first=True, nh=0, qj0=4),
            dict(tiles=list(range(26, 32)), full=True, sample=True, last=True, first=False, nh=4, qj0=0),
        ]
        import os
        STOP = int(os.environ.get("KSTOP", "99"))
        for bi, blk in enumerate(blocks):
            nt = len(blk["tiles"])
            Tn_ = (nt + (1 if blk["sample"] else 0)) * 128
            if STOP <= 1:
                break
            KONLY = int(os.environ.get("KONLY", "0"))
            if KONLY:
                if bi != 3:
                    continue
                stage_attn(blk)
                break
            P.barrier()
            load_x(blk)
            norm(0, 0, Tn_)
            if STOP <= 2:
                break
            stage_hgrn(blk)
            if STOP <= 3:
                break
            if STOP <= 4 and bi == 2:
                break
            if blk["full"]:
                qa = blk["qj0"] * 128
                if STOP <= 5:
                    break
                stage_ffn(0, blk, 0)
                if STOP <= 6:
                    break
                norm(2, 0, Tn_)
                P.barrier()
                stage_attn(blk)
                if STOP <= 7:
                    break
                stage_ffn(1, blk, qa)
                store_x(blk)
                if STOP <= 8:
                    break
        P.finish()
        with nc.Block() as block:
            P.replay(block)
    return nc


_CACHE = {}


def _host_consts():
    ident = np.eye(128, dtype=np.float32)
    s = np.arange(128)[:, None]
    t = np.arange(128)[None, :]
    cmask = (s <= t).astype(np.float32)
    smask = ((s <= t) & (s // 32 == t // 32)).astype(np.float32)
    scanp = np.ones((128, 512), np.float32)
    scanp[:, ::64] = 0.0
    scans = np.ones((128, 128), np.float32)
    scans[:, ::32] = 0.0
    seqm = (np.arange(128)[:, None] // 32 == np.arange(4)[None, :]).astype(np.float32)
    return dict(ident=ident, cmask=cmask, smask=smask, scanp=scanp, scans=scans, seqm=seqm)


def kernel(x_prompt, x_sample, state_a_S, cache_b_k, cache_b_v, state_ffn_conv,
           norm_mix, norm_ffn, a_w_in, a_gamma_lb, a_norm_o, a_w_o,
           b_w_qkv, b_q_norm, b_k_norm, b_rel_bias, b_w_o,
           f_w_in, f_conv_w, f_conv_b, f_w_down):
    f32 = lambda a: np.ascontiguousarray(np.asarray(a, dtype=np.float32))
    x_prompt, x_sample = f32(x_prompt), f32(x_sample)
    if "nc" not in _CACHE:
        _CACHE["nc"] = build_program()
    nc = _CACHE["nc"]
    cst_ = _host_consts()
    gains = np.stack([f32(norm_mix)[0], f32(norm_ffn)[0], f32(norm_mix)[1], f32(norm_ffn)[1]], 0)
    gains = np.ascontiguousarray(gains.reshape(4, 16, 128).transpose(2, 0, 1))
    gam = np.ascontiguousarray(f32(a_gamma_lb).reshape(3, 16, 128).transpose(2, 0, 1))
    pvec = np.ascontiguousarray(np.stack([f32(a_norm_o)[0], f32(b_q_norm)[0], f32(b_k_norm)[0]], 1))
    rb = f32(b_rel_bias)[0]
    c0t = np.ascontiguousarray(np.broadcast_to(rb[0][None, :], (128, 16)))
    kk = np.arange(128)[:, None]
    qq = np.arange(128)[None, :]
    T3 = np.ascontiguousarray(rb[np.maximum(kk - qq, 0)].transpose(0, 2, 1))
    T4 = np.ascontiguousarray(rb[np.minimum(kk - qq, 63) + 128].transpose(0, 2, 1))
    ii = np.arange(32)[None, :]
    Tn = np.ascontiguousarray(rb[(kk % 32) - ii + 128].transpose(0, 2, 1))
    cw = np.ascontiguousarray(f32(f_conv_w).reshape(2, 3, NFU, 128).transpose(3, 0, 1, 2))
    cb = np.ascontiguousarray(f32(f_conv_b).reshape(2, NFU, 128).transpose(2, 0, 1))
    def tile_units(w):
        n = w.shape[1] // 256
        return np.ascontiguousarray(w.reshape(16, 128, n, 256).transpose(2, 1, 0, 3)).reshape(n, 128, 4096)
    fw = f32(f_w_in)
    shared = dict(w_in=tile_units(f32(a_w_in)[0]), w_oa=f32(a_w_o)[0], w_qkv=tile_units(f32(b_w_qkv)[0]), w_ob=f32(b_w_o)[0],
                  f_in=np.stack([tile_units(fw[0]), tile_units(fw[1])], 0), f_dn=f32(f_w_down), gains=gains, gam=gam, pvec=pvec, c0t=c0t,
                  T3=T3, T4=T4, Tn=Tn, cw=cw, cb=cb, **cst_)
    sS = f32(state_a_S)[0]
    cK = f32(cache_b_k)[0].reshape(32, 512, D)
    cV = f32(cache_b_v)[0].reshape(32, 512, D)
    cst = f32(state_ffn_conv)
    in_maps = []
    for c in range(8):
        b, q = c // 4, c % 4
        base = 1024 * (q - 3)
        xs = np.zeros((33 * 128, D), np.float32)
        lo = max(base, 0)
        xs[lo - base:4096] = x_prompt[b, lo:base + 4096]
        xs[4096:] = x_sample[4 * c:4 * c + 4].reshape(128, D)
        kneg = np.zeros((128, 13), np.float32)
        for t in range(13):
            if base + (19 + t) * 128 < 0:
                kneg[:, t] = NEG
        m = dict(shared)
        m.update(xs=xs, s0=np.ascontiguousarray(sS[4 * c:4 * c + 4]), ck=np.ascontiguousarray(cK[4 * c:4 * c + 4]),
                 cv=np.ascontiguousarray(cV[4 * c:4 * c + 4]),
                 cst=np.ascontiguousarray(cst[:, 4 * c:4 * c + 4].reshape(16, DFF)), kneg=kneg)
        in_maps.append(m)
    import os
    ncores = int(os.environ.get("KCORES", "8"))
    res = run_bass_kernel_spmd(nc, in_maps[:ncores], core_ids=list(range(ncores)))
    R = list(res.results) + [res.results[0]] * (8 - ncores)
    y_p = np.zeros((2, 4096, D), np.float32)
    y_s = np.zeros((32, 32, D), np.float32)
    Sp = np.zeros((1, 2, 16, 128, 128), np.float32)
    Ss = np.zeros((1, 32, 16, 128, 128), np.float32)
    kp = np.zeros((1, 2, 512, 16, 128), np.float32)
    vp = np.zeros((1, 2, 512, 16, 128), np.float32)
    ks = np.zeros((1, 32, 32, 16, 128), np.float32)
    vs = np.zeros((1, 32, 32, 16, 128), np.float32)
    cp = np.zeros((2, 2, 2, DFF), np.float32)
    cs = np.zeros((2, 32, 2, DFF), np.float32)
    for c in range(8):
        b, q = c // 4, c % 4
        r = R[c]
        y_p[b, 1024 * q:1024 * (q + 1)] = r["yp"]
        y_s[4 * c:4 * c + 4] = r["ys"].reshape(4, 32, D)
        Ss[0, 4 * c:4 * c + 4] = r["Ss"]
        ks[0, 4 * c:4 * c + 4] = r["ks"].reshape(4, 32, 16, 128)
        vs[0, 4 * c:4 * c + 4] = r["vs"].reshape(4, 32, 16, 128)
        cs[:, 4 * c:4 * c + 4] = r["co"][:, 2:10].reshape(2, 4, 2, DFF)
        if q == 3:
            Sp[0, b] = r["Sp"]
            kp[0, b] = r["kp"].reshape(512, 16, 128)
            vp[0, b] = r["vp"].reshape(512, 16, 128)
            cp[:, b] = r["co"][:, 0:2]
    return (y_p, y_s, Sp, Ss, kp, vp, ks, vs, cp, cs)
```
